# Optimizing a Trainium2 kernel written in Bass

```python
import math
import jax
import jax.numpy as jnp
from jax import lax
import numpy as np

D_MODEL = 1024
BATCH = 8
SEQ = 2048
DEPTH = 1

GRID_W = 64
CTX_LEN = 256
N_DIR = 2
CONV_K = 5
CHUNK = 64
EPS = 1e-6
FFN_RES = 0.5
D_FF = 2816

GDN_HEADS = 8
GDN_DK = 128
GDN_DV = 128
GDN_QK = GDN_HEADS * GDN_DK
GDN_V = GDN_HEADS * GDN_DV

MB_INNER = 2 * D_MODEL
MB_HEADDIM = 64
MB_HEADS = MB_INNER // MB_HEADDIM
MB_GROUPS = 2
MB_HPG = MB_HEADS // MB_GROUPS
MB_STATE = 128
MB_XBC = MB_INNER + 2 * MB_GROUPS * MB_STATE

IN_SIZES = (2 * GDN_QK + GDN_V, GDN_V, N_DIR * GDN_HEADS, N_DIR * GDN_HEADS, MB_INNER, MB_XBC, N_DIR * MB_HEADS, 2 * D_MODEL)
D_IN = 2 * GDN_QK + 2 * GDN_V + 2 * N_DIR * GDN_HEADS + MB_INNER + MB_XBC + N_DIR * MB_HEADS + 2 * D_MODEL

kernel_name = "hybrid_gdn_ssd_macaron_prefix"


def rmsnorm(x, g):
    xf = x.astype(jnp.float32)
    y = xf * lax.rsqrt(jnp.mean(xf * xf, axis=-1, keepdims=True) + EPS)
    return (y * g.astype(jnp.float32)).astype(x.dtype)


def l2norm(t):
    tf = t.astype(jnp.float32)
    return tf * lax.rsqrt(jnp.sum(tf * tf, axis=-1, keepdims=True) + EPS)


def swiglu(h, w_gu, w_down):
    gt, up = jnp.split(h @ w_gu, 2, axis=-1)
    return (jax.nn.silu(gt) * up) @ w_down


def ffn_sublayer(x, g, shift, scale, gate, w_gu, w_down):
    h = rmsnorm(x, g) * (1.0 + scale) + shift
    return x + FFN_RES * gate * swiglu(h, w_gu, w_down)


def dwconv(u, w):
    return lax.conv_general_dilated(
        u, w[:, None, :].astype(u.dtype), window_strides=(1,),
        padding=[(CONV_K // 2, CONV_K // 2)],
        dimension_numbers=('NWC', 'WIO', 'NWC'), feature_group_count=u.shape[-1])


def grid_conv(u, w):
    b, n, ch = u.shape
    rows = n // GRID_W
    return dwconv(u.reshape(b * rows, GRID_W, ch), w).reshape(b, n, ch)


def gated_delta_chunked(q, k, v, g, beta, s0):
    f32 = jnp.float32
    b, n_tok, nh, _ = q.shape
    dv = v.shape[-1]
    nc = n_tok // CHUNK

    def blocks(t):
        t = t.astype(f32).reshape(b, nc, CHUNK, nh, *t.shape[3:])
        return jnp.moveaxis(t, 2, 3)

    q, k, v, beta = blocks(q), blocks(k), blocks(v), blocks(beta)
    g = jnp.cumsum(blocks(g), axis=-1)
    idx = jnp.arange(CHUNK)
    causal = idx[:, None] >= idx[None, :]
    strict = idx[:, None] > idx[None, :]
    decay = jnp.exp(jnp.where(causal, g[..., :, None] - g[..., None, :], -jnp.inf))
    kb = k * beta[..., None]
    a_mat = jnp.where(strict, jnp.einsum('bnhid,bnhjd->bnhij', kb, k) * decay, 0.0) + jnp.eye(CHUNK, dtype=f32)
    rhs = jnp.concatenate([v * beta[..., None], kb * jnp.exp(g)[..., None]], axis=-1)
    sol = lax.linalg.triangular_solve(a_mat, rhs, left_side=True, lower=True, unit_diagonal=True)
    u, w = sol[..., :dv], sol[..., dv:]
    qk = jnp.einsum('bnhid,bnhjd->bnhij', q, k) * decay
    q_in = q * jnp.exp(g)[..., None]
    k_out = k * jnp.exp(g[..., -1:] - g)[..., None]
    g_tot = jnp.exp(g[..., -1])

    def step(s, inp):
        q_c, k_c, u_c, w_c, qk_c, gt_c = inp
        v_new = u_c - jnp.einsum('bhcd,bhde->bhce', w_c, s)
        o_c = jnp.einsum('bhcd,bhde->bhce', q_c, s) + jnp.einsum('bhij,bhje->bhie', qk_c, v_new)
        s = s * gt_c[..., None, None] + jnp.einsum('bhcd,bhce->bhde', k_c, v_new)
        return s, o_c

    seq = tuple(jnp.moveaxis(t, 1, 0) for t in (q_in, k_out, u, w, qk, g_tot))
    s_fin, o = lax.scan(step, s0.astype(f32), seq)
    o = jnp.moveaxis(jnp.moveaxis(o, 0, 1), 2, 3).reshape(b, n_tok, nh, dv)
    return o, s_fin


def ssd_chunked(xs, dt, a_neg, bm, cm, s0):
    f32 = jnp.float32
    b, n_tok = xs.shape[:2]
    nc = n_tok // CHUNK
    a = (dt * a_neg).reshape(b, nc, CHUNK, MB_GROUPS, MB_HPG)
    a_cs = jnp.cumsum(jnp.moveaxis(a, 2, -1), axis=-1)
    xdt = (xs.astype(f32) * dt[..., None]).reshape(b, nc, CHUNK, MB_GROUPS, MB_HPG, MB_HEADDIM)
    bc = bm.astype(f32).reshape(b, nc, CHUNK, MB_GROUPS, MB_STATE)
    cc = cm.astype(f32).reshape(b, nc, CHUNK, MB_GROUPS, MB_STATE)
    idx = jnp.arange(CHUNK)
    causal = idx[:, None] >= idx[None, :]
    seg = jnp.exp(jnp.where(causal, a_cs[..., :, None] - a_cs[..., None, :], -jnp.inf))
    cb = jnp.einsum('bclgd,bcsgd->bcgls', cc, bc)
    y_diag = jnp.einsum('bcghls,bcsghp->bclghp', cb[:, :, :, None] * seg, xdt)

    def step(s, inp):
        c_c, b_c, x_c, acs = inp
        y_c = jnp.einsum('blgd,bghpd,bghl->blghp', c_c, s, jnp.exp(acs))
        w_end = jnp.exp(acs[..., -1:] - acs)
        s = s * jnp.exp(acs[..., -1])[..., None, None] + jnp.einsum('bsgd,bghs,bsghp->bghpd', b_c, w_end, x_c)
        return s, y_c

    seq = tuple(jnp.moveaxis(t, 1, 0) for t in (cc, bc, xdt, a_cs))
    s_fin, y_off = lax.scan(step, s0.astype(f32), seq)
    y = y_diag + jnp.moveaxis(y_off, 0, 1)
    return y.reshape(b, n_tok, MB_GROUPS, MB_HPG, MB_HEADDIM), s_fin


def mixer_features(h, w_in, gdn_conv_w, mb_conv_w, mb_conv_b, conv):
    b, n, _ = h.shape
    u = h @ w_in
    qkv, za, a, beta, zb, xbc, dt, gates = jnp.split(u, np.cumsum(IN_SIZES)[:-1].tolist(), axis=-1)
    qkv = jax.nn.silu(conv(qkv, gdn_conv_w))
    q, k, v = jnp.split(qkv, [GDN_QK, 2 * GDN_QK], axis=-1)
    q = l2norm(q.reshape(b, n, GDN_HEADS, GDN_DK)) * GDN_DK ** -0.5
    k = l2norm(k.reshape(b, n, GDN_HEADS, GDN_DK))
    xbc = jax.nn.silu(conv(xbc, mb_conv_w) + mb_conv_b)
    xs, bm, cm = jnp.split(xbc, [MB_INNER, MB_INNER + MB_GROUPS * MB_STATE], axis=-1)
    return {
        'q': q, 'k': k, 'v': v.reshape(b, n, GDN_HEADS, GDN_DV), 'zA': za,
        'a': a.reshape(b, n, N_DIR, GDN_HEADS), 'beta': beta.reshape(b, n, N_DIR, GDN_HEADS),
        'zB': zb, 'xs': xs.reshape(b, n, MB_GROUPS, MB_HPG, MB_HEADDIM),
        'Bm': bm.reshape(b, n, MB_GROUPS, MB_STATE), 'Cm': cm.reshape(b, n, MB_GROUPS, MB_STATE),
        'dt': dt.reshape(b, n, N_DIR, MB_HEADS), 'gates': gates,
    }


def rev(t):
    return jnp.flip(t, axis=1)


def gdn_mixer(f, a_log, dt_bias, s0):
    f32 = jnp.float32
    g = -jnp.exp(a_log.astype(f32)) * jax.nn.softplus((f['a'] + dt_bias).astype(f32))
    beta = jax.nn.sigmoid(f['beta'].astype(f32))
    q, k, v = f['q'], f['k'], f['v']
    o_f, s_f = gated_delta_chunked(q, k, v, g[:, :, 0], beta[:, :, 0], s0[0])
    o_b, s_b = gated_delta_chunked(rev(q), rev(k), rev(v), rev(g[:, :, 1]), rev(beta[:, :, 1]), s0[1])
    return o_f + rev(o_b), jnp.stack([s_f, s_b])


def ssd_mixer(f, a_log, dt_bias, s0):
    f32 = jnp.float32
    b, n = f['dt'].shape[:2]
    dt = jax.nn.softplus((f['dt'] + dt_bias).astype(f32)).reshape(b, n, N_DIR, MB_GROUPS, MB_HPG)
    a_neg = -jnp.exp(a_log.astype(f32)).reshape(N_DIR, MB_GROUPS, MB_HPG)
    xs, bm, cm = f['xs'], f['Bm'], f['Cm']
    y_f, s_f = ssd_chunked(xs, dt[:, :, 0], a_neg[0], bm, cm, s0[0])
    y_b, s_b = ssd_chunked(rev(xs), rev(dt[:, :, 1]), a_neg[1], rev(bm), rev(cm), s0[1])
    return y_f + rev(y_b), jnp.stack([s_f, s_b])


def mixer_out(f, o, y, gdn_norm_g, mb_d, mb_norm_g, w_branch_gdn, w_branch_mb, w_out):
    b, n = o.shape[:2]
    dt = f['gates'].dtype
    za = f['zA'].reshape(b, n, GDN_HEADS, GDN_DV)
    ya = (rmsnorm(o, gdn_norm_g) * jax.nn.silu(za)).astype(dt).reshape(b, n, GDN_V)
    yb = y + mb_d.reshape(MB_GROUPS, MB_HPG)[..., None] * f['xs']
    yb = yb.reshape(b, n, MB_GROUPS, MB_HPG * MB_HEADDIM) * jax.nn.silu(f['zB'].reshape(b, n, MB_GROUPS, MB_HPG * MB_HEADDIM))
    yb = rmsnorm(yb, mb_norm_g.reshape(MB_GROUPS, MB_HPG * MB_HEADDIM)).astype(dt).reshape(b, n, MB_INNER)
    ga, gb = jnp.split(f['gates'], 2, axis=-1)
    merged = jax.nn.sigmoid(ga) * (ya @ w_branch_gdn) + jax.nn.sigmoid(gb) * (yb @ w_branch_mb)
    return merged @ w_out


def setup_inputs(seed: int = 0) -> dict:
    key = jax.random.key(seed)
    ks = jax.random.split(key, 24)
    f32 = jnp.float32
    L, D = DEPTH, D_MODEL

    def nrm(k, shape, scale):
        return jax.random.normal(k, shape, f32) * scale

    def dt_bias_init(k, shape):
        dt = jnp.exp(jax.random.uniform(k, shape, f32, math.log(1e-3), math.log(1e-1)))
        return dt + jnp.log(-jnp.expm1(-dt))

    def a_log_init(k, shape):
        return jnp.log(jax.random.uniform(k, shape, f32, 1.0, 16.0))

    return {
        'x': nrm(ks[0], (BATCH, SEQ, D), 1.0),
        'c': nrm(ks[1], (BATCH, D), 1.0),
        'ctx': nrm(ks[2], (BATCH, CTX_LEN, D), 1.0),
        'c_ctx': nrm(ks[3], (D,), 1.0),
        'w_ada': nrm(ks[4], (L, D, 9 * D), D ** -0.5),
        'b_ada': nrm(ks[5], (L, 9 * D), 0.02),
        'norm_g': 1.0 + nrm(ks[6], (L, 3, D), 0.02),
        'ffn_w_gu': nrm(ks[7], (L, 2, D, 2 * D_FF), D ** -0.5),
        'ffn_w_down': nrm(ks[8], (L, 2, D_FF, D), D_FF ** -0.5),
        'w_in': nrm(ks[9], (L, D, D_IN), D ** -0.5),
        'gdn_conv_w': nrm(ks[10], (L, CONV_K, 2 * GDN_QK + GDN_V), CONV_K ** -0.5),
        'gdn_A_log': a_log_init(ks[11], (L, N_DIR, GDN_HEADS)),
        'gdn_dt_bias': dt_bias_init(ks[12], (L, N_DIR, GDN_HEADS)),
        'gdn_norm_g': 1.0 + nrm(ks[13], (L, GDN_DV), 0.02),
        'mb_conv_w': nrm(ks[14], (L, CONV_K, MB_XBC), CONV_K ** -0.5),
        'mb_conv_b': nrm(ks[15], (L, MB_XBC), 0.02),
        'mb_A_log': a_log_init(ks[16], (L, N_DIR, MB_HEADS)),
        'mb_dt_bias': dt_bias_init(ks[17], (L, N_DIR, MB_HEADS)),
        'mb_D': 1.0 + nrm(ks[18], (L, MB_HEADS), 0.02),
        'mb_norm_g': 1.0 + nrm(ks[19], (L, MB_INNER), 0.02),
        'w_branch_gdn': nrm(ks[20], (L, GDN_V, D), GDN_V ** -0.5),
        'w_branch_mb': nrm(ks[21], (L, MB_INNER, D), MB_INNER ** -0.5),
        'w_out': nrm(ks[22], (L, D, D), D ** -0.5),
        'final_g': 1.0 + nrm(ks[23], (D,), 0.02),
    }


def reference(x, c, ctx, c_ctx, w_ada, b_ada, norm_g, ffn_w_gu, ffn_w_down, w_in, gdn_conv_w, gdn_A_log,
              gdn_dt_bias, gdn_norm_g, mb_conv_w, mb_conv_b, mb_A_log, mb_dt_bias, mb_D, mb_norm_g,
              w_branch_gdn, w_branch_mb, w_out, final_g):
    b = x.shape[0]
    xc = ctx
    for i in range(DEPTH):
        last = i == DEPTH - 1
        m = jnp.split((jax.nn.silu(c) @ w_ada[i] + b_ada[i])[:, None, :], 9, axis=-1)
        mc = jnp.split(jax.nn.silu(c_ctx) @ w_ada[i] + b_ada[i], 9, axis=-1)

        x = ffn_sublayer(x, norm_g[i, 0], m[0], m[1], m[2], ffn_w_gu[i, 0], ffn_w_down[i, 0])
        xc = ffn_sublayer(xc, norm_g[i, 0], mc[0], mc[1], mc[2], ffn_w_gu[i, 0], ffn_w_down[i, 0])

        hc = rmsnorm(xc, norm_g[i, 1]) * (1.0 + mc[4]) + mc[3]
        h = rmsnorm(x, norm_g[i, 1]) * (1.0 + m[4]) + m[3]
        fc = mixer_features(hc, w_in[i], gdn_conv_w[i], mb_conv_w[i], mb_conv_b[i], dwconv)
        f = mixer_features(h, w_in[i], gdn_conv_w[i], mb_conv_w[i], mb_conv_b[i], grid_conv)
        s_gdn0 = jnp.zeros((N_DIR, b, GDN_HEADS, GDN_DK, GDN_DV), jnp.float32)
        s_mb0 = jnp.zeros((N_DIR, b, MB_GROUPS, MB_HPG, MB_HEADDIM, MB_STATE), jnp.float32)
        oc, s_gdn = gdn_mixer(fc, gdn_A_log[i], gdn_dt_bias[i], s_gdn0)
        yc, s_mb = ssd_mixer(fc, mb_A_log[i], mb_dt_bias[i], s_mb0)
        o, _ = gdn_mixer(f, gdn_A_log[i], gdn_dt_bias[i], s_gdn)
        y, _ = ssd_mixer(f, mb_A_log[i], mb_dt_bias[i], s_mb)
        x = x + m[5] * mixer_out(f, o, y, gdn_norm_g[i], mb_D[i], mb_norm_g[i],
                                 w_branch_gdn[i], w_branch_mb[i], w_out[i])

        x = ffn_sublayer(x, norm_g[i, 2], m[6], m[7], m[8], ffn_w_gu[i, 1], ffn_w_down[i, 1])
        if not last:
            xc = xc + mc[5] * mixer_out(fc, oc, yc, gdn_norm_g[i], mb_D[i], mb_norm_g[i],
                                        w_branch_gdn[i], w_branch_mb[i], w_out[i])
            xc = ffn_sublayer(xc, norm_g[i, 2], mc[6], mc[7], mc[8], ffn_w_gu[i, 1], ffn_w_down[i, 1])
    return rmsnorm(x, final_g)
```

```python
import os
from contextlib import ExitStack
import numpy as np
import concourse.bass as bass
import concourse.mybir as mybir
from concourse.bass_utils import run_bass_kernel_spmd

F32 = mybir.dt.float32
BF16 = mybir.dt.bfloat16
AF = mybir.ActivationFunctionType
ALU = mybir.AluOpType
AX = mybir.AxisListType

PE, ACT, DVE, POOL, SP = "pe", "act", "dve", "pool", "sp"
ENGS = [PE, ACT, DVE, POOL, SP]
EPOCH = 24000
NEPOCH = 4
NDMASEM = 12

D = 1024
T = 2304
NCH = 36
DFF = 2816
DIN = 10848
EPS = 1e-6
BLKS = [(0, 256), (256, 512), (768, 512), (1280, 512), (1792, 512)]
LBLKS = BLKS[1:]
STAGE = int(os.environ.get("KSTAGE", "99"))


class Op:
    __slots__ = ("eng", "fn", "deps", "sig", "is_dma", "idx", "signals", "q")


class Prog:
    def __init__(self, nc, stack):
        self.nc = nc
        self.ops = []
        self.pending = []
        self.lastw = {}
        self.readers = {}
        self.dma_count = 0
        self.dma_last = {}
        self.cnt = {e: 0 for e in ENGS}
        self.dcnt = [0] * NDMASEM
        self.sems = {}
        for e in ENGS:
            for ep in range(NEPOCH):
                self.sems[(e, ep)] = stack.enter_context(nc.semaphore(f"s_{e}_{ep}"))
        self.dsems = [stack.enter_context(nc.semaphore(f"s_dma_{i}")) for i in range(NDMASEM)]
        self.seen = {e: {} for e in ENGS}
        self.barrier = []
        self.lastop = {}
        self.got_barrier = set()
        self.final_waits = []

    def add(self, eng, name, r=(), w=(), dma=False, **kw):
        o = Op()
        o.eng = eng
        o.is_dma = dma
        o.sig = None
        o.signals = False
        o.q = None
        o.fn = lambda e: getattr(e, name)(**kw)
        o.idx = len(self.ops)
        deps = set()
        pr = [k for k in r if isinstance(k, tuple) and k and k[0] == "ps"]
        if pr:
            r = [k for k in r if k not in pr]
            w = list(w) + pr
        for k in r:
            lw = self.lastw.get(k)
            if lw is not None:
                deps.add(lw)
        for k in w:
            lw = self.lastw.get(k)
            if lw is not None:
                deps.add(lw)
            for rd in self.readers.get(k, ()):
                deps.add(rd)
        for k in r:
            self.readers.setdefault(k, []).append(o.idx)
        for k in w:
            self.lastw[k] = o.idx
            self.readers[k] = []
        if dma:
            slot = self.dma_count % NDMASEM
            o.q = slot
            prev = self.dma_last.get(slot)
            if prev is not None:
                deps.add(prev)
            self.dma_last[slot] = o.idx
            self.dma_count += 1
        if eng not in self.got_barrier:
            self.got_barrier.add(eng)
            deps.update(self.barrier)
        deps.discard(o.idx)
        o.deps = deps
        self.ops.append(o)
        self.pending.append(o)
        if not dma:
            self.lastop[eng] = o.idx
        return o

    def dma(self, out, in_, r=(), w=(), eng=SP):
        return self.add(eng, "dma_start", r=r, w=w, dma=True, out=out, in_=in_)

    def sem_of(self, sig):
        if sig[0] == "d":
            return ("d", sig[1]), self.dsems[sig[1]], sig[2]
        return ("c", sig[1], sig[2]), self.sems[(sig[1], sig[2])], sig[3]

    def flush(self, final=False):
        nc = self.nc
        ops = self.ops
        pend = self.pending
        self.pending = []
        nb = [i for i in self.lastop.values()] + [i for i in self.dma_last.values()]
        for i in nb:
            ops[i].signals = True
        for o in pend:
            for d in o.deps:
                od = ops[d]
                if od.eng == PE and o.eng == PE and not od.is_dma and not o.is_dma:
                    continue
                if od.sig is None:
                    od.signals = True
        for o in pend:
            if o.is_dma:
                self.dcnt[o.q] += 16
                o.sig = ("d", o.q, self.dcnt[o.q])
            elif o.signals:
                c = self.cnt[o.eng]
                assert c // EPOCH < NEPOCH
                o.sig = ("c", o.eng, c // EPOCH, (c % EPOCH) + 1)
                self.cnt[o.eng] = c + 1
        per = {e: [o for o in pend if o.eng == e] for e in ENGS}
        finals = list(self.final_waits) if final else []

        def run(engobj, ename):
            seen = self.seen[ename]
            for o in per[ename]:
                waits = {}
                for d in o.deps:
                    od = ops[d]
                    if od.sig is None:
                        assert od.eng == PE and o.eng == PE, (od.eng, o.eng, d, o.idx)
                        continue
                    key, sh, val = self.sem_of(od.sig)
                    if seen.get(key, 0) >= val:
                        continue
                    if key not in waits or waits[key][1] < val:
                        waits[key] = (sh, val)
                for key, (sh, val) in waits.items():
                    engobj.wait_ge(sh, val)
                    seen[key] = val
                ins = o.fn(engobj)
                if o.sig is not None:
                    key, sh, val = self.sem_of(o.sig)
                    ins.then_inc(sh, 16 if o.is_dma else 1)
            if ename == SP:
                for i in finals:
                    key, sh, val = self.sem_of(ops[i].sig)
                    engobj.wait_ge(sh, val)

        with nc.Block() as block:
            @block.tensor
            def _(e):
                run(e, PE)

            @block.scalar
            def _(e):
                run(e, ACT)

            @block.vector
            def _(e):
                run(e, DVE)

            @block.gpsimd
            def _(e):
                run(e, POOL)

            @block.sync
            def _(e):
                run(e, SP)
        self.barrier = nb
        self.got_barrier = set()
        self.lastw = {}
        self.readers = {}


def v3(ap, b):
    return ap.rearrange("p (a b) -> p a b", b=b)


def build_program():
    nc = bass.Bass("TRN2", target_bir_lowering=False)
    xin = nc.dram_tensor("xin", [T, D], F32, kind="ExternalInput").ap()
    crow = nc.dram_tensor("crow", [16, 128], F32, kind="ExternalInput").ap()
    rows1 = nc.dram_tensor("rows1", [128, 128], F32, kind="ExternalInput").ap()
    rows2 = nc.dram_tensor("rows2", [128, 128], F32, kind="ExternalInput").ap()
    rows3 = nc.dram_tensor("rows3", [128, 128], F32, kind="ExternalInput").ap()
    brow = nc.dram_tensor("brow", [1, 256], F32, kind="ExternalInput").ap()
    w_ada = nc.dram_tensor("w_ada", [D, 9 * D], F32, kind="ExternalInput").ap()
    w_gu = nc.dram_tensor("w_gu", [2, D, 2 * DFF], F32, kind="ExternalInput").ap()
    w_dn = nc.dram_tensor("w_dn", [2, DFF, D], F32, kind="ExternalInput").ap()
    w_in = nc.dram_tensor("w_in", [D, DIN], F32, kind="ExternalInput").ap()
    w_bg = nc.dram_tensor("w_bg", [D, D], F32, kind="ExternalInput").ap()
    w_bm = nc.dram_tensor("w_bm", [2 * D, D], F32, kind="ExternalInput").ap()
    w_o = nc.dram_tensor("w_o", [D, D], F32, kind="ExternalInput").ap()
    out = nc.dram_tensor("out", [2048, D], F32, kind="ExternalOutput").ap()
    xsp = nc.dram_tensor("xsp", [128, 8, 2048], F32, kind="Internal").ap()
    yab = nc.dram_tensor("yab", [24, 128, 2048], BF16, kind="Internal").ap()

    w_ada_v = w_ada.rearrange("(k p) n -> p k n", p=128)
    w_in_v = w_in.rearrange("(k p) n -> p k n", p=128)

    with ExitStack() as g:
        uniq = [0]

        def sb(st, name, shape, dt=F32):
            uniq[0] += 1
            return st.enter_context(nc.sbuf_tensor(f"{name}_{uniq[0]}", shape, dt))

        P = Prog(nc, g)
        ps = [g.enter_context(nc.psum_tensor(f"ps{i}", [128, 512], F32)) for i in range(8)]
        PSK = [("ps", i) for i in range(8)]

        ident_f = sb(g, "ident_f", [128, 128])
        ident_b = sb(g, "ident_b", [128, 128], BF16)
        ones_f = sb(g, "ones_f", [128, 128])
        zeros_f = sb(g, "zeros_f", [64, 256])
        negones = sb(g, "negones", [64, 64])
        m_le = sb(g, "m_le", [64, 64])
        m_ge = sb(g, "m_ge", [64, 64])
        am_f = sb(g, "am_f", [64, 128])
        am_b = sb(g, "am_b", [64, 128])
        neg4_f = sb(g, "neg4_f", [64, 256])
        neg4_b = sb(g, "neg4_b", [64, 256])
        cols1 = sb(g, "cols1", [128, 128])
        cols2 = sb(g, "cols2", [128, 128])
        cols3 = sb(g, "cols3", [128, 128])
        mv = sb(g, "mv", [128, 144])
        hT = sb(g, "hT", [128, 8, T], BF16)
        brt = sb(g, "brt", [64, 256])
        epsc = sb(g, "epsc", [128, 1])

        def mvap(s, w, kind, k):
            i = ((s * 2 + w) * 3 + kind) * 8 + k
            return mv[:, i:i + 1]

        P.add(POOL, "memset", w=["ones"], ap=ones_f[:], constant=1.0)
        P.add(POOL, "memset", w=["zeros"], ap=zeros_f[:], constant=0.0)
        P.add(POOL, "memset", w=["epsc"], ap=epsc[:], constant=EPS)
        P.add(POOL, "memset", w=["negones"], ap=negones[:], constant=-1.0)
        P.add(POOL, "affine_select", r=["ones"], w=["ident_f"], out=ident_f[:], in_=ones_f[:], pattern=[[-1, 128]],
              compare_op=ALU.is_equal, fill=0.0, base=0, channel_multiplier=1)
        P.add(DVE, "tensor_copy", r=["ident_f"], w=["ident_b"], out=ident_b[:], in_=ident_f[:])
        P.add(POOL, "affine_select", r=["ones"], w=["m_le"], out=m_le[:], in_=ones_f[0:64, 0:64], pattern=[[1, 64]],
              compare_op=ALU.is_ge, fill=0.0, base=0, channel_multiplier=-1)
        P.add(POOL, "affine_select", r=["ones"], w=["m_ge"], out=m_ge[:], in_=ones_f[0:64, 0:64], pattern=[[-1, 64]],
              compare_op=ALU.is_ge, fill=0.0, base=0, channel_multiplier=1)
        P.add(POOL, "affine_select", r=["zeros"], w=["am_f"], out=am_f[:, 0:64], in_=zeros_f[:, 0:64], pattern=[[1, 64]],
              compare_op=ALU.is_ge, fill=-30000.0, base=0, channel_multiplier=-1)
        P.add(POOL, "affine_select", r=["zeros"], w=["am_f"], out=am_f[:, 64:128], in_=zeros_f[:, 0:64], pattern=[[-1, 64]],
              compare_op=ALU.is_gt, fill=30000.0, base=0, channel_multiplier=1)
        P.add(POOL, "affine_select", r=["zeros"], w=["am_b"], out=am_b[:, 0:64], in_=zeros_f[:, 0:64], pattern=[[-1, 64]],
              compare_op=ALU.is_ge, fill=-30000.0, base=0, channel_multiplier=1)
        P.add(POOL, "affine_select", r=["zeros"], w=["am_b"], out=am_b[:, 64:128], in_=zeros_f[:, 0:64], pattern=[[1, 64]],
              compare_op=ALU.is_gt, fill=30000.0, base=0, channel_multiplier=-1)
        P.add(POOL, "affine_select", r=["zeros"], w=["neg4"], out=v3(neg4_f[:], 64), in_=v3(zeros_f[:], 64),
              pattern=[[0, 4], [1, 64]], compare_op=ALU.is_ge, fill=-30000.0, base=0, channel_multiplier=-1)
        P.add(POOL, "affine_select", r=["zeros"], w=["neg4"], out=v3(neg4_b[:], 64), in_=v3(zeros_f[:], 64),
              pattern=[[0, 4], [-1, 64]], compare_op=ALU.is_ge, fill=-30000.0, base=0, channel_multiplier=1)
        rst = sb(g, "rst", [128, 128])
        for i, (rw, cl) in enumerate(((rows1, cols1), (rows2, cols2), (rows3, cols3))):
            P.dma(rst[:], rw, w=["rst"])
            P.add(PE, "transpose", r=["rst", "ident_f"], w=[PSK[0]], out=ps[0][:, 0:128], in_=rst[:], identity=ident_f[:])
            P.add(DVE, "tensor_copy", r=[PSK[0]], w=[("cols", i)], out=cl[:], in_=ps[0][:, 0:128])
        P.dma(brt[:], brow.partition_broadcast(64), w=["brt"])
        P.flush()

        cur = {}

        def rmsnorm_mod(s, blks):
            xT = cur["xT"]
            with ExitStack() as st:
                sq = [sb(st, f"nsq{i}", [128, 512]) for i in range(2)]
                rs = sb(st, "nrs", [128, 512])
                tmp = [sb(st, f"ntmp{i}", [128, 512]) for i in range(2)]
                for bi, (t0, n) in enumerate(blks):
                    w = 1 if t0 == 0 else 0
                    pn = 6 + (bi % 2)
                    for k in range(8):
                        P.add(ACT, "activation", r=[("xT", k, t0)], w=[("nsq", k % 2)], out=sq[k % 2][:, :n], in_=xT[:, k, t0:t0 + n], func=AF.Square)
                        P.add(PE, "matmul", r=[("nsq", k % 2), "ones"], w=[PSK[pn]], out=ps[pn][:, :n], lhsT=ones_f[:], rhs=sq[k % 2][:, :n],
                              start=(k == 0), stop=(k == 7))
                    P.add(ACT, "activation", r=[PSK[pn]], w=["nrs"], out=rs[:, :n], in_=ps[pn][:, :n], func=AF.Sqrt, scale=1.0 / D, bias=epsc[:, 0:1])
                    P.add(DVE, "reciprocal", r=["nrs"], w=["nrs"], out=rs[:, :n], in_=rs[:, :n])
                    for k in range(8):
                        P.add(DVE, "tensor_tensor", r=[("xT", k, t0), "nrs"], w=[("ntmp", k % 2)], out=tmp[k % 2][:, :n], in0=xT[:, k, t0:t0 + n],
                              in1=rs[:, :n], op=ALU.mult)
                        P.add(ACT, "activation", r=[("ntmp", k % 2), "mv"], w=[("hT", k, t0)], out=hT[:, k, t0:t0 + n], in_=tmp[k % 2][:, :n],
                              func=AF.Identity, scale=mvap(s, w, 0, k), bias=mvap(s, w, 1, k))
                P.flush()

        def ffn(s, li, blks):
            xT = cur["xT"]
            wgu_v = w_gu[li].rearrange("(k p) n -> p k n", p=128)
            groups = [list(range(0, 6)), list(range(6, 12)), list(range(12, 17)), list(range(17, 22))]
            with ExitStack() as st:
                act = sb(st, "fact", [128, 6, T], BF16)
                wgs = [sb(st, f"fwgs{i}", [128, 8, 256]) for i in range(2)]
                wgb = [sb(st, f"fwgb{i}", [128, 8, 256], BF16) for i in range(2)]
                wds = [sb(st, f"fwds{i}", [128, D]) for i in range(2)]
                wdb = sb(st, "fwdb", [128, 6, D], BF16)
                sl = [sb(st, f"fsl{i}", [128, 512]) for i in range(2)]
                it = 0
                for grp in groups:
                    for jj, j in enumerate(grp):
                        b = j % 2
                        P.dma(wgs[b][:, :, 0:128], wgu_v[:, :, j * 128:(j + 1) * 128], w=[("fwgs", b)])
                        P.dma(wgs[b][:, :, 128:256], wgu_v[:, :, DFF + j * 128:DFF + (j + 1) * 128], w=[("fwgs", b)])
                        P.add(POOL, "tensor_copy", r=[("fwgs", b)], w=[("fwgb", b)], out=wgb[b][:], in_=wgs[b][:])
                        for (t0, n) in blks:
                            q = it % 2
                            it += 1
                            pg, pu = q * 2, q * 2 + 1
                            for k in range(8):
                                P.add(PE, "matmul", r=[("fwgb", b), ("hT", k, t0)], w=[PSK[pg]], out=ps[pg][:, :n], lhsT=wgb[b][:, k, 0:128],
                                      rhs=hT[:, k, t0:t0 + n], start=(k == 0), stop=(k == 7))
                            for k in range(8):
                                P.add(PE, "matmul", r=[("fwgb", b), ("hT", k, t0)], w=[PSK[pu]], out=ps[pu][:, :n], lhsT=wgb[b][:, k, 128:256],
                                      rhs=hT[:, k, t0:t0 + n], start=(k == 0), stop=(k == 7))
                            P.add(ACT, "activation", r=[PSK[pg]], w=[("fsl", q)], out=sl[q][:, :n], in_=ps[pg][:, :n], func=AF.Silu)
                            P.add(DVE, "tensor_tensor", r=[("fsl", q), PSK[pu]], w=[("fact", jj, t0)], out=act[:, jj, t0:t0 + n], in0=sl[q][:, :n],
                                  in1=ps[pu][:, :n], op=ALU.mult)
                    for jj, j in enumerate(grp):
                        b = j % 2
                        P.dma(wds[b][:], w_dn[li, j * 128:(j + 1) * 128, :], w=[("fwds", b)])
                        P.add(POOL, "tensor_copy", r=[("fwds", b)], w=[("fwdb", jj)], out=wdb[:, jj, :], in_=wds[b][:])
                    ng = len(grp)
                    for d in range(8):
                        for (t0, n) in blks:
                            w = 1 if t0 == 0 else 0
                            q = 4 + (it % 2)
                            it += 1
                            for jj in range(ng):
                                P.add(PE, "matmul", r=[("fwdb", jj), ("fact", jj, t0)], w=[PSK[q]], out=ps[q][:, :n],
                                      lhsT=wdb[:, jj, d * 128:(d + 1) * 128], rhs=act[:, jj, t0:t0 + n], start=(jj == 0), stop=(jj == ng - 1))
                            P.add(DVE, "scalar_tensor_tensor", r=[PSK[q], ("xT", d, t0), "mv"], w=[("xT", d, t0)], out=xT[:, d, t0:t0 + n],
                                  in0=ps[q][:, :n], scalar=mvap(s, w, 2, d), in1=xT[:, d, t0:t0 + n], op0=ALU.mult, op1=ALU.add)
                P.flush()

        def write_out(final_norm):
            xT = cur["xT"]
            with ExitStack() as st:
                sq = [sb(st, f"osq{i}", [128, 512]) for i in range(2)]
                rs = sb(st, "ors", [128, 512])
                yt = [sb(st, f"oyt{i}", [128, 8, 512]) for i in range(2)]
                ost = [sb(st, f"oost{i}", [128, D]) for i in range(2)]
                fin = []
                for bi, (t0, n) in enumerate(LBLKS):
                    yb_ = yt[bi % 2]
                    if final_norm:
                        pn = 6 + (bi % 2)
                        for k in range(8):
                            P.add(ACT, "activation", r=[("xT", k, t0)], w=[("osq", k % 2)], out=sq[k % 2][:, :n], in_=xT[:, k, t0:t0 + n], func=AF.Square)
                            P.add(PE, "matmul", r=[("osq", k % 2), "ones"], w=[PSK[pn]], out=ps[pn][:, :n], lhsT=ones_f[:], rhs=sq[k % 2][:, :n],
                                  start=(k == 0), stop=(k == 7))
                        P.add(ACT, "activation", r=[PSK[pn]], w=["ors"], out=rs[:, :n], in_=ps[pn][:, :n], func=AF.Sqrt, scale=1.0 / D, bias=epsc[:, 0:1])
                        P.add(DVE, "reciprocal", r=["ors"], w=["ors"], out=rs[:, :n], in_=rs[:, :n])
                        for k in range(8):
                            P.add(DVE, "scalar_tensor_tensor", r=[("xT", k, t0), "ors", ("cols", 0)], w=[("oyt", bi % 2, k)], out=yb_[:, k, :n],
                                  in0=xT[:, k, t0:t0 + n], scalar=cols1[:, 96 + k:97 + k], in1=rs[:, :n], op0=ALU.mult, op1=ALU.mult)
                    else:
                        for k in range(8):
                            P.add(DVE if k % 2 else POOL, "tensor_copy", r=[("xT", k, t0)], w=[("oyt", bi % 2, k)], out=yb_[:, k, :n], in_=xT[:, k, t0:t0 + n])
                    for tt in range(4):
                        ti = bi * 4 + tt
                        ob = ost[ti % 2]
                        for half in range(2):
                            pi = 2 * (ti % 2) + half
                            for j in range(4):
                                k = half * 4 + j
                                P.add(PE, "transpose", r=[("oyt", bi % 2, k), "ident_f"], w=[PSK[pi]], out=ps[pi][:, j * 128:(j + 1) * 128],
                                      in_=yb_[:, k, tt * 128:(tt + 1) * 128], identity=ident_f[:])
                            if half:
                                P.add(ACT, "copy", r=[PSK[pi]], w=[("oost", ti % 2)], out=ob[:, half * 512:(half + 1) * 512], in_=ps[pi][:, 0:512])
                            else:
                                P.add(DVE, "tensor_copy", r=[PSK[pi]], w=[("oost", ti % 2)], out=ob[:, half * 512:(half + 1) * 512], in_=ps[pi][:, 0:512])
                        o = P.dma(out[ti * 128:(ti + 1) * 128, :], ob[:], r=[("oost", ti % 2)])
                        fin.append(o.idx)
                P.final_waits = fin
                P.flush(final=True)

        def gdn(dbg=False):
            with ExitStack() as mx:
                gr = sb(mx, "gr", [64, NCH, 8, 2])
                bet = sb(mx, "bet", [64, NCH, 8, 2])
                nbet = sb(mx, "nbet", [64, NCH, 8, 2])
                gcs = sb(mx, "gcs", [64, NCH, 8, 2])
                ngcs = sb(mx, "ngcs", [64, NCH, 8, 2])
                egc = sb(mx, "egc", [64, NCH, 8, 2])
                ks = sb(mx, "ks", [64, NCH, 8, 2])
                kbs = sb(mx, "kbs", [64, NCH, 8, 2])
                etot = sb(mx, "etot", [128, NCH, 8, 2])
                br2 = sb(mx, "br2", [64, 2, 8, 2])
                gng = sb(mx, "gng", [64, 128])
                m2 = sb(mx, "m2", [64, 2, 128])
                am2 = sb(mx, "am2", [64, 2, 128])
                pro = ExitStack()
                wsm_s = sb(pro, "wsm_s", [128, 8, 32])
                wsm_b = sb(pro, "wsm_b", [128, 8, 32], BF16)
                ab = sb(pro, "ab", [64, NCH, 2, 8, 2])
                P.dma(wsm_s[:], w_in_v[:, :, 4096:4128], w=["wsm_s"])
                P.dma(gng[:], rows1[104:105, :].partition_broadcast(64), w=["gng"])
                P.add(POOL, "tensor_copy", r=["wsm_s"], w=["wsm_b"], out=wsm_b[:], in_=wsm_s[:])
                for hh in range(2):
                    P.add(POOL, "tensor_copy", r=["m_le"], w=["m2"], out=m2[:, 0, hh * 64:(hh + 1) * 64], in_=m_le[:])
                    P.add(POOL, "tensor_copy", r=["m_ge"], w=["m2"], out=m2[:, 1, hh * 64:(hh + 1) * 64], in_=m_ge[:])
                P.add(POOL, "tensor_copy", r=["am_f"], w=["am2"], out=am2[:, 0, :], in_=am_f[:])
                P.add(POOL, "tensor_copy", r=["am_b"], w=["am2"], out=am2[:, 1, :], in_=am_b[:])
                P.add(DVE, "tensor_copy", r=["brt"], w=["br2"], out=br2[:, 0, :, :].rearrange("p h d -> p d h"),
                      in_=brt[:, 0:16].rearrange("p (d h) -> p d h", d=2))
                P.add(ACT, "activation", r=["brt"], w=["br2"], out=br2[:, 1, :, :].rearrange("p h d -> p d h"),
                      in_=brt[:, 96:112].rearrange("p (d h) -> p d h", d=2), func=AF.Exp)
                P.add(DVE, "tensor_scalar", r=["br2"], w=["br2"], out=br2[:, 1, :, :], in0=br2[:, 1, :, :], scalar1=-1.0, scalar2=None, op0=ALU.mult)
                PSTOP = int(os.environ.get("KPSTOP", "9"))
                if PSTOP == 0:
                    P.flush(); pro.close(); return
                for c in range(NCH):
                    pi = c % 2
                    for k in range(8):
                        P.add(PE, "matmul", r=["wsm_b"], w=[PSK[pi]], out=ps[pi][0:64, 0:32], lhsT=hT[:, k, c * 64:(c + 1) * 64], rhs=wsm_b[:, k, :],
                              start=(k == 0), stop=(k == 7))
                    P.add(DVE if pi else ACT, "tensor_copy" if pi else "copy", r=[PSK[pi]], w=["ab"],
                          out=ab[:, c, :, :, :].rearrange("p a h d -> p a d h"), in_=ps[pi][0:64, 0:32].rearrange("p (a d h) -> p a d h", a=2, d=2))
                bshape = [64, NCH, 8, 2]
                if PSTOP == 1:
                    P.flush(); pro.close(); return
                P.add(DVE, "tensor_tensor", r=["ab", "br2"], w=["gr"], out=gr[:], in0=ab[:, :, 0, :, :], in1=br2[:, 0, :, :].unsqueeze(1).broadcast_to(bshape), op=ALU.add)
                P.add(ACT, "activation", r=["gr"], w=["gr"], out=gr[:], in_=gr[:], func=AF.Exp)
                P.add(ACT, "activation", r=["gr"], w=["gr"], out=gr[:], in_=gr[:], func=AF.Ln, bias=1.0)
                P.add(DVE, "tensor_tensor", r=["gr", "br2"], w=["gr"], out=gr[:], in0=gr[:], in1=br2[:, 1, :, :].unsqueeze(1).broadcast_to(bshape), op=ALU.mult)
                P.add(ACT, "activation", r=["ab"], w=["bet"], out=bet[:], in_=ab[:, :, 1, :, :], func=AF.Sigmoid)
                P.add(DVE, "tensor_scalar", r=["bet"], w=["nbet"], out=nbet[:], in0=bet[:], scalar1=-1.0, scalar2=None, op0=ALU.mult)
                if PSTOP == 2:
                    P.flush(); pro.close(); return
                for half in range(2):
                    cs = slice(half * 18, half * 18 + 18)
                    rhs = gr[:, cs, :, :].rearrange("p c h d -> p (c h d)")
                    P.add(PE, "matmul", r=["gr", "m_le"], w=[PSK[2]], out=ps[2][0:64, 0:288], lhsT=m_le[:], rhs=rhs, start=True, stop=True)
                    P.add(PE, "matmul", r=["gr", "m_ge"], w=[PSK[3]], out=ps[3][0:64, 0:288], lhsT=m_ge[:], rhs=rhs, start=True, stop=True)
                    P.add(PE, "matmul", r=["gr", "ones"], w=[PSK[4]], out=ps[4][:, 0:288], lhsT=ones_f[0:64, :], rhs=rhs, start=True, stop=True)
                    P.add(DVE, "tensor_copy", r=[PSK[2]], w=["gcs"], out=gcs[:, cs, :, 0], in_=ps[2][0:64, 0:288].rearrange("p (c h d) -> p c h d", h=8, d=2)[:, :, :, 0])
                    P.add(DVE, "tensor_copy", r=[PSK[3]], w=["gcs"], out=gcs[:, cs, :, 1], in_=ps[3][0:64, 0:288].rearrange("p (c h d) -> p c h d", h=8, d=2)[:, :, :, 1])
                    P.add(ACT, "activation", r=[PSK[4]], w=["etot"], out=etot[:, cs, :, :].rearrange("p c h d -> p (c h d)"), in_=ps[4][:, 0:288], func=AF.Exp)
                    P.add(DVE, "tensor_tensor", r=[PSK[4], "gcs"], w=["ks"], out=ks[:, cs, :, :].rearrange("p c h d -> p (c h d)"), in0=ps[4][0:64, 0:288],
                          in1=gcs[:, cs, :, :].rearrange("p c h d -> p (c h d)"), op=ALU.subtract)
                P.add(ACT, "activation", r=["ks"], w=["ks"], out=ks[:], in_=ks[:], func=AF.Exp)
                P.add(ACT, "activation", r=["gcs"], w=["egc"], out=egc[:], in_=gcs[:], func=AF.Exp)
                P.add(DVE, "tensor_scalar", r=["gcs"], w=["ngcs"], out=ngcs[:], in0=gcs[:], scalar1=-1.0, scalar2=None, op0=ALU.mult)
                P.add(DVE, "tensor_tensor", r=["egc", "bet"], w=["kbs"], out=kbs[:], in0=egc[:], in1=bet[:], op=ALU.mult)
                P.flush()
                pro.close()
                GSTOP = int(os.environ.get("KGSTOP", "9"))
                if GSTOP == 0:
                    return

                wst = [sb(mx, "gwst0", [128, 8, 128])] * 2
                wb = [sb(mx, f"gwb{i}", [128, 8, 128], BF16) for i in range(2)]
                feat = sb(mx, "gfeat", [128, 2436 + T])
                cin = feat[:, 0:2436]
                co = feat[:, 2436:2436 + T]
                qT = sb(mx, "gqT", [128, T])
                kT = sb(mx, "gkT", [128, T])
                qTb = sb(mx, "gqTb", [128, T], BF16)
                sqt = [sb(mx, "gsq0", [128, 512])] * 2
                rsn = sb(mx, "grsn", [128, 512])
                k_tok = sb(mx, "gktok", [64, NCH, 128], BF16)
                v_tok = sb(mx, "gvtok", [64, NCH, 128], BF16)
                z_tok = sb(mx, "gztok", [64, 32, 128], BF16)
                o_tok = sb(mx, "gotok", [64, 32, 128])
                u_tok = feat[0:64, 0:4608].bitcast(BF16).rearrange("p (c d f) -> p c d f", c=NCH, d=2)
                wT = sb(mx, "gwT", [128, 2, T], BF16)
                qkd = sb(mx, "gqkd", [64, NCH, 2, 64], BF16)
                ko = [sb(mx, f"gko{i}", [64, 128], BF16) for i in range(4)]
                kq = [sb(mx, f"gkq{i}", [64, 128]) for i in range(2)]
                R1 = [sb(mx, f"gR1{i}", [64, 2, 128]) for i in range(2)]
                E = [sb(mx, f"gE{i}", [64, 2, 128]) for i in range(2)]
                tmpA = [sb(mx, f"gtA{i}", [64, 2, 64]) for i in range(2)]
                ABs = [sb(mx, f"gAB{i}", [64, 2, 128]) for i in range(4)]
                Xs = [sb(mx, f"gX{i}", [64, 2, 64]) for i in range(4)]
                TTb = [sb(mx, f"gTT{i}", [64, 2, 64], BF16) for i in range(2)]
                vb = [sb(mx, f"gvb{i}", [64, 2, 128], BF16) for i in range(2)]
                kbg = [sb(mx, f"gkbg{i}", [64, 2, 128], BF16) for i in range(2)]
                S = [sb(mx, f"gS{i}", [128, 128]) for i in range(2)]
                Sb = [sb(mx, f"gSb{i}", [128, 128], BF16) for i in range(2)]
                vnew = [sb(mx, f"gvn{i}", [64, 128], BF16) for i in range(4)]
                otmp = [sb(mx, f"got{i}", [64, 128]) for i in range(4)]
                ssq = sb(mx, "gssq", [64, 32])
                yst = [sb(mx, f"gyst{i}", [128, 512], BF16) for i in range(2)]
                psb = ps[7][:].bitcast(BF16)
                cinL = cin[:, 260:2436].rearrange("p (r c) -> p r c", c=68)
                id64 = ident_f[0:64, 0:64]
                offs4 = [0, 1024, 2048, 3072]

                for h in range(8):
                    P.add(POOL, "memset", w=["cin"], ap=cin, constant=0.0)

                    def project(c4, evac):
                        wbi = c4 % 2
                        P.dma(wst[wbi][:], w_in_v[:, :, offs4[c4] + h * 128:offs4[c4] + (h + 1) * 128], w=["gwst"])
                        P.add(POOL, "tensor_copy", r=["gwst"], w=[("gwb", wbi)], out=wb[wbi][:], in_=wst[wbi][:])
                        for bi, (t0, n) in enumerate(BLKS):
                            pi = bi % 2
                            for k in range(8):
                                P.add(PE, "matmul", r=[("gwb", wbi)], w=[PSK[pi]], out=ps[pi][:, :n], lhsT=wb[wbi][:, k, :], rhs=hT[:, k, t0:t0 + n],
                                      start=(k == 0), stop=(k == 7))
                            evac(bi, t0, n, pi)

                    def evac_cin(bi, t0, n, pi):
                        if t0 == 0:
                            P.add(ACT, "copy", r=[PSK[pi]], w=["cin"], out=cin[:, 2:258], in_=ps[pi][:, 0:256])
                        else:
                            r0 = (t0 - 256) // 64
                            P.add(ACT, "copy", r=[PSK[pi]], w=["cin"], out=cinL[:, r0:r0 + 8, 2:66], in_=v3(ps[pi][:, 0:512], 64))

                    def conv_silu(c4, dst):
                        for kk in range(5):
                            wc = cols2[:, kk * 24 + c4 * 8 + h:kk * 24 + c4 * 8 + h + 1]
                            if kk == 0:
                                P.add(DVE, "tensor_scalar", r=["cin", ("cols", 1)], w=["co_c"], out=co[:, 0:256], in0=cin[:, kk:kk + 256], scalar1=wc, scalar2=None, op0=ALU.mult)
                                P.add(DVE, "tensor_scalar", r=["cin", ("cols", 1)], w=["co_l"], out=v3(co[:, 256:T], 64), in0=cinL[:, :, kk:kk + 64], scalar1=wc, scalar2=None, op0=ALU.mult)
                            else:
                                P.add(DVE, "scalar_tensor_tensor", r=["cin", ("cols", 1), "co_c"], w=["co_c"], out=co[:, 0:256], in0=cin[:, kk:kk + 256], scalar=wc,
                                      in1=co[:, 0:256], op0=ALU.mult, op1=ALU.add)
                                P.add(DVE, "scalar_tensor_tensor", r=["cin", ("cols", 1), "co_l"], w=["co_l"], out=v3(co[:, 256:T], 64), in0=cinL[:, :, kk:kk + 64], scalar=wc,
                                      in1=v3(co[:, 256:T], 64), op0=ALU.mult, op1=ALU.add)
                        P.add(ACT, "activation", r=["co_c", "co_l"], w=[dst[1]], out=dst[0][:, :], in_=co, func=AF.Silu)

                    def l2norm(src, key, scale, extra_bf16=None):
                        for bi, (t0, n) in enumerate(BLKS):
                            pn = 2 + (bi % 2)
                            P.add(ACT, "activation", r=[key], w=["gsq"], out=sqt[bi % 2][:, :n], in_=src[:, t0:t0 + n], func=AF.Square)
                            P.add(PE, "matmul", r=["gsq", "ones"], w=[PSK[pn]], out=ps[pn][:, :n], lhsT=ones_f[:], rhs=sqt[bi % 2][:, :n], start=True, stop=True)
                            P.add(ACT, "activation", r=[PSK[pn]], w=["grsn"], out=rsn[:, :n], in_=ps[pn][:, :n], func=AF.Sqrt, bias=epsc[:, 0:1])
                            P.add(DVE, "reciprocal", r=["grsn"], w=["grsn"], out=rsn[:, :n], in_=rsn[:, :n])
                            P.add(DVE, "scalar_tensor_tensor", r=[key, "grsn"], w=[key], out=src[:, t0:t0 + n], in0=src[:, t0:t0 + n], scalar=scale, in1=rsn[:, :n],
                                  op0=ALU.mult, op1=ALU.mult)
                        if extra_bf16 is not None:
                            P.add(POOL, "tensor_copy", r=[key], w=["gqTb"], out=extra_bf16[:, :], in_=src[:, :])

                    def to_tok(src, key, dst, dkey, c0, silu=False):
                        nch = NCH - c0
                        for g4 in range(nch // 4):
                            pi = g4 % 2
                            for j in range(4):
                                c = c0 + g4 * 4 + j
                                P.add(PE, "transpose", r=[key, "ident_f"], w=[PSK[pi]], out=ps[pi][0:64, j * 128:(j + 1) * 128], in_=src[:, c * 64:(c + 1) * 64], identity=ident_f[:])
                            if silu:
                                P.add(ACT, "activation", r=[PSK[pi]], w=[dkey], out=dst[:, g4 * 4:g4 * 4 + 4, :], in_=v3(ps[pi][0:64, 0:512], 128), func=AF.Silu)
                            else:
                                P.add(DVE if pi else ACT, "tensor_copy" if pi else "copy", r=[PSK[pi]], w=[dkey], out=dst[:, g4 * 4:g4 * 4 + 4, :], in_=v3(ps[pi][0:64, 0:512], 128))

                    project(0, evac_cin); conv_silu(0, (qT, "gqT")); l2norm(qT, "gqT", 128 ** -0.5, qTb)
                    project(1, evac_cin); conv_silu(1, (kT, "gkT")); l2norm(kT, "gkT", 1.0)
                    to_tok(kT, "gkT", k_tok, "gktok", 0)
                    project(2, evac_cin); conv_silu(2, (co, "gco"))
                    to_tok(co, "gco", v_tok, "gvtok", 0)
                    project(3, lambda bi, t0, n, pi: P.add(ACT, "copy", r=[PSK[pi]], w=["gco"], out=co[:, t0:t0 + n], in_=ps[pi][:, :n]))
                    to_tok(co, "gco", z_tok, "gztok", 4, silu=True)
                    P.flush()
                    if GSTOP == 1:
                        return

                    for c in range(NCH):
                        a = c % 2
                        pk = 2 + a
                        pw = 4 + a
                        P.add(PE, "matmul", r=["gkT"], w=[PSK[pk]], out=ps[pk][0:64, 0:64], lhsT=kT[:, c * 64:(c + 1) * 64], rhs=kT[:, c * 64:(c + 1) * 64], start=True, stop=True)
                        P.add(PE, "matmul", r=["gkT", "gqT"], w=[PSK[pk]], out=ps[pk][0:64, 64:128], lhsT=kT[:, c * 64:(c + 1) * 64], rhs=qT[:, c * 64:(c + 1) * 64], start=True, stop=True)
                        P.add(ACT, "copy", r=[PSK[pk]], w=[("gkq", a)], out=kq[a][:], in_=ps[pk][0:64, 0:128])
                        for d in range(2):
                            P.add(DVE, "tensor_scalar", r=["m2", "gr"], w=[("gR1", a)], out=R1[a][:, d, :], in0=m2[:, d, :], scalar1=gr[:, c, h, d:d + 1], scalar2=None, op0=ALU.mult)
                        P.add(PE, "matmul", r=[("gR1", a), "ones"], w=[PSK[pk]], out=ps[pk][0:64, 128:384], lhsT=ones_f[0:64, 0:64], rhs=R1[a][:].rearrange("p d f -> p (d f)"), start=True, stop=False)
                        P.add(PE, "matmul", r=["am2", "ident_f"], w=[PSK[pk]], out=ps[pk][0:64, 128:384], lhsT=id64, rhs=am2[:].rearrange("p d f -> p (d f)"), start=False, stop=True)
                        for d in range(2):
                            P.add(ACT, "activation", r=[PSK[pk], "ngcs"], w=[("gE", a)], out=E[a][:, d, 0:64], in_=ps[pk][0:64, 128 + d * 128:128 + d * 128 + 64], func=AF.Exp,
                                  bias=ngcs[:, c, h, d:d + 1], scale=1.0)
                            P.add(ACT, "activation", r=[PSK[pk], "gcs"], w=[("gE", a)], out=E[a][:, d, 64:128], in_=ps[pk][0:64, 128 + d * 128 + 64:128 + (d + 1) * 128], func=AF.Exp,
                                  bias=gcs[:, c, h, d:d + 1], scale=-1.0)
                        P.add(DVE, "tensor_tensor", r=[("gkq", a), ("gE", a)], w=["gqkd"], out=qkd[:, c, :, :], in0=kq[a][:, 64:128].unsqueeze(1).broadcast_to([64, 2, 64]),
                              in1=E[a][:, :, 0:64], op=ALU.mult)
                        P.add(DVE, "tensor_tensor", r=[("gkq", a), ("gE", a)], w=[("gtA", a)], out=tmpA[a][:], in0=kq[a][:, 0:64].unsqueeze(1).broadcast_to([64, 2, 64]),
                              in1=E[a][:, :, 64:128], op=ALU.mult)
                        ab0 = ABs[2 * a]
                        ab1 = ABs[2 * a + 1]
                        P.add(DVE, "tensor_tensor", r=[("gtA", a), "nbet"], w=[("gAB", 2 * a, "A")], out=ab0[:, :, 0:64], in0=tmpA[a][:],
                              in1=nbet[:, c, h, :].unsqueeze(2).broadcast_to([64, 2, 64]), op=ALU.mult)
                        for d in range(2):
                            P.add(PE, "transpose", r=[("gAB", 2 * a, "A"), "ident_f"], w=[PSK[pw]], out=ps[pw][0:64, d * 64:(d + 1) * 64], in_=ab0[:, d, 0:64], identity=id64)
                        P.add(ACT, "copy", r=[PSK[pw]], w=[("gAB", 2 * a, "B")], out=ab0[:, :, 64:128], in_=v3(ps[pw][0:64, 0:128], 64))
                        x0 = Xs[2 * a]
                        x1 = Xs[2 * a + 1]
                        P.add(DVE, "tensor_tensor", r=[PSK[pw], "ident_f"], w=[("gX", 2 * a)], out=x0[:], in0=v3(ps[pw][0:64, 0:128], 64),
                              in1=id64.unsqueeze(1).broadcast_to([64, 2, 64]), op=ALU.add)
                        cur_ab, nxt_ab, ci, ni = ab0, ab1, 2 * a, 2 * a + 1
                        cur_x, nxt_x, cxi, nxi = x0, x1, 2 * a, 2 * a + 1
                        for lvl in range(5):
                            last = lvl == 4
                            for d in range(2):
                                P.add(PE, "matmul", r=[("gAB", ci, "A"), ("gAB", ci, "B")], w=[PSK[pw]], out=ps[pw][0:64, 128 + d * 128:128 + d * 128 + 64],
                                      lhsT=cur_ab[:, d, 64:128], rhs=cur_ab[:, d, 0:64], start=True, stop=True)
                                if not last:
                                    P.add(PE, "matmul", r=[("gAB", ci, "A"), ("gAB", ci, "B")], w=[PSK[pw]], out=ps[pw][0:64, 128 + d * 128 + 64:128 + (d + 1) * 128],
                                          lhsT=cur_ab[:, d, 0:64], rhs=cur_ab[:, d, 64:128], start=True, stop=True)
                            if last:
                                P.add(ACT, "copy", r=[PSK[pw]], w=[("gAB", ni, "A")], out=nxt_ab[:, :, 0:64], in_=v3(ps[pw][0:64, 128:384], 128)[:, :, 0:64])
                            else:
                                P.add(ACT, "copy", r=[PSK[pw]], w=[("gAB", ni, "A"), ("gAB", ni, "B")], out=nxt_ab[:], in_=v3(ps[pw][0:64, 128:384], 128))
                            for d in range(2):
                                P.add(PE, "matmul", r=[("gAB", ni, "A"), ("gX", cxi)], w=[PSK[pw]], out=ps[pw][0:64, 384 + d * 64:384 + (d + 1) * 64],
                                      lhsT=nxt_ab[:, d, 0:64], rhs=cur_x[:, d, :], start=True, stop=True)
                            if last:
                                P.add(DVE, "tensor_tensor", r=[PSK[pw], ("gX", cxi)], w=[("gTT", a)], out=TTb[a][:], in0=v3(ps[pw][0:64, 384:512], 64), in1=cur_x[:], op=ALU.add)
                            else:
                                P.add(DVE, "tensor_tensor", r=[PSK[pw], ("gX", cxi)], w=[("gX", nxi)], out=nxt_x[:], in0=v3(ps[pw][0:64, 384:512], 64), in1=cur_x[:], op=ALU.add)
                            cur_ab, nxt_ab, ci, ni = nxt_ab, cur_ab, ni, ci
                            cur_x, nxt_x, cxi, nxi = nxt_x, cur_x, nxi, cxi
                        P.add(POOL, "tensor_tensor", r=["gvtok", "bet"], w=[("gvb", a)], out=vb[a][:], in0=v_tok[:, c, :].unsqueeze(1).broadcast_to([64, 2, 128]),
                              in1=bet[:, c, h, :].unsqueeze(2).broadcast_to([64, 2, 128]), op=ALU.mult)
                        P.add(POOL, "tensor_tensor", r=["gktok", "kbs"], w=[("gkbg", a)], out=kbg[a][:], in0=k_tok[:, c, :].unsqueeze(1).broadcast_to([64, 2, 128]),
                              in1=kbs[:, c, h, :].unsqueeze(2).broadcast_to([64, 2, 128]), op=ALU.mult)
                        pu = 6 + a
                        for d in range(2):
                            P.add(PE, "matmul", r=[("gTT", a), ("gvb", a)], w=[PSK[pu]], out=ps[pu][0:64, d * 128:(d + 1) * 128], lhsT=TTb[a][:, d, :], rhs=vb[a][:, d, :], start=True, stop=True)
                            P.add(PE, "matmul", r=[("gTT", a), ("gkbg", a)], w=[PSK[pu]], out=ps[pu][:, 256 + d * 64:256 + (d + 1) * 64], lhsT=kbg[a][:, d, :], rhs=TTb[a][:, d, :], start=True, stop=True)
                        P.add(ACT, "copy", r=[PSK[pu]], w=["gutok"], out=u_tok[:, c, :, :], in_=v3(ps[pu][0:64, 0:256], 128))
                        P.add(DVE, "tensor_copy", r=[PSK[pu]], w=["gwT"], out=wT[:, :, c * 64:(c + 1) * 64], in_=v3(ps[pu][:, 256:384], 64))
                    P.flush()
                    if GSTOP == 2:
                        return

                    orders = [list(range(NCH)), [3, 2, 1, 0] + list(range(NCH - 1, 3, -1))]
                    for d in range(2):
                        P.add(POOL, "memset", w=[("gS", d)], ap=S[d][:], constant=0.0)
                        P.add(POOL, "memset", w=[("gSb", d)], ap=Sb[d][:], constant=0.0)
                    P.add(POOL, "memset", w=[("gotok", c) for c in range(4, NCH)], ap=o_tok[:], constant=0.0)
                    for step in range(NCH):
                        for d in range(2):
                            c = orders[d][step]
                            pb = d
                            vi = d * 2 + (step % 2)
                            P.add(PE, "matmul", r=["gwT", ("gSb", d)], w=[PSK[pb]], out=ps[pb][0:64, 0:128], lhsT=wT[:, d, c * 64:(c + 1) * 64], rhs=Sb[d][:], start=True, stop=True)
                            P.add(DVE, "tensor_tensor", r=["gutok", PSK[pb]], w=[("gvn", vi)], out=vnew[vi][:], in0=u_tok[:, c, d, :], in1=ps[pb][0:64, 0:128], op=ALU.subtract)
                            if c >= 4:
                                P.add(PE, "matmul", r=["gqTb", ("gSb", d)], w=[PSK[2 + pb]], out=ps[2 + pb][0:64, 0:128], lhsT=qTb[:, c * 64:(c + 1) * 64], rhs=Sb[d][:], start=True, stop=True)
                                P.add(PE, "matmul", r=["gqkd", ("gvn", vi)], w=[PSK[4 + pb]], out=ps[4 + pb][0:64, 0:128], lhsT=qkd[:, c, d, :], rhs=vnew[vi][:], start=True, stop=True)
                                P.add(ACT, "copy", r=[PSK[4 + pb]], w=[("got", vi)], out=otmp[vi][:], in_=ps[4 + pb][0:64, 0:128])
                                P.add(DVE, "scalar_tensor_tensor", r=[PSK[2 + pb], ("got", vi), "egc"], w=[("got", vi)], out=otmp[vi][:], in0=ps[2 + pb][0:64, 0:128],
                                      scalar=egc[:, c, h, d:d + 1], in1=otmp[vi][:], op0=ALU.mult, op1=ALU.add)
                                P.add(POOL, "tensor_tensor", r=[("got", vi), ("gotok", c)], w=[("gotok", c)], out=o_tok[:, c - 4, :], in0=o_tok[:, c - 4, :], in1=otmp[vi][:], op=ALU.add)
                            P.add(POOL, "tensor_scalar", r=["gktok", "ks"], w=[("gko", vi)], out=ko[vi][:], in0=k_tok[:, c, :], scalar1=ks[:, c, h, d:d + 1], scalar2=None, op0=ALU.mult)
                            P.add(PE, "matmul", r=[("gko", vi), ("gvn", vi)], w=[PSK[6 + pb]], out=ps[6 + pb][:, 0:128], lhsT=ko[vi][:], rhs=vnew[vi][:], start=True, stop=True)
                            P.add(DVE, "scalar_tensor_tensor", r=[("gS", d), PSK[6 + pb], "etot"], w=[("gS", d)], out=S[d][:], in0=S[d][:], scalar=etot[:, c, h, d:d + 1], in1=ps[6 + pb][:, 0:128],
                                  op0=ALU.mult, op1=ALU.add)
                            P.add(ACT, "copy", r=[("gS", d)], w=[("gSb", d)], out=Sb[d][:], in_=S[d][:])
                    P.flush()
                    if dbg:
                        P.dma(out.rearrange("(c p) n -> p c n", p=64)[:, :, h * 128:(h + 1) * 128], o_tok[:], r=[("gotok", c) for c in range(4, NCH)])
                        P.flush()
                        continue
                    sq16 = u_tok[:].rearrange("p c d f -> p (c d) f")[:, 0:32, :]
                    ya_tok = k_tok[:, 0:32, :]
                    P.add(DVE, "tensor_tensor", r=[("gotok", c) for c in range(4, NCH)], w=["gutok"], out=sq16, in0=o_tok[:], in1=o_tok[:], op=ALU.mult)
                    P.add(DVE, "tensor_reduce", r=["gutok"], w=["gssq"], out=ssq[:], in_=sq16, axis=AX.X, op=ALU.add)
                    P.add(ACT, "activation", r=["gssq"], w=["gssq"], out=ssq[:], in_=ssq[:], func=AF.Sqrt, scale=1.0 / 128, bias=epsc[0:64, 0:1])
                    P.add(DVE, "reciprocal", r=["gssq"], w=["gssq"], out=ssq[:], in_=ssq[:])
                    P.add(DVE, "tensor_tensor", r=["gssq"] + [("gotok", c) for c in range(4, NCH)], w=[("gotok", c) for c in range(4, NCH)], out=o_tok[:], in0=o_tok[:],
                          in1=ssq[:].unsqueeze(2).broadcast_to([64, 32, 128]), op=ALU.mult)
                    P.add(DVE, "tensor_tensor", r=["gng"] + [("gotok", c) for c in range(4, NCH)], w=[("gotok", c) for c in range(4, NCH)], out=o_tok[:], in0=o_tok[:],
                          in1=gng[:].unsqueeze(1).broadcast_to([64, 32, 128]), op=ALU.mult)
                    P.add(DVE, "tensor_tensor", r=["gztok"] + [("gotok", c) for c in range(4, NCH)], w=["gktok"], out=ya_tok, in0=o_tok[:], in1=z_tok[:], op=ALU.mult)
                    for bi in range(4):
                        for j in range(8):
                            P.add(PE, "transpose", r=["gktok", "ident_b"], w=[PSK[7]], out=psb[:, j * 64:(j + 1) * 64], in_=ya_tok[:, bi * 8 + j, :], identity=ident_b[0:64, 0:64])
                        P.add(ACT, "copy", r=[PSK[7]], w=[("gyst", bi % 2)], out=yst[bi % 2][:], in_=psb[:, 0:512])
                        P.dma(yab[h, :, bi * 512:(bi + 1) * 512], yst[bi % 2][:], r=[("gyst", bi % 2)])
                    P.flush()

        def ssd(dbg=False):
            XBC0 = 6176
            with ExitStack() as mx:
                m2d = sb(mx, "sm2d", [64, 2, 64])
                neg8 = sb(mx, "sneg8", [64, 2, 4, 64])
                Dt = sb(mx, "sDt", [64, 32])
                for d_, m_ in ((0, m_le), (1, m_ge)):
                    P.add(POOL, "tensor_copy", w=["sm2d"], out=m2d[:, d_, :], in_=m_[:])
                P.add(POOL, "tensor_copy", w=["sneg8"], out=neg8[:, 0, :, :], in_=v3(neg4_f[:], 64))
                P.add(POOL, "tensor_copy", w=["sneg8"], out=neg8[:, 1, :, :], in_=v3(neg4_b[:], 64))
                P.add(POOL, "tensor_copy", w=["sDt"], out=Dt[:], in_=brt[:, 192:224])
                BT = sb(mx, "sBT", [128, T], BF16)
                CT = sb(mx, "sCT", [128, T], BF16)
                B_tok = sb(mx, "sBtok", [64, NCH, 128], BF16)
                cbT = sb(mx, "scbT", [64, NCH, 64])
                wst = sb(mx, "swst", [128, 8, 128])
                wb = [sb(mx, f"swb{i}", [128, 8, 128], BF16) for i in range(2)]
                feat = sb(mx, "sfeat", [128, 2436 + T])
                cin = feat[:, 0:2436]
                co = feat[:, 2436:2436 + T]
                cinL = cin[:, 260:2436].rearrange("p (r c) -> p r c", c=68)
                xs_tok = sb(mx, "sxstok", [64, NCH, 256], BF16)
                z_tok = sb(mx, "sztok", [64, 32, 256], BF16)
                y_tok = sb(mx, "sytok", [64, 32, 256])
                wsm_s = sb(mx, "swsm_s", [128, 8, 8])
                wsm_b = sb(mx, "swsm_b", [128, 8, 8], BF16)
                dtr = sb(mx, "sdtr", [64, NCH, 4, 2])
                ar = sb(mx, "sar", [64, NCH, 4, 2])
                acs = sb(mx, "sacs", [64, NCH, 4, 2])
                eacs = sb(mx, "seacs", [64, NCH, 4, 2])
                wsc = sb(mx, "swsc", [64, NCH, 4, 2])
                etot = sb(mx, "setot", [128, NCH, 4, 2])
                bq = sb(mx, "sbq", [64, 2, 4, 2])
                R1 = [sb(mx, f"sR1{i}", [64, 2, 4, 64]) for i in range(2)]
                MT = [sb(mx, f"sMT{i}", [64, 2, 4, 64]) for i in range(2)]
                MTb = [sb(mx, f"sMTb{i}", [64, 2, 4, 64], BF16) for i in range(2)]
                xsdt = [sb(mx, f"sxsdt{i}", [64, 2, 4, 64], BF16) for i in range(2)]
                xsw = [sb(mx, f"sxsw{i}", [64, 256], BF16) for i in range(4)]
                ytmp = [sb(mx, f"sytmp{i}", [64, 256]) for i in range(4)]
                ST = [sb(mx, f"sST{i}", [128, 256]) for i in range(2)]
                STb = [sb(mx, f"sSTb{i}", [128, 256], BF16) for i in range(2)]
                yst = [sb(mx, f"syst{i}", [128, 512], BF16) for i in range(2)]
                psb = ps[7][:].bitcast(BF16)
                id64 = ident_f[0:64, 0:64]

                def project(col0, evac, wbi):
                    P.dma(wst[:], w_in_v[:, :, col0:col0 + 128], w=["swst"])
                    P.add(POOL, "tensor_copy", r=["swst"], w=[("swb", wbi)], out=wb[wbi][:], in_=wst[:])
                    for bi, (t0, n) in enumerate(BLKS):
                        pi = bi % 2
                        for k in range(8):
                            P.add(PE, "matmul", r=[("swb", wbi)], w=[PSK[pi]], out=ps[pi][:, :n], lhsT=wb[wbi][:, k, :], rhs=hT[:, k, t0:t0 + n], start=(k == 0), stop=(k == 7))
                        evac(bi, t0, n, pi)

                def evac_cin(bi, t0, n, pi):
                    if t0 == 0:
                        P.add(ACT, "copy", r=[PSK[pi]], w=["cin"], out=cin[:, 2:258], in_=ps[pi][:, 0:256])
                    else:
                        r0 = (t0 - 256) // 64
                        P.add(ACT, "copy", r=[PSK[pi]], w=["cin"], out=cinL[:, r0:r0 + 8, 2:66], in_=v3(ps[pi][:, 0:512], 64))

                def evac_co(bi, t0, n, pi):
                    P.add(ACT, "copy", r=[PSK[pi]], w=["sco"], out=co[:, t0:t0 + n], in_=ps[pi][:, :n])

                def conv_silu(ctile, dst, dkey):
                    for kk in range(5):
                        wc = cols3[:, kk * 20 + ctile:kk * 20 + ctile + 1]
                        if kk == 0:
                            P.add(DVE, "tensor_scalar", r=["cin"], w=["co_c", "sco"], out=co[:, 0:256], in0=cin[:, kk:kk + 256], scalar1=wc, scalar2=None, op0=ALU.mult)
                            P.add(DVE, "tensor_scalar", r=["cin"], w=["co_l", "sco"], out=v3(co[:, 256:T], 64), in0=cinL[:, :, kk:kk + 64], scalar1=wc, scalar2=None, op0=ALU.mult)
                        else:
                            P.add(DVE, "scalar_tensor_tensor", r=["cin", "co_c"], w=["co_c"], out=co[:, 0:256], in0=cin[:, kk:kk + 256], scalar=wc, in1=co[:, 0:256], op0=ALU.mult, op1=ALU.add)
                            P.add(DVE, "scalar_tensor_tensor", r=["cin", "co_l"], w=["co_l"], out=v3(co[:, 256:T], 64), in0=cinL[:, :, kk:kk + 64], scalar=wc, in1=v3(co[:, 256:T], 64),
                                  op0=ALU.mult, op1=ALU.add)
                    P.add(ACT, "activation", r=["co_c", "co_l"], w=[dkey], out=dst, in_=co, func=AF.Silu, bias=cols3[:, 100 + ctile:101 + ctile])

                def to_tok(src, key, dst_fn, dkey, c0, silu=False):
                    nch = NCH - c0
                    for g4 in range(nch // 4):
                        pi = g4 % 2
                        for j in range(4):
                            c = c0 + g4 * 4 + j
                            P.add(PE, "transpose", r=[key, "ident_f"], w=[PSK[pi]], out=ps[pi][0:64, j * 128:(j + 1) * 128], in_=src[:, c * 64:(c + 1) * 64], identity=ident_f[:])
                        if silu:
                            P.add(ACT, "activation", r=[PSK[pi]], w=[dkey], out=dst_fn(g4), in_=v3(ps[pi][0:64, 0:512], 128), func=AF.Silu)
                        else:
                            P.add(DVE if pi else ACT, "tensor_copy" if pi else "copy", r=[PSK[pi]], w=[dkey], out=dst_fn(g4), in_=v3(ps[pi][0:64, 0:512], 128))

                SG = int(os.environ.get("KSSDG", "0"))
                for quad in (range(4 * SG, 4 * SG + 4) if dbg else range(8)):
                    grp = quad // 4
                    P.add(POOL, "memset", w=["cin"], ap=cin, constant=0.0)
                    if quad % 4 == 0:
                        project(XBC0 + 2048 + grp * 128, evac_cin, 0)
                        conv_silu(16 + grp, co, "sco")
                        P.add(POOL, "tensor_copy", r=["sco"], w=["sBT"], out=BT[:], in_=co)
                        to_tok(co, "sco", lambda g4: B_tok[:, g4 * 4:g4 * 4 + 4, :], "sBtok", 0)
                        project(XBC0 + 2304 + grp * 128, evac_cin, 1)
                        conv_silu(18 + grp, CT[:], "sCT")
                        for c in range(NCH):
                            pi = 2 + (c % 2)
                            P.add(PE, "matmul", r=["sBT", "sCT"], w=[PSK[pi]], out=ps[pi][0:64, 0:64], lhsT=BT[:, c * 64:(c + 1) * 64], rhs=CT[:, c * 64:(c + 1) * 64], start=True, stop=True)
                            P.add(DVE if c % 2 else ACT, "tensor_copy" if c % 2 else "copy", r=[PSK[pi]], w=["scbT"], out=cbT[:, c, :], in_=ps[pi][0:64, 0:64])
                        P.flush()
                    for d in range(2):
                        c0_ = 8736 + d * 32 + quad * 4
                        P.dma(wsm_s[:, :, d * 4:(d + 1) * 4], w_in_v[:, :, c0_:c0_ + 4], w=["swsm_s"])
                    P.add(POOL, "tensor_copy", r=["swsm_s"], w=["swsm_b"], out=wsm_b[:], in_=wsm_s[:])
                    for d in range(2):
                        P.add(DVE, "tensor_copy", w=["sbq"], out=bq[:, 0, :, d], in_=brt[:, 32 + d * 32 + quad * 4:32 + d * 32 + quad * 4 + 4])
                        P.add(ACT, "activation", w=["sbq"], out=bq[:, 1, :, d], in_=brt[:, 128 + d * 32 + quad * 4:128 + d * 32 + quad * 4 + 4], func=AF.Exp)
                    P.add(DVE, "tensor_scalar", r=["sbq"], w=["sbq"], out=bq[:, 1, :, :], in0=bq[:, 1, :, :], scalar1=-1.0, scalar2=None, op0=ALU.mult)
                    for c in range(NCH):
                        pi = 2 + (c % 2)
                        for k in range(8):
                            P.add(PE, "matmul", r=["swsm_b"], w=[PSK[pi]], out=ps[pi][0:64, 0:8], lhsT=hT[:, k, c * 64:(c + 1) * 64], rhs=wsm_b[:, k, :], start=(k == 0), stop=(k == 7))
                        P.add(DVE if c % 2 else ACT, "tensor_copy" if c % 2 else "copy", r=[PSK[pi]], w=["sdtr"], out=dtr[:, c, :, :].rearrange("p h d -> p d h"),
                              in_=ps[pi][0:64, 0:8].rearrange("p (d h) -> p d h", d=2))
                    qshape = [64, NCH, 4, 2]
                    P.add(DVE, "tensor_tensor", r=["sdtr", "sbq"], w=["sdtr"], out=dtr[:], in0=dtr[:], in1=bq[:, 0, :, :].unsqueeze(1).broadcast_to(qshape), op=ALU.add)
                    P.add(ACT, "activation", r=["sdtr"], w=["sdtr"], out=dtr[:], in_=dtr[:], func=AF.Exp)
                    P.add(ACT, "activation", r=["sdtr"], w=["sdtr"], out=dtr[:], in_=dtr[:], func=AF.Ln, bias=1.0)
                    P.add(DVE, "tensor_tensor", r=["sdtr", "sbq"], w=["sar"], out=ar[:], in0=dtr[:], in1=bq[:, 1, :, :].unsqueeze(1).broadcast_to(qshape), op=ALU.mult)
                    rhs = ar[:].rearrange("p c h d -> p (c h d)")
                    P.add(PE, "matmul", r=["sar", "m_le"], w=[PSK[4]], out=ps[4][0:64, 0:288], lhsT=m_le[:], rhs=rhs, start=True, stop=True)
                    P.add(PE, "matmul", r=["sar", "m_ge"], w=[PSK[5]], out=ps[5][0:64, 0:288], lhsT=m_ge[:], rhs=rhs, start=True, stop=True)
                    P.add(PE, "matmul", r=["sar", "ones"], w=[PSK[6]], out=ps[6][:, 0:288], lhsT=ones_f[0:64, :], rhs=rhs, start=True, stop=True)
                    P.add(DVE, "tensor_copy", r=[PSK[4]], w=["sacs"], out=acs[:, :, :, 0], in_=ps[4][0:64, 0:288].rearrange("p (c h d) -> p c h d", h=4, d=2)[:, :, :, 0])
                    P.add(DVE, "tensor_copy", r=[PSK[5]], w=["sacs"], out=acs[:, :, :, 1], in_=ps[5][0:64, 0:288].rearrange("p (c h d) -> p c h d", h=4, d=2)[:, :, :, 1])
                    P.add(ACT, "activation", r=[PSK[6]], w=["setot"], out=etot[:].rearrange("p c h d -> p (c h d)"), in_=ps[6][:, 0:288], func=AF.Exp)
                    P.add(DVE, "tensor_tensor", r=[PSK[6], "sacs"], w=["swsc"], out=wsc[:].rearrange("p c h d -> p (c h d)"), in0=ps[6][0:64, 0:288],
                          in1=acs[:].rearrange("p c h d -> p (c h d)"), op=ALU.subtract)
                    P.add(ACT, "activation", r=["swsc"], w=["swsc"], out=wsc[:], in_=wsc[:], func=AF.Exp)
                    P.add(DVE, "tensor_tensor", r=["swsc", "sdtr"], w=["swsc"], out=wsc[:], in0=wsc[:], in1=dtr[:], op=ALU.mult)
                    P.add(ACT, "activation", r=["sacs"], w=["seacs"], out=eacs[:], in_=acs[:], func=AF.Exp)
                    for ft in range(2):
                        tile_i = quad * 2 + ft
                        project(XBC0 + tile_i * 128, evac_cin, ft)
                        conv_silu(tile_i, co, "sco")
                        to_tok(co, "sco", lambda g4, ft=ft: xs_tok[:, g4 * 4:g4 * 4 + 4, ft * 128:(ft + 1) * 128], "sxstok", 0)
                    for ft in range(2):
                        tile_i = quad * 2 + ft
                        project(4128 + tile_i * 128, evac_co, ft)
                        to_tok(co, "sco", lambda g4, ft=ft: z_tok[:, g4 * 4:g4 * 4 + 4, ft * 128:(ft + 1) * 128], "sztok", 4, silu=True)
                    P.flush()
                    for c in range(4, NCH):
                        a = c % 2
                        pm = 2 + a
                        py = 4 + a
                        P.add(DVE, "tensor_tensor", r=["sm2d", "sar"], w=[("sR1", a)], out=R1[a][:], in0=m2d[:].unsqueeze(2).broadcast_to([64, 2, 4, 64]),
                              in1=ar[:, c, :, :].rearrange("p h d -> p d h").unsqueeze(3).broadcast_to([64, 2, 4, 64]), op=ALU.mult)
                        P.add(PE, "matmul", r=[("sR1", a), "ones"], w=[PSK[pm]], out=ps[pm][0:64, 0:512], lhsT=ones_f[0:64, 0:64], rhs=R1[a][:].rearrange("p d h l -> p (d h l)"), start=True, stop=False)
                        P.add(PE, "matmul", r=["sneg8", "ident_f"], w=[PSK[pm]], out=ps[pm][0:64, 0:512], lhsT=id64, rhs=neg8[:].rearrange("p d h l -> p (d h l)"), start=False, stop=True)
                        P.add(DVE, "tensor_tensor", r=[PSK[pm], "sacs"], w=[("sMT", a)], out=MT[a][:], in0=ps[pm][0:64, 0:512].rearrange("p (d h l) -> p d h l", d=2, h=4),
                              in1=acs[:, c, :, :].rearrange("p h d -> p d h").unsqueeze(3).broadcast_to([64, 2, 4, 64]), op=ALU.subtract)
                        P.add(ACT, "activation", r=[("sMT", a)], w=[("sMT", a)], out=MT[a][:], in_=MT[a][:], func=AF.Exp)
                        P.add(DVE, "tensor_tensor", r=[("sMT", a), "scbT"], w=[("sMTb", a)], out=MTb[a][:].rearrange("p d h l -> p (d h) l"), in0=MT[a][:].rearrange("p d h l -> p (d h) l"),
                              in1=cbT[:, c, :].unsqueeze(1).broadcast_to([64, 8, 64]), op=ALU.mult)
                        P.add(POOL, "tensor_tensor", r=["sxstok", "sdtr"], w=[("sxsdt", a)], out=xsdt[a][:], in0=xs_tok[:, c, :].rearrange("p (h q) -> p h q", h=4).unsqueeze(1).broadcast_to([64, 2, 4, 64]),
                              in1=dtr[:, c, :, :].rearrange("p h d -> p d h").unsqueeze(3).broadcast_to([64, 2, 4, 64]), op=ALU.mult)
                        for hh in range(4):
                            for d in range(2):
                                P.add(PE, "matmul", r=[("sMTb", a), ("sxsdt", a)], w=[PSK[py]], out=ps[py][0:64, hh * 64:(hh + 1) * 64], lhsT=MTb[a][:, d, hh, :], rhs=xsdt[a][:, d, hh, :],
                                      start=(d == 0), stop=(d == 1))
                        P.add(ACT, "copy", r=[PSK[py]], w=[("sytok", c)], out=y_tok[:, c - 4, :], in_=ps[py][0:64, 0:256])
                    P.flush()
                    orders = [list(range(NCH)), [3, 2, 1, 0] + list(range(NCH - 1, 3, -1))]
                    for d in range(2):
                        P.add(POOL, "memset", w=[("sST", d)], ap=ST[d][:], constant=0.0)
                        P.add(POOL, "memset", w=[("sSTb", d)], ap=STb[d][:], constant=0.0)
                    for step in range(NCH):
                        for d in range(2):
                            c = orders[d][step]
                            vi = d * 2 + (step % 2)
                            po, pst = d, 2 + d
                            if c >= 4:
                                P.add(PE, "matmul", r=["sCT", ("sSTb", d)], w=[PSK[po]], out=ps[po][0:64, 0:256], lhsT=CT[:, c * 64:(c + 1) * 64], rhs=STb[d][:], start=True, stop=True)
                                P.add(DVE, "tensor_tensor", r=[PSK[po], "seacs"], w=[("sytmp", vi)], out=ytmp[vi][:].rearrange("p (h q) -> p h q", h=4),
                                      in0=ps[po][0:64, 0:256].rearrange("p (h q) -> p h q", h=4), in1=eacs[:, c, :, d].unsqueeze(2).broadcast_to([64, 4, 64]), op=ALU.mult)
                                P.add(POOL, "tensor_tensor", r=[("sytmp", vi), ("sytok", c)], w=[("sytok", c)], out=y_tok[:, c - 4, :], in0=y_tok[:, c - 4, :], in1=ytmp[vi][:], op=ALU.add)
                            P.add(POOL, "tensor_tensor", r=["sxstok", "swsc"], w=[("sxsw", vi)], out=xsw[vi][:].rearrange("p (h q) -> p h q", h=4), in0=xs_tok[:, c, :].rearrange("p (h q) -> p h q", h=4),
                                  in1=wsc[:, c, :, d].unsqueeze(2).broadcast_to([64, 4, 64]), op=ALU.mult)
                            P.add(PE, "matmul", r=["sBtok", ("sxsw", vi)], w=[PSK[pst]], out=ps[pst][:, 0:256], lhsT=B_tok[:, c, :], rhs=xsw[vi][:], start=True, stop=True)
                            P.add(DVE, "tensor_tensor", r=[("sST", d), "setot"], w=[("sST", d)], out=ST[d][:].rearrange("p (h q) -> p h q", h=4), in0=ST[d][:].rearrange("p (h q) -> p h q", h=4),
                                  in1=etot[:, c, :, d].unsqueeze(2).broadcast_to([128, 4, 64]), op=ALU.mult)
                            P.add(DVE, "tensor_tensor", r=[("sST", d), PSK[pst]], w=[("sST", d)], out=ST[d][:], in0=ST[d][:], in1=ps[pst][:, 0:256], op=ALU.add)
                            P.add(ACT, "copy", r=[("sST", d)], w=[("sSTb", d)], out=STb[d][:], in_=ST[d][:])
                    P.flush()
                    if dbg:
                        P.dma(out.rearrange("(c p) n -> p c n", p=64)[:, :, (quad % 4) * 256:(quad % 4 + 1) * 256], y_tok[:], r=[("sytok", c) for c in range(4, NCH)])
                        P.flush()
                        continue
                    scr = feat[0:64, 0:16 * 256].rearrange("p (c q) -> p c q", q=256)
                    yk = [("sytok", c) for c in range(4, NCH)]
                    for half in range(2):
                        cs = slice(half * 16, half * 16 + 16)
                        P.add(DVE, "tensor_tensor", r=["sxstok", "sDt"], w=["sscr"], out=scr.rearrange("p c (h q) -> p c h q", h=4),
                              in0=xs_tok[:, 4 + half * 16:4 + half * 16 + 16, :].rearrange("p c (h q) -> p c h q", h=4),
                              in1=Dt[:, quad * 4:quad * 4 + 4].unsqueeze(1).unsqueeze(3).broadcast_to([64, 16, 4, 64]), op=ALU.mult)
                        P.add(DVE, "tensor_tensor", r=["sscr"] + yk, w=yk, out=y_tok[:, cs, :], in0=y_tok[:, cs, :], in1=scr, op=ALU.add)
                    P.add(DVE, "tensor_tensor", r=["sztok"] + yk, w=["sztok"], out=z_tok[:], in0=y_tok[:], in1=z_tok[:], op=ALU.mult)
                    it = 0
                    for ft in range(2):
                        for bi in range(4):
                            for j in range(8):
                                P.add(PE, "transpose", r=["sztok", "ident_b"], w=[PSK[7]], out=psb[:, j * 64:(j + 1) * 64], in_=z_tok[:, bi * 8 + j, ft * 128:(ft + 1) * 128], identity=ident_b[0:64, 0:64])
                            P.add(ACT, "copy", r=[PSK[7]], w=[("syst", it % 2)], out=yst[it % 2][:], in_=psb[:, 0:512])
                            P.dma(yab[8 + quad * 2 + ft, :, bi * 512:(bi + 1) * 512], yst[it % 2][:], r=[("syst", it % 2)])
                            it += 1
                    P.flush()

        def merge():
            w_bg_v = w_bg.rearrange("(k p) n -> p k n", p=128)
            w_bm_v = w_bm.rearrange("(k p) n -> p k n", p=128)
            w_o_v = w_o.rearrange("(k p) n -> p k n", p=128)
            with ExitStack() as mx:
                print("merge: sbuf remaining", nc.sbuf_bytes_remaining, flush=True)
                yaT = sb(mx, "myaT", [128, 8, 1024], BF16)
                ybT = sb(mx, "mybT", [128, 16, 1024], BF16)
                mrg = sb(mx, "mmrg", [128, 8, 1024], BF16)
                wst = sb(mx, "mwst", [128, 16, 128])
                wgb = [sb(mx, f"mwgb{i}", [128, 8, 128], BF16) for i in range(2)]
                wmb = [sb(mx, "mwmb0", [128, 16, 128], BF16)] * 2
                gab = [sb(mx, f"mgab{i}", [128, 8, 128], BF16) for i in range(2)]
                gbb = [sb(mx, f"mgbb{i}", [128, 8, 128], BF16) for i in range(2)]
                sq = [sb(mx, "msq0", [128, 512])] * 2
                rs = sb(mx, "mrs", [128, 512])
                sg = [sb(mx, f"msg{i}", [128, 512]) for i in range(2)]
                ma = [sb(mx, f"mma{i}", [128, 512]) for i in range(2)]
                xb = [sb(mx, "mxb0", [128, 1024])] * 2
                for H0 in (0, 1024):
                    for t in range(8):
                        P.dma(yaT[:, t, :], yab[t][:, H0:H0 + 1024], w=[("myaT", t)])
                    for t in range(16):
                        P.dma(ybT[:, t, :], yab[8 + t][:, H0:H0 + 1024], w=[("mybT", t)])
                    for g_ in range(2):
                        for bi in range(2):
                            pn = 6 + (bi % 2)
                            tsl = slice(bi * 512, (bi + 1) * 512)
                            for t in range(8):
                                tt = g_ * 8 + t
                                P.add(ACT, "activation", r=[("mybT", tt)], w=[("msq", 0)], out=sq[t % 2][:], in_=ybT[:, tt, tsl], func=AF.Square)
                                P.add(PE, "matmul", r=[("msq", 0), "ones"], w=[PSK[pn]], out=ps[pn][:, :], lhsT=ones_f[:], rhs=sq[t % 2][:], start=(t == 0), stop=(t == 7))
                            P.add(ACT, "activation", r=[PSK[pn]], w=["mrs"], out=rs[:], in_=ps[pn][:, :], func=AF.Sqrt, scale=1.0 / 1024, bias=epsc[:, 0:1])
                            P.add(DVE, "reciprocal", r=["mrs"], w=["mrs"], out=rs[:], in_=rs[:])
                            for t in range(8):
                                tt = g_ * 8 + t
                                P.add(DVE, "scalar_tensor_tensor", r=[("mybT", tt), "mrs"], w=[("mybT", tt)], out=ybT[:, tt, tsl], in0=ybT[:, tt, tsl], scalar=cols1[:, 105 + tt:106 + tt],
                                      in1=rs[:], op0=ALU.mult, op1=ALU.mult)
                    it = 0
                    for d in range(8):
                        b = d % 2
                        dsl = slice(d * 128, (d + 1) * 128)
                        P.dma(wst[:, 0:8, :], w_bg_v[:, :, dsl], w=["mwst"])
                        P.add(POOL, "tensor_copy", r=["mwst"], w=[("mwgb", b)], out=wgb[b][:], in_=wst[:, 0:8, :])
                        P.dma(wst[:, :, :], w_bm_v[:, :, dsl], w=["mwst"])
                        P.add(POOL, "tensor_copy", r=["mwst"], w=[("mwmb", 0)], out=wmb[b][:], in_=wst[:, :, :])
                        P.dma(wst[:, 0:8, :], w_in_v[:, :, 8800 + d * 128:8800 + (d + 1) * 128], w=["mwst"])
                        P.add(POOL, "tensor_copy", r=["mwst"], w=[("mgab", b)], out=gab[b][:], in_=wst[:, 0:8, :])
                        P.dma(wst[:, 0:8, :], w_in_v[:, :, 9824 + d * 128:9824 + (d + 1) * 128], w=["mwst"])
                        P.add(POOL, "tensor_copy", r=["mwst"], w=[("mgbb", b)], out=gbb[b][:], in_=wst[:, 0:8, :])
                        for bi in range(2):
                            tsl = slice(bi * 512, (bi + 1) * 512)
                            hsl = slice(256 + H0 + bi * 512, 256 + H0 + (bi + 1) * 512)
                            q = it % 2
                            it += 1
                            pa, pga, pb_, pgb = q * 4, q * 4 + 1, q * 4 + 2, q * 4 + 3
                            for k in range(8):
                                P.add(PE, "matmul", r=[("mwgb", b), ("myaT", k)], w=[PSK[pa]], out=ps[pa][:, :], lhsT=wgb[b][:, k, :], rhs=yaT[:, k, tsl], start=(k == 0), stop=(k == 7))
                            for k in range(8):
                                P.add(PE, "matmul", r=[("mgab", b)], w=[PSK[pga]], out=ps[pga][:, :], lhsT=gab[b][:, k, :], rhs=hT[:, k, hsl], start=(k == 0), stop=(k == 7))
                            for k in range(16):
                                P.add(PE, "matmul", r=[("mwmb", 0), ("mybT", k)], w=[PSK[pb_]], out=ps[pb_][:, :], lhsT=wmb[b][:, k, :], rhs=ybT[:, k, tsl], start=(k == 0), stop=(k == 15))
                            for k in range(8):
                                P.add(PE, "matmul", r=[("mgbb", b)], w=[PSK[pgb]], out=ps[pgb][:, :], lhsT=gbb[b][:, k, :], rhs=hT[:, k, hsl], start=(k == 0), stop=(k == 7))
                            P.add(ACT, "activation", r=[PSK[pga]], w=[("msg", 0)], out=sg[0][:], in_=ps[pga][:, :], func=AF.Sigmoid)
                            P.add(DVE, "tensor_tensor", r=[("msg", 0), PSK[pa]], w=[("mma", 0)], out=ma[0][:], in0=sg[0][:], in1=ps[pa][:, :], op=ALU.mult)
                            P.add(ACT, "activation", r=[PSK[pgb]], w=[("msg", 1)], out=sg[1][:], in_=ps[pgb][:, :], func=AF.Sigmoid)
                            P.add(DVE, "tensor_tensor", r=[("msg", 1), PSK[pb_]], w=[("mma", 1)], out=ma[1][:], in0=sg[1][:], in1=ps[pb_][:, :], op=ALU.mult)
                            P.add(POOL, "tensor_tensor", r=[("mma", 0), ("mma", 1)], w=[("mmrg", d)], out=mrg[:, d, tsl], in0=ma[0][:], in1=ma[1][:], op=ALU.add)
                    for d in range(8):
                        b = d % 2
                        dsl = slice(d * 128, (d + 1) * 128)
                        P.dma(wst[:, 0:8, :], w_o_v[:, :, dsl], w=["mwst"])
                        P.add(POOL, "tensor_copy", r=["mwst"], w=[("mwgb", b)], out=wgb[b][:], in_=wst[:, 0:8, :])
                        P.dma(xb[b][:], xsp[:, d, H0:H0 + 1024], w=[("mxb", 0)])
                        for bi in range(2):
                            tsl = slice(bi * 512, (bi + 1) * 512)
                            po = (it % 2) * 4
                            it += 1
                            for k in range(8):
                                P.add(PE, "matmul", r=[("mwgb", b)] + [("mmrg", k)], w=[PSK[po]], out=ps[po][:, :], lhsT=wgb[b][:, k, :], rhs=mrg[:, k, tsl], start=(k == 0), stop=(k == 7))
                            P.add(DVE, "scalar_tensor_tensor", r=[PSK[po], ("mxb", 0)], w=[("mxb", 0)], out=xb[b][:, tsl], in0=ps[po][:, :], scalar=mvap(1, 0, 2, d), in1=xb[b][:, tsl],
                                  op0=ALU.mult, op1=ALU.add)
                        P.dma(xsp[:, d, H0:H0 + 1024], xb[b][:], r=[("mxb", 0)])
                    P.flush()

        with ExitStack() as pa:
            xT = sb(pa, "xT", [128, 8, T])
            cur["xT"] = xT
            with ExitStack() as s1:
                xst = [sb(s1, f"xst{i}", [128, D]) for i in range(2)]
                for t in range(18):
                    b = t % 2
                    P.dma(xst[b][:], xin[t * 128:(t + 1) * 128, :], w=[("xst", b)])
                    for half in range(2):
                        pi = 2 * b + half
                        for j in range(4):
                            P.add(PE, "transpose", r=[("xst", b), "ident_f"], w=[PSK[pi]], out=ps[pi][:, j * 128:(j + 1) * 128],
                                  in_=xst[b][:, (half * 4 + j) * 128:(half * 4 + j + 1) * 128], identity=ident_f[:])
                        wk = [("xT", k, (BLKS[0][0] if t < 2 else BLKS[1 + (t - 2) // 4][0])) for k in range(half * 4, half * 4 + 4)]
                        if half:
                            P.add(ACT, "copy", r=[PSK[pi]], w=wk, out=xT[:, half * 4:half * 4 + 4, t * 128:(t + 1) * 128], in_=v3(ps[pi][:, 0:512], 128))
                        else:
                            P.add(DVE, "tensor_copy", r=[PSK[pi]], w=wk, out=xT[:, half * 4:half * 4 + 4, t * 128:(t + 1) * 128], in_=v3(ps[pi][:, 0:512], 128))
                crt = sb(s1, "crt", [16, 128])
                cs2 = sb(s1, "cs2", [128, 8, 2])
                modT = sb(s1, "modT", [128, 72, 2])
                wst = [sb(s1, f"wst{i}", [128, 8, 512]) for i in range(2)]
                P.dma(crt[:], crow, w=["crt"])
                P.add(ACT, "activation", r=["crt"], w=["crt2"], out=crt[:], in_=crt[:], func=AF.Silu)
                P.add(PE, "transpose", r=["crt2", "ident_f"], w=[PSK[4]], out=ps[4][:, 0:16], in_=crt[:], identity=ident_f[0:16, 0:16])
                P.add(DVE, "tensor_copy", r=[PSK[4]], w=["cs2"], out=cs2[:, :, 0], in_=ps[4][:, 0:8])
                P.add(DVE, "tensor_copy", r=[PSK[4]], w=["cs2"], out=cs2[:, :, 1], in_=ps[4][:, 8:16])
                for cg in range(18):
                    b = cg % 2
                    P.dma(wst[b][:], w_ada_v[:, :, cg * 512:(cg + 1) * 512], w=[("wst", b)])
                    for jj in range(4):
                        j = cg * 4 + jj
                        for k in range(8):
                            P.add(PE, "matmul", r=[("wst", b), "cs2"], w=[PSK[5]], out=ps[5][:, 2 * j:2 * j + 2],
                                  lhsT=wst[b][:, k, jj * 128:(jj + 1) * 128], rhs=cs2[:, k, :], start=(k == 0), stop=(k == 7))
                P.add(DVE, "tensor_tensor", r=[PSK[5], ("cols", 0)], w=["modT"], out=modT[:], in0=v3(ps[5][:, 0:144], 2),
                      in1=cols1[:, 0:72].unsqueeze(2).broadcast_to([128, 72, 2]), op=ALU.add)
                for s in range(3):
                    for w in range(2):
                        i0 = ((s * 2 + w) * 3) * 8
                        P.add(DVE, "scalar_tensor_tensor", r=["modT", ("cols", 0)], w=["mv"], out=mv[:, i0:i0 + 8],
                              in0=modT[:, (3 * s + 1) * 8:(3 * s + 2) * 8, w], scalar=1.0, in1=cols1[:, 72 + s * 8:72 + s * 8 + 8],
                              op0=ALU.add, op1=ALU.mult)
                        P.add(DVE, "tensor_copy", r=["modT"], w=["mv"], out=mv[:, i0 + 8:i0 + 16], in_=modT[:, (3 * s) * 8:(3 * s + 1) * 8, w])
                        P.add(DVE, "tensor_scalar", r=["modT"], w=["mv"], out=mv[:, i0 + 16:i0 + 24],
                              in0=modT[:, (3 * s + 2) * 8:(3 * s + 3) * 8, w], scalar1=(1.0 if s == 1 else 0.5), scalar2=None, op0=ALU.mult)
                P.flush()
            if STAGE >= 1 and not os.environ.get("KSKIPFFN"):
                rmsnorm_mod(0, BLKS)
                ffn(0, 0, BLKS)
            if STAGE <= 1:
                write_out(False)
                return nc
            rmsnorm_mod(1, BLKS)
            for k in range(8):
                if not os.environ.get("KNOSPILL"):
                    P.dma(xsp[:, k, :], xT[:, k, 256:T], r=[("xT", k, t0) for (t0, n) in BLKS])
            P.flush()
        if STAGE != 3:
            gdn(dbg=(STAGE == 2))
        if STAGE >= 3:
            ssd(dbg=(STAGE == 3))
        if STAGE in (2, 3):
            P.final_waits = [i for i in P.dma_last.values()]
            P.add(POOL, "memset", w=["zeros"], ap=zeros_f[:], constant=0.0)
            P.flush(final=True)
            return nc
        merge()
        with ExitStack() as pc:
            xT = sb(pc, "xT2", [128, 8, T])
            cur["xT"] = xT
            for k in range(8):
                P.dma(xT[:, k, 256:T], xsp[:, k, :], w=[("xT", k, t0) for (t0, n) in LBLKS])
            if STAGE == 4:
                write_out(False)
                return nc
            rmsnorm_mod(2, LBLKS)
            ffn(2, 1, LBLKS)
            write_out(True)
        return nc


def prep_inputs(inputs, b):
    f = lambda a: np.ascontiguousarray(np.asarray(a, dtype=np.float32))
    m = {}
    m["xin"] = f(np.concatenate([inputs["ctx"][b], inputs["x"][b]], axis=0))
    m["crow"] = f(np.concatenate([np.asarray(inputs["c"][b]).reshape(8, 128), np.asarray(inputs["c_ctx"]).reshape(8, 128)], 0))
    r1 = np.zeros((128, 128), np.float32)
    r1[0:72] = np.asarray(inputs["b_ada"][0]).reshape(72, 128)
    r1[72:96] = np.asarray(inputs["norm_g"][0]).reshape(24, 128)
    r1[96:104] = np.asarray(inputs["final_g"]).reshape(8, 128)
    r1[104:105] = np.asarray(inputs["gdn_norm_g"][0]).reshape(1, 128)
    r1[105:121] = np.asarray(inputs["mb_norm_g"][0]).reshape(16, 128)
    m["rows1"] = r1
    r2 = np.zeros((128, 128), np.float32)
    r2[0:120] = np.asarray(inputs["gdn_conv_w"][0]).reshape(120, 128)
    m["rows2"] = r2
    r3 = np.zeros((128, 128), np.float32)
    r3[0:100] = np.asarray(inputs["mb_conv_w"][0]).reshape(100, 128)
    r3[100:120] = np.asarray(inputs["mb_conv_b"][0]).reshape(20, 128)
    m["rows3"] = r3
    br = np.zeros((1, 256), np.float32)
    br[0, 0:16] = np.asarray(inputs["gdn_dt_bias"][0]).reshape(16)
    br[0, 32:96] = np.asarray(inputs["mb_dt_bias"][0]).reshape(64)
    br[0, 96:112] = np.asarray(inputs["gdn_A_log"][0]).reshape(16)
    br[0, 128:192] = np.asarray(inputs["mb_A_log"][0]).reshape(64)
    br[0, 192:224] = np.asarray(inputs["mb_D"][0]).reshape(32)
    m["brow"] = br
    m["w_ada"] = f(inputs["w_ada"][0])
    m["w_gu"] = f(inputs["ffn_w_gu"][0])
    m["w_dn"] = f(inputs["ffn_w_down"][0])
    m["w_in"] = f(inputs["w_in"][0])
    m["w_bg"] = f(inputs["w_branch_gdn"][0])
    m["w_bm"] = f(inputs["w_branch_mb"][0])
    m["w_o"] = f(inputs["w_out"][0])
    return m


def kernel(**inputs):
    nc = build_program()
    shared = None
    in_maps = []
    for b in range(8):
        m = prep_inputs(inputs, b)
        if shared is None:
            shared = {k: m[k] for k in ("w_ada", "w_gu", "w_dn", "w_in", "w_bg", "w_bm", "w_o", "rows1", "rows2", "rows3", "brow")}
        else:
            m.update(shared)
        in_maps.append(m)
    res = run_bass_kernel_spmd(nc, in_maps, core_ids=list(range(8)))
    return np.stack([np.asarray(r["out"], dtype=np.float32) for r in res.results], axis=0)
```

```python
import os
from contextlib import ExitStack
import numpy as np
import concourse.bass as bass
import concourse.mybir as mybir
from concourse.bass_utils import run_bass_kernel_spmd

F32 = mybir.dt.float32
BF16 = mybir.dt.bfloat16
AF = mybir.ActivationFunctionType
ALU = mybir.AluOpType
AX = mybir.AxisListType

PE, ACT, DVE, POOL, SP = "pe", "act", "dve", "pool", "sp"
ENGS = [PE, ACT, DVE, POOL, SP]
EPOCH = 24000
NEPOCH = 4
NDMASEM = 12

D = 1024
T = 2304
NCH = 36
DFF = 2816
DIN = 10848
EPS = 1e-6
BLKS = [(0, 256), (256, 512), (768, 512), (1280, 512), (1792, 512)]
LBLKS = BLKS[1:]
STAGE = int(os.environ.get("KSTAGE", "99"))


class Op:
    __slots__ = ("eng", "fn", "deps", "sig", "is_dma", "idx", "signals", "q")


class Prog:
    def __init__(self, nc, stack):
        self.nc = nc
        self.ops = []
        self.pending = []
        self.lastw = {}
        self.readers = {}
        self.dma_count = 0
        self.dma_last = {}
        self.cnt = {e: 0 for e in ENGS}
        self.dcnt = [0] * NDMASEM
        self.sems = {}
        for e in ENGS:
            for ep in range(NEPOCH):
                self.sems[(e, ep)] = stack.enter_context(nc.semaphore(f"s_{e}_{ep}"))
        self.dsems = [stack.enter_context(nc.semaphore(f"s_dma_{i}")) for i in range(NDMASEM)]
        self.seen = {e: {} for e in ENGS}
        self.barrier = []
        self.lastop = {}
        self.got_barrier = set()
        self.final_waits = []

    def add(self, eng, name, r=(), w=(), dma=False, **kw):
        o = Op()
        o.eng = eng
        o.is_dma = dma
        o.sig = None
        o.signals = False
        o.q = None
        o.fn = lambda e: getattr(e, name)(**kw)
        o.idx = len(self.ops)
        deps = set()
        pr = [k for k in r if isinstance(k, tuple) and k and k[0] == "ps"]
        if pr:
            r = [k for k in r if k not in pr]
            w = list(w) + pr
        for k in r:
            lw = self.lastw.get(k)
            if lw is not None:
                deps.add(lw)
        for k in w:
            lw = self.lastw.get(k)
            if lw is not None:
                deps.add(lw)
            for rd in self.readers.get(k, ()):
                deps.add(rd)
        for k in r:
            lst = self.readers.setdefault(k, [])
            if not dma:
                lst[:] = [i for i in lst if self.ops[i].eng != eng or self.ops[i].is_dma]
            lst.append(o.idx)
        for k in w:
            self.lastw[k] = o.idx
            self.readers[k] = []
        if dma:
            slot = self.dma_count % NDMASEM
            o.q = slot
            prev = self.dma_last.get(slot)
            if prev is not None:
                deps.add(prev)
            self.dma_last[slot] = o.idx
            self.dma_count += 1
        if eng not in self.got_barrier:
            self.got_barrier.add(eng)
            deps.update(self.barrier)
        deps.discard(o.idx)
        o.deps = deps
        self.ops.append(o)
        self.pending.append(o)
        if not dma:
            self.lastop[eng] = o.idx
        return o

    def dma(self, out, in_, r=(), w=(), eng=SP):
        return self.add(eng, "dma_start", r=r, w=w, dma=True, out=out, in_=in_)

    def sem_of(self, sig):
        if sig[0] == "d":
            return ("d", sig[1]), self.dsems[sig[1]], sig[2]
        return ("c", sig[1], sig[2]), self.sems[(sig[1], sig[2])], sig[3]

    def flush(self, final=False):
        nc = self.nc
        ops = self.ops
        pend = self.pending
        self.pending = []
        nb = [i for i in self.lastop.values()] + [i for i in self.dma_last.values()]
        for i in nb:
            ops[i].signals = True
        for o in pend:
            for d in o.deps:
                od = ops[d]
                if od.eng == PE and o.eng == PE and not od.is_dma and not o.is_dma:
                    continue
                if od.sig is None:
                    od.signals = True
        for o in pend:
            if o.is_dma:
                self.dcnt[o.q] += 16
                o.sig = ("d", o.q, self.dcnt[o.q])
            elif o.signals:
                c = self.cnt[o.eng]
                assert c // EPOCH < NEPOCH
                o.sig = ("c", o.eng, c // EPOCH, (c % EPOCH) + 1)
                self.cnt[o.eng] = c + 1
        per = {e: [o for o in pend if o.eng == e] for e in ENGS}
        finals = list(self.final_waits) if final else []

        def run(engobj, ename):
            seen = self.seen[ename]
            for o in per[ename]:
                waits = {}
                for d in o.deps:
                    od = ops[d]
                    if od.eng == PE and o.eng == PE and not od.is_dma and not o.is_dma:
                        continue
                    if od.sig is None:
                        raise AssertionError((od.eng, o.eng, d, o.idx))
                    key, sh, val = self.sem_of(od.sig)
                    if seen.get(key, 0) >= val:
                        continue
                    if key not in waits or waits[key][1] < val:
                        waits[key] = (sh, val)
                for key, (sh, val) in waits.items():
                    engobj.wait_ge(sh, val)
                    seen[key] = val
                ins = o.fn(engobj)
                if o.sig is not None:
                    key, sh, val = self.sem_of(o.sig)
                    ins.then_inc(sh, 16 if o.is_dma else 1)
            if ename == SP:
                for i in finals:
                    key, sh, val = self.sem_of(ops[i].sig)
                    engobj.wait_ge(sh, val)

        with nc.Block() as block:
            @block.tensor
            def _(e):
                run(e, PE)

            @block.scalar
            def _(e):
                run(e, ACT)

            @block.vector
            def _(e):
                run(e, DVE)

            @block.gpsimd
            def _(e):
                run(e, POOL)

            @block.sync
            def _(e):
                run(e, SP)
        self.barrier = nb
        self.got_barrier = set()
        self.lastw = {}
        self.readers = {}


def v3(ap, b):
    return ap.rearrange("p (a b) -> p a b", b=b)


def build_program():
    nc = bass.Bass("TRN2", target_bir_lowering=False)
    xin = nc.dram_tensor("xin", [T, D], F32, kind="ExternalInput").ap()
    crow = nc.dram_tensor("crow", [16, 128], F32, kind="ExternalInput").ap()
    rows1 = nc.dram_tensor("rows1", [128, 128], F32, kind="ExternalInput").ap()
    rows2 = nc.dram_tensor("rows2", [128, 128], F32, kind="ExternalInput").ap()
    rows3 = nc.dram_tensor("rows3", [128, 128], F32, kind="ExternalInput").ap()
    brow = nc.dram_tensor("brow", [1, 256], F32, kind="ExternalInput").ap()
    w_ada = nc.dram_tensor("w_ada", [D, 9 * D], F32, kind="ExternalInput").ap()
    w_gu = nc.dram_tensor("w_gu", [2, D, 2 * DFF], F32, kind="ExternalInput").ap()
    w_dn = nc.dram_tensor("w_dn", [2, DFF, D], F32, kind="ExternalInput").ap()
    w_in = nc.dram_tensor("w_in", [D, DIN], F32, kind="ExternalInput").ap()
    w_bg = nc.dram_tensor("w_bg", [D, D], F32, kind="ExternalInput").ap()
    w_bm = nc.dram_tensor("w_bm", [2 * D, D], F32, kind="ExternalInput").ap()
    w_o = nc.dram_tensor("w_o", [D, D], F32, kind="ExternalInput").ap()
    out = nc.dram_tensor("out", [2048, D], F32, kind="ExternalOutput").ap()
    xsp = nc.dram_tensor("xsp", [128, 8, 2048], F32, kind="Internal").ap()
    yab = nc.dram_tensor("yab", [24, 128, 2048], BF16, kind="Internal").ap()

    w_ada_v = w_ada.rearrange("(k p) n -> p k n", p=128)
    w_in_v = w_in.rearrange("(k p) n -> p k n", p=128)

    with ExitStack() as g:
        uniq = [0]

        def sb(st, name, shape, dt=F32):
            uniq[0] += 1
            return st.enter_context(nc.sbuf_tensor(f"{name}_{uniq[0]}", shape, dt))

        P = Prog(nc, g)
        ps = [g.enter_context(nc.psum_tensor(f"ps{i}", [128, 512], F32)) for i in range(8)]
        PSK = [("ps", i) for i in range(8)]

        ident_f = sb(g, "ident_f", [128, 128])
        ident_b = sb(g, "ident_b", [128, 128], BF16)
        ones_f = sb(g, "ones_f", [128, 128])
        zeros_f = sb(g, "zeros_f", [64, 256])
        negones = sb(g, "negones", [64, 64])
        m_le = sb(g, "m_le", [64, 64])
        m_ge = sb(g, "m_ge", [64, 64])
        am_f = sb(g, "am_f", [64, 128])
        am_b = sb(g, "am_b", [64, 128])
        neg4_f = sb(g, "neg4_f", [64, 256])
        neg4_b = sb(g, "neg4_b", [64, 256])
        cols1 = sb(g, "cols1", [128, 128])
        cols2 = sb(g, "cols2", [128, 128])
        cols3 = sb(g, "cols3", [128, 128])
        mv = sb(g, "mv", [128, 144])
        hT = sb(g, "hT", [128, 8, T], BF16)
        brt = sb(g, "brt", [64, 256])
        epsc = sb(g, "epsc", [128, 1])

        def mvap(s, w, kind, k):
            i = ((s * 2 + w) * 3 + kind) * 8 + k
            return mv[:, i:i + 1]

        P.add(POOL, "memset", w=["ones"], ap=ones_f[:], constant=1.0)
        P.add(POOL, "memset", w=["zeros"], ap=zeros_f[:], constant=0.0)
        P.add(POOL, "memset", w=["epsc"], ap=epsc[:], constant=EPS)
        P.add(POOL, "memset", w=["negones"], ap=negones[:], constant=-1.0)
        P.add(POOL, "affine_select", r=["ones"], w=["ident_f"], out=ident_f[:], in_=ones_f[:], pattern=[[-1, 128]],
              compare_op=ALU.is_equal, fill=0.0, base=0, channel_multiplier=1)
        P.add(DVE, "tensor_copy", r=["ident_f"], w=["ident_b"], out=ident_b[:], in_=ident_f[:])
        P.add(POOL, "affine_select", r=["ones"], w=["m_le"], out=m_le[:], in_=ones_f[0:64, 0:64], pattern=[[1, 64]],
              compare_op=ALU.is_ge, fill=0.0, base=0, channel_multiplier=-1)
        P.add(POOL, "affine_select", r=["ones"], w=["m_ge"], out=m_ge[:], in_=ones_f[0:64, 0:64], pattern=[[-1, 64]],
              compare_op=ALU.is_ge, fill=0.0, base=0, channel_multiplier=1)
        P.add(POOL, "affine_select", r=["zeros"], w=["am_f"], out=am_f[:, 0:64], in_=zeros_f[:, 0:64], pattern=[[1, 64]],
              compare_op=ALU.is_ge, fill=-30000.0, base=0, channel_multiplier=-1)
        P.add(POOL, "affine_select", r=["zeros"], w=["am_f"], out=am_f[:, 64:128], in_=zeros_f[:, 0:64], pattern=[[-1, 64]],
              compare_op=ALU.is_gt, fill=30000.0, base=0, channel_multiplier=1)
        P.add(POOL, "affine_select", r=["zeros"], w=["am_b"], out=am_b[:, 0:64], in_=zeros_f[:, 0:64], pattern=[[-1, 64]],
              compare_op=ALU.is_ge, fill=-30000.0, base=0, channel_multiplier=1)
        P.add(POOL, "affine_select", r=["zeros"], w=["am_b"], out=am_b[:, 64:128], in_=zeros_f[:, 0:64], pattern=[[1, 64]],
              compare_op=ALU.is_gt, fill=30000.0, base=0, channel_multiplier=-1)
        P.add(POOL, "affine_select", r=["zeros"], w=["neg4"], out=v3(neg4_f[:], 64), in_=v3(zeros_f[:], 64),
              pattern=[[0, 4], [1, 64]], compare_op=ALU.is_ge, fill=-30000.0, base=0, channel_multiplier=-1)
        P.add(POOL, "affine_select", r=["zeros"], w=["neg4"], out=v3(neg4_b[:], 64), in_=v3(zeros_f[:], 64),
              pattern=[[0, 4], [-1, 64]], compare_op=ALU.is_ge, fill=-30000.0, base=0, channel_multiplier=1)
        rst = sb(g, "rst", [128, 128])
        for i, (rw, cl) in enumerate(((rows1, cols1), (rows2, cols2), (rows3, cols3))):
            P.dma(rst[:], rw, w=["rst"])
            P.add(PE, "transpose", r=["rst", "ident_f"], w=[PSK[0]], out=ps[0][:, 0:128], in_=rst[:], identity=ident_f[:])
            P.add(DVE, "tensor_copy", r=[PSK[0]], w=[("cols", i)], out=cl[:], in_=ps[0][:, 0:128])
        P.dma(brt[:], brow.partition_broadcast(64), w=["brt"])
        P.flush()

        cur = {}

        def rmsnorm_mod(s, blks):
            xT = cur["xT"]
            with ExitStack() as st:
                sq = [sb(st, f"nsq{i}", [128, 512]) for i in range(2)]
                rs = sb(st, "nrs", [128, 512])
                tmp = [sb(st, f"ntmp{i}", [128, 512]) for i in range(2)]
                for bi, (t0, n) in enumerate(blks):
                    w = 1 if t0 == 0 else 0
                    pn = 6 + (bi % 2)
                    for k in range(8):
                        P.add(ACT, "activation", r=[("xT", k, t0)], w=[("nsq", k % 2)], out=sq[k % 2][:, :n], in_=xT[:, k, t0:t0 + n], func=AF.Square)
                        P.add(PE, "matmul", r=[("nsq", k % 2), "ones"], w=[PSK[pn]], out=ps[pn][:, :n], lhsT=ones_f[:], rhs=sq[k % 2][:, :n],
                              start=(k == 0), stop=(k == 7))
                    P.add(ACT, "activation", r=[PSK[pn]], w=["nrs"], out=rs[:, :n], in_=ps[pn][:, :n], func=AF.Sqrt, scale=1.0 / D, bias=epsc[:, 0:1])
                    P.add(DVE, "reciprocal", r=["nrs"], w=["nrs"], out=rs[:, :n], in_=rs[:, :n])
                    for k in range(8):
                        P.add(DVE, "tensor_tensor", r=[("xT", k, t0), "nrs"], w=[("ntmp", k % 2)], out=tmp[k % 2][:, :n], in0=xT[:, k, t0:t0 + n],
                              in1=rs[:, :n], op=ALU.mult)
                        P.add(ACT, "activation", r=[("ntmp", k % 2), "mv"], w=[("hT", k, t0)], out=hT[:, k, t0:t0 + n], in_=tmp[k % 2][:, :n],
                              func=AF.Identity, scale=mvap(s, w, 0, k), bias=mvap(s, w, 1, k))
                P.flush()

        def ffn(s, li, blks):
            xT = cur["xT"]
            wgu_v = w_gu[li].rearrange("(k p) n -> p k n", p=128)
            groups = [list(range(0, 6)), list(range(6, 12)), list(range(12, 17)), list(range(17, 22))]
            with ExitStack() as st:
                act = sb(st, "fact", [128, 6, T], BF16)
                wgs = [sb(st, f"fwgs{i}", [128, 8, 256]) for i in range(2)]
                wgb = [sb(st, f"fwgb{i}", [128, 8, 256], BF16) for i in range(2)]
                wds = [sb(st, f"fwds{i}", [128, D]) for i in range(2)]
                wdb = sb(st, "fwdb", [128, 6, D], BF16)
                sl = [sb(st, f"fsl{i}", [128, 512]) for i in range(2)]
                it = 0
                for grp in groups:
                    for jj, j in enumerate(grp):
                        b = j % 2
                        P.dma(wgs[b][:, :, 0:128], wgu_v[:, :, j * 128:(j + 1) * 128], w=[("fwgs", b)])
                        P.dma(wgs[b][:, :, 128:256], wgu_v[:, :, DFF + j * 128:DFF + (j + 1) * 128], w=[("fwgs", b)])
                        P.add(POOL, "tensor_copy", r=[("fwgs", b)], w=[("fwgb", b)], out=wgb[b][:], in_=wgs[b][:])
                        for (t0, n) in blks:
                            q = it % 2
                            it += 1
                            pg, pu = q * 2, q * 2 + 1
                            for k in range(8):
                                P.add(PE, "matmul", r=[("fwgb", b), ("hT", k, t0)], w=[PSK[pg]], out=ps[pg][:, :n], lhsT=wgb[b][:, k, 0:128],
                                      rhs=hT[:, k, t0:t0 + n], start=(k == 0), stop=(k == 7))
                            for k in range(8):
                                P.add(PE, "matmul", r=[("fwgb", b), ("hT", k, t0)], w=[PSK[pu]], out=ps[pu][:, :n], lhsT=wgb[b][:, k, 128:256],
                                      rhs=hT[:, k, t0:t0 + n], start=(k == 0), stop=(k == 7))
                            P.add(ACT, "activation", r=[PSK[pg]], w=[("fsl", q)], out=sl[q][:, :n], in_=ps[pg][:, :n], func=AF.Silu)
                            P.add(DVE, "tensor_tensor", r=[("fsl", q), PSK[pu]], w=[("fact", jj, t0)], out=act[:, jj, t0:t0 + n], in0=sl[q][:, :n],
                                  in1=ps[pu][:, :n], op=ALU.mult)
                    for jj, j in enumerate(grp):
                        b = j % 2
                        P.dma(wds[b][:], w_dn[li, j * 128:(j + 1) * 128, :], w=[("fwds", b)])
                        P.add(POOL, "tensor_copy", r=[("fwds", b)], w=[("fwdb", jj)], out=wdb[:, jj, :], in_=wds[b][:])
                    ng = len(grp)
                    for d in range(8):
                        for (t0, n) in blks:
                            w = 1 if t0 == 0 else 0
                            q = 4 + (it % 2)
                            it += 1
                            for jj in range(ng):
                                P.add(PE, "matmul", r=[("fwdb", jj), ("fact", jj, t0)], w=[PSK[q]], out=ps[q][:, :n],
                                      lhsT=wdb[:, jj, d * 128:(d + 1) * 128], rhs=act[:, jj, t0:t0 + n], start=(jj == 0), stop=(jj == ng - 1))
                            P.add(DVE, "scalar_tensor_tensor", r=[PSK[q], ("xT", d, t0), "mv"], w=[("xT", d, t0)], out=xT[:, d, t0:t0 + n],
                                  in0=ps[q][:, :n], scalar=mvap(s, w, 2, d), in1=xT[:, d, t0:t0 + n], op0=ALU.mult, op1=ALU.add)
                P.flush()

        def write_out(final_norm):
            xT = cur["xT"]
            with ExitStack() as st:
                sq = [sb(st, f"osq{i}", [128, 512]) for i in range(2)]
                rs = sb(st, "ors", [128, 512])
                yt = [sb(st, f"oyt{i}", [128, 8, 512]) for i in range(2)]
                ost = [sb(st, f"oost{i}", [128, D]) for i in range(2)]
                fin = []
                for bi, (t0, n) in enumerate(LBLKS):
                    yb_ = yt[bi % 2]
                    if final_norm:
                        pn = 6 + (bi % 2)
                        for k in range(8):
                            P.add(ACT, "activation", r=[("xT", k, t0)], w=[("osq", k % 2)], out=sq[k % 2][:, :n], in_=xT[:, k, t0:t0 + n], func=AF.Square)
                            P.add(PE, "matmul", r=[("osq", k % 2), "ones"], w=[PSK[pn]], out=ps[pn][:, :n], lhsT=ones_f[:], rhs=sq[k % 2][:, :n],
                                  start=(k == 0), stop=(k == 7))
                        P.add(ACT, "activation", r=[PSK[pn]], w=["ors"], out=rs[:, :n], in_=ps[pn][:, :n], func=AF.Sqrt, scale=1.0 / D, bias=epsc[:, 0:1])
                        P.add(DVE, "reciprocal", r=["ors"], w=["ors"], out=rs[:, :n], in_=rs[:, :n])
                        for k in range(8):
                            P.add(DVE, "scalar_tensor_tensor", r=[("xT", k, t0), "ors", ("cols", 0)], w=[("oyt", bi % 2, k)], out=yb_[:, k, :n],
                                  in0=xT[:, k, t0:t0 + n], scalar=cols1[:, 96 + k:97 + k], in1=rs[:, :n], op0=ALU.mult, op1=ALU.mult)
                    else:
                        for k in range(8):
                            P.add(DVE if k % 2 else POOL, "tensor_copy", r=[("xT", k, t0)], w=[("oyt", bi % 2, k)], out=yb_[:, k, :n], in_=xT[:, k, t0:t0 + n])
                    for tt in range(4):
                        ti = bi * 4 + tt
                        ob = ost[ti % 2]
                        for half in range(2):
                            pi = 2 * (ti % 2) + half
                            for j in range(4):
                                k = half * 4 + j
                                P.add(PE, "transpose", r=[("oyt", bi % 2, k), "ident_f"], w=[PSK[pi]], out=ps[pi][:, j * 128:(j + 1) * 128],
                                      in_=yb_[:, k, tt * 128:(tt + 1) * 128], identity=ident_f[:])
                            if half:
                                P.add(ACT, "copy", r=[PSK[pi]], w=[("oost", ti % 2)], out=ob[:, half * 512:(half + 1) * 512], in_=ps[pi][:, 0:512])
                            else:
                                P.add(DVE, "tensor_copy", r=[PSK[pi]], w=[("oost", ti % 2)], out=ob[:, half * 512:(half + 1) * 512], in_=ps[pi][:, 0:512])
                        o = P.dma(out[ti * 128:(ti + 1) * 128, :], ob[:], r=[("oost", ti % 2)])
                        fin.append(o.idx)
                P.final_waits = fin
                P.flush(final=True)

        def gdn(dbg=False):
            with ExitStack() as mx:
                gr = sb(mx, "gr", [64, NCH, 8, 2])
                bet = sb(mx, "bet", [64, NCH, 8, 2])
                nbet = sb(mx, "nbet", [64, NCH, 8, 2])
                gcs = sb(mx, "gcs", [64, NCH, 8, 2])
                ngcs = sb(mx, "ngcs", [64, NCH, 8, 2])
                egc = sb(mx, "egc", [64, NCH, 8, 2])
                ks = sb(mx, "ks", [64, NCH, 8, 2])
                kbs = sb(mx, "kbs", [64, NCH, 8, 2])
                etot = sb(mx, "etot", [128, NCH, 8, 2])
                br2 = sb(mx, "br2", [64, 2, 8, 2])
                gng = sb(mx, "gng", [64, 128])
                m2 = sb(mx, "m2", [64, 2, 128])
                am2 = sb(mx, "am2", [64, 2, 128])
                pro = ExitStack()
                wsm_s = sb(pro, "wsm_s", [128, 8, 32])
                wsm_b = sb(pro, "wsm_b", [128, 8, 32], BF16)
                ab = sb(pro, "ab", [64, NCH, 2, 8, 2])
                P.dma(wsm_s[:], w_in_v[:, :, 4096:4128], w=["wsm_s"])
                P.dma(gng[:], rows1[104:105, :].partition_broadcast(64), w=["gng"])
                P.add(POOL, "tensor_copy", r=["wsm_s"], w=["wsm_b"], out=wsm_b[:], in_=wsm_s[:])
                for hh in range(2):
                    P.add(POOL, "tensor_copy", r=["m_le"], w=["m2"], out=m2[:, 0, hh * 64:(hh + 1) * 64], in_=m_le[:])
                    P.add(POOL, "tensor_copy", r=["m_ge"], w=["m2"], out=m2[:, 1, hh * 64:(hh + 1) * 64], in_=m_ge[:])
                P.add(POOL, "tensor_copy", r=["am_f"], w=["am2"], out=am2[:, 0, :], in_=am_f[:])
                P.add(POOL, "tensor_copy", r=["am_b"], w=["am2"], out=am2[:, 1, :], in_=am_b[:])
                P.add(DVE, "tensor_copy", r=["brt"], w=["br2"], out=br2[:, 0, :, :].rearrange("p h d -> p d h"),
                      in_=brt[:, 0:16].rearrange("p (d h) -> p d h", d=2))
                P.add(ACT, "activation", r=["brt"], w=["br2"], out=br2[:, 1, :, :].rearrange("p h d -> p d h"),
                      in_=brt[:, 96:112].rearrange("p (d h) -> p d h", d=2), func=AF.Exp)
                P.add(DVE, "tensor_scalar", r=["br2"], w=["br2"], out=br2[:, 1, :, :], in0=br2[:, 1, :, :], scalar1=-1.0, scalar2=None, op0=ALU.mult)
                PSTOP = int(os.environ.get("KPSTOP", "9"))
                if PSTOP == 0:
                    P.flush(); pro.close(); return
                for c in range(NCH):
                    pi = c % 2
                    for k in range(8):
                        P.add(PE, "matmul", r=["wsm_b"], w=[PSK[pi]], out=ps[pi][0:64, 0:32], lhsT=hT[:, k, c * 64:(c + 1) * 64], rhs=wsm_b[:, k, :],
                              start=(k == 0), stop=(k == 7))
                    P.add(DVE if pi else ACT, "tensor_copy" if pi else "copy", r=[PSK[pi]], w=["ab"],
                          out=ab[:, c, :, :, :].rearrange("p a h d -> p a d h"), in_=ps[pi][0:64, 0:32].rearrange("p (a d h) -> p a d h", a=2, d=2))
                bshape = [64, NCH, 8, 2]
                if PSTOP == 1:
                    P.flush(); pro.close(); return
                P.add(DVE, "tensor_tensor", r=["ab", "br2"], w=["gr"], out=gr[:], in0=ab[:, :, 0, :, :], in1=br2[:, 0, :, :].unsqueeze(1).broadcast_to(bshape), op=ALU.add)
                P.add(ACT, "activation", r=["gr"], w=["gr"], out=gr[:], in_=gr[:], func=AF.Exp)
                P.add(ACT, "activation", r=["gr"], w=["gr"], out=gr[:], in_=gr[:], func=AF.Ln, bias=1.0)
                P.add(DVE, "tensor_tensor", r=["gr", "br2"], w=["gr"], out=gr[:], in0=gr[:], in1=br2[:, 1, :, :].unsqueeze(1).broadcast_to(bshape), op=ALU.mult)
                P.add(ACT, "activation", r=["ab"], w=["bet"], out=bet[:], in_=ab[:, :, 1, :, :], func=AF.Sigmoid)
                P.add(DVE, "tensor_scalar", r=["bet"], w=["nbet"], out=nbet[:], in0=bet[:], scalar1=-1.0, scalar2=None, op0=ALU.mult)
                if PSTOP == 2:
                    P.flush(); pro.close(); return
                for half in range(2):
                    cs = slice(half * 18, half * 18 + 18)
                    rhs = gr[:, cs, :, :].rearrange("p c h d -> p (c h d)")
                    P.add(PE, "matmul", r=["gr", "m_le"], w=[PSK[2]], out=ps[2][0:64, 0:288], lhsT=m_le[:], rhs=rhs, start=True, stop=True)
                    P.add(PE, "matmul", r=["gr", "m_ge"], w=[PSK[3]], out=ps[3][0:64, 0:288], lhsT=m_ge[:], rhs=rhs, start=True, stop=True)
                    P.add(PE, "matmul", r=["gr", "ones"], w=[PSK[4]], out=ps[4][:, 0:288], lhsT=ones_f[0:64, :], rhs=rhs, start=True, stop=True)
                    P.add(DVE, "tensor_copy", r=[PSK[2]], w=["gcs"], out=gcs[:, cs, :, 0], in_=ps[2][0:64, 0:288].rearrange("p (c h d) -> p c h d", h=8, d=2)[:, :, :, 0])
                    P.add(DVE, "tensor_copy", r=[PSK[3]], w=["gcs"], out=gcs[:, cs, :, 1], in_=ps[3][0:64, 0:288].rearrange("p (c h d) -> p c h d", h=8, d=2)[:, :, :, 1])
                    P.add(ACT, "activation", r=[PSK[4]], w=["etot"], out=etot[:, cs, :, :].rearrange("p c h d -> p (c h d)"), in_=ps[4][:, 0:288], func=AF.Exp)
                    P.add(DVE, "tensor_tensor", r=[PSK[4], "gcs"], w=["ks"], out=ks[:, cs, :, :].rearrange("p c h d -> p (c h d)"), in0=ps[4][0:64, 0:288],
                          in1=gcs[:, cs, :, :].rearrange("p c h d -> p (c h d)"), op=ALU.subtract)
                P.add(ACT, "activation", r=["ks"], w=["ks"], out=ks[:], in_=ks[:], func=AF.Exp)
                P.add(ACT, "activation", r=["gcs"], w=["egc"], out=egc[:], in_=gcs[:], func=AF.Exp)
                P.add(DVE, "tensor_scalar", r=["gcs"], w=["ngcs"], out=ngcs[:], in0=gcs[:], scalar1=-1.0, scalar2=None, op0=ALU.mult)
                P.add(DVE, "tensor_tensor", r=["egc", "bet"], w=["kbs"], out=kbs[:], in0=egc[:], in1=bet[:], op=ALU.mult)
                P.flush()
                pro.close()
                GSTOP = int(os.environ.get("KGSTOP", "9"))
                if GSTOP == 0:
                    return

                wst = [sb(mx, "gwst0", [128, 8, 128])] * 2
                wb = [sb(mx, f"gwb{i}", [128, 8, 128], BF16) for i in range(2)]
                feat = sb(mx, "gfeat", [128, 2436 + T])
                cin = feat[:, 0:2436]
                co = feat[:, 2436:2436 + T]
                qT = sb(mx, "gqT", [128, T])
                kT = sb(mx, "gkT", [128, T])
                qTb = sb(mx, "gqTb", [128, T], BF16)
                sqt = [sb(mx, "gsq0", [128, 512])] * 2
                rsn = sb(mx, "grsn", [128, 512])
                k_tok = sb(mx, "gktok", [64, NCH, 128], BF16)
                v_tok = sb(mx, "gvtok", [64, NCH, 128], BF16)
                z_tok = sb(mx, "gztok", [64, 32, 128], BF16)
                o_tok = sb(mx, "gotok", [64, 32, 128])
                u_tok = feat[0:64, 0:4608].bitcast(BF16).rearrange("p (c d f) -> p c d f", c=NCH, d=2)
                wT = sb(mx, "gwT", [128, 2, T], BF16)
                qkd = sb(mx, "gqkd", [64, NCH, 2, 64], BF16)
                ko = [sb(mx, f"gko{i}", [64, 128], BF16) for i in range(4)]
                kq = [sb(mx, f"gkq{i}", [64, 128])[:] for i in range(2)]
                R1 = [sb(mx, f"gR1{i}", [64, 2, 128])[:] for i in range(2)]
                E = [sb(mx, f"gE{i}", [64, 2, 128])[:] for i in range(2)]
                tmpA = [sb(mx, f"gtA{i}", [64, 2, 64])[:] for i in range(2)]
                ABs = [sb(mx, f"gAB{i}", [64, 2, 128])[:] for i in range(4)]
                Xs = [sb(mx, f"gX{i}", [64, 2, 64])[:] for i in range(4)]
                TTb = [sb(mx, f"gTT{i}", [64, 2, 64], BF16)[:] for i in range(2)]
                vb = [sb(mx, f"gvb{i}", [64, 2, 128], BF16)[:] for i in range(2)]
                kbg = [sb(mx, f"gkbg{i}", [64, 2, 128], BF16)[:] for i in range(2)]
                oflat = o_tok[:].rearrange("p c f -> p (c f)")
                for sl in range(2):
                    o0 = sl * 1856
                    def reg(off, n, shp=None, bf=False):
                        v = oflat[:, o0 + off:o0 + off + n]
                        if bf:
                            v = v.bitcast(BF16)
                        if shp:
                            v = v.rearrange("p (d f) -> p d f", d=2)
                        return v
                    kq.append(reg(0, 128))
                    R1.append(reg(128, 256, True))
                    E.append(reg(384, 256, True))
                    tmpA.append(reg(640, 128, True))
                    ABs.append(reg(768, 256, True)); ABs.append(reg(1024, 256, True))
                    Xs.append(reg(1280, 128, True)); Xs.append(reg(1408, 128, True))
                    TTb.append(reg(1536, 64, True, True))
                    vb.append(reg(1600, 128, True, True))
                    kbg.append(reg(1728, 128, True, True))
                S = [sb(mx, f"gS{i}", [128, 128]) for i in range(2)]
                Sb = [sb(mx, f"gSb{i}", [128, 128], BF16) for i in range(2)]
                vnew = [sb(mx, f"gvn{i}", [64, 128], BF16) for i in range(4)]
                otmp = [sb(mx, f"got{i}", [64, 128]) for i in range(4)]
                ssq = sb(mx, "gssq", [64, 32])
                yst = [sb(mx, f"gyst{i}", [128, 512], BF16) for i in range(2)]
                psb = ps[7][:].bitcast(BF16)
                cinL = cin[:, 260:2436].rearrange("p (r c) -> p r c", c=68)
                id64 = ident_f[0:64, 0:64]
                offs4 = [0, 1024, 2048, 3072]

                for h in range(8):
                    P.add(POOL, "memset", w=["cin"], ap=cin, constant=0.0)

                    def project(c4, evac):
                        wbi = c4 % 2
                        P.dma(wst[wbi][:], w_in_v[:, :, offs4[c4] + h * 128:offs4[c4] + (h + 1) * 128], w=["gwst"])
                        P.add(POOL, "tensor_copy", r=["gwst"], w=[("gwb", wbi)], out=wb[wbi][:], in_=wst[wbi][:])
                        for bi, (t0, n) in enumerate(BLKS):
                            pi = bi % 2
                            for k in range(8):
                                P.add(PE, "matmul", r=[("gwb", wbi)], w=[PSK[pi]], out=ps[pi][:, :n], lhsT=wb[wbi][:, k, :], rhs=hT[:, k, t0:t0 + n],
                                      start=(k == 0), stop=(k == 7))
                            evac(bi, t0, n, pi)

                    def evac_cin(bi, t0, n, pi):
                        if t0 == 0:
                            P.add(ACT, "copy", r=[PSK[pi]], w=["cin"], out=cin[:, 2:258], in_=ps[pi][:, 0:256])
                        else:
                            r0 = (t0 - 256) // 64
                            P.add(ACT, "copy", r=[PSK[pi]], w=["cin"], out=cinL[:, r0:r0 + 8, 2:66], in_=v3(ps[pi][:, 0:512], 64))

                    def conv_silu(c4, dst):
                        for kk in range(5):
                            wc = cols2[:, kk * 24 + c4 * 8 + h:kk * 24 + c4 * 8 + h + 1]
                            if kk == 0:
                                P.add(DVE, "tensor_scalar", r=["cin", ("cols", 1)], w=["co_c"], out=co[:, 0:256], in0=cin[:, kk:kk + 256], scalar1=wc, scalar2=None, op0=ALU.mult)
                                P.add(DVE, "tensor_scalar", r=["cin", ("cols", 1)], w=["co_l"], out=v3(co[:, 256:T], 64), in0=cinL[:, :, kk:kk + 64], scalar1=wc, scalar2=None, op0=ALU.mult)
                            else:
                                P.add(DVE, "scalar_tensor_tensor", r=["cin", ("cols", 1), "co_c"], w=["co_c"], out=co[:, 0:256], in0=cin[:, kk:kk + 256], scalar=wc,
                                      in1=co[:, 0:256], op0=ALU.mult, op1=ALU.add)
                                P.add(DVE, "scalar_tensor_tensor", r=["cin", ("cols", 1), "co_l"], w=["co_l"], out=v3(co[:, 256:T], 64), in0=cinL[:, :, kk:kk + 64], scalar=wc,
                                      in1=v3(co[:, 256:T], 64), op0=ALU.mult, op1=ALU.add)
                        P.add(ACT, "activation", r=["co_c", "co_l"], w=[dst[1]], out=dst[0][:, :], in_=co, func=AF.Silu)

                    def l2norm(src, key, scale, extra_bf16=None):
                        for bi, (t0, n) in enumerate(BLKS):
                            pn = 2 + (bi % 2)
                            P.add(ACT, "activation", r=[key], w=["gsq"], out=sqt[bi % 2][:, :n], in_=src[:, t0:t0 + n], func=AF.Square)
                            P.add(PE, "matmul", r=["gsq", "ones"], w=[PSK[pn]], out=ps[pn][:, :n], lhsT=ones_f[:], rhs=sqt[bi % 2][:, :n], start=True, stop=True)
                            P.add(ACT, "activation", r=[PSK[pn]], w=["grsn"], out=rsn[:, :n], in_=ps[pn][:, :n], func=AF.Sqrt, bias=epsc[:, 0:1])
                            P.add(DVE, "reciprocal", r=["grsn"], w=["grsn"], out=rsn[:, :n], in_=rsn[:, :n])
                            P.add(DVE, "scalar_tensor_tensor", r=[key, "grsn"], w=[key], out=src[:, t0:t0 + n], in0=src[:, t0:t0 + n], scalar=scale, in1=rsn[:, :n],
                                  op0=ALU.mult, op1=ALU.mult)
                        if extra_bf16 is not None:
                            P.add(POOL, "tensor_copy", r=[key], w=["gqTb"], out=extra_bf16[:, :], in_=src[:, :])

                    def to_tok(src, key, dst, dkey, c0, silu=False):
                        nch = NCH - c0
                        for g4 in range(nch // 4):
                            pi = g4 % 2
                            for j in range(4):
                                c = c0 + g4 * 4 + j
                                P.add(PE, "transpose", r=[key, "ident_f"], w=[PSK[pi]], out=ps[pi][0:64, j * 128:(j + 1) * 128], in_=src[:, c * 64:(c + 1) * 64], identity=ident_f[:])
                            if silu:
                                P.add(ACT, "activation", r=[PSK[pi]], w=[dkey], out=dst[:, g4 * 4:g4 * 4 + 4, :], in_=v3(ps[pi][0:64, 0:512], 128), func=AF.Silu)
                            else:
                                P.add(DVE if pi else ACT, "tensor_copy" if pi else "copy", r=[PSK[pi]], w=[dkey], out=dst[:, g4 * 4:g4 * 4 + 4, :], in_=v3(ps[pi][0:64, 0:512], 128))

                    project(0, evac_cin); conv_silu(0, (qT, "gqT")); l2norm(qT, "gqT", 128 ** -0.5, qTb)
                    project(1, evac_cin); conv_silu(1, (kT, "gkT")); l2norm(kT, "gkT", 1.0)
                    to_tok(kT, "gkT", k_tok, "gktok", 0)
                    project(2, evac_cin); conv_silu(2, (co, "gco"))
                    to_tok(co, "gco", v_tok, "gvtok", 0)
                    project(3, lambda bi, t0, n, pi: P.add(ACT, "copy", r=[PSK[pi]], w=["gco"], out=co[:, t0:t0 + n], in_=ps[pi][:, :n]))
                    to_tok(co, "gco", z_tok, "gztok", 4, silu=True)
                    P.flush()
                    if GSTOP == 1:
                        return

                    def solve_chunk(c, a):
                        pk = 2 * a
                        pw = 2 * a + 1
                        P.add(PE, "matmul", r=["gkT"], w=[PSK[pk]], out=ps[pk][0:64, 0:64], lhsT=kT[:, c * 64:(c + 1) * 64], rhs=kT[:, c * 64:(c + 1) * 64], start=True, stop=True)
                        P.add(PE, "matmul", r=["gkT", "gqT"], w=[PSK[pk]], out=ps[pk][0:64, 64:128], lhsT=kT[:, c * 64:(c + 1) * 64], rhs=qT[:, c * 64:(c + 1) * 64], start=True, stop=True)
                        P.add(ACT, "copy", r=[PSK[pk]], w=[("gkq", a)], out=kq[a][:], in_=ps[pk][0:64, 0:128])
                        yield
                        for d in range(2):
                            P.add(DVE, "tensor_scalar", r=["m2", "gr"], w=[("gR1", a)], out=R1[a][:, d, :], in0=m2[:, d, :], scalar1=gr[:, c, h, d:d + 1], scalar2=None, op0=ALU.mult)
                        P.add(PE, "matmul", r=[("gR1", a), "ones"], w=[PSK[pk]], out=ps[pk][0:64, 128:384], lhsT=ones_f[0:64, 0:64], rhs=R1[a][:].rearrange("p d f -> p (d f)"), start=True, stop=False)
                        P.add(PE, "matmul", r=["am2", "ident_f"], w=[PSK[pk]], out=ps[pk][0:64, 128:384], lhsT=id64, rhs=am2[:].rearrange("p d f -> p (d f)"), start=False, stop=True)
                        yield
                        for d in range(2):
                            P.add(ACT, "activation", r=[PSK[pk], "ngcs"], w=[("gE", a)], out=E[a][:, d, 0:64], in_=ps[pk][0:64, 128 + d * 128:128 + d * 128 + 64], func=AF.Exp,
                                  bias=ngcs[:, c, h, d:d + 1], scale=1.0)
                            P.add(ACT, "activation", r=[PSK[pk], "gcs"], w=[("gE", a)], out=E[a][:, d, 64:128], in_=ps[pk][0:64, 128 + d * 128 + 64:128 + (d + 1) * 128], func=AF.Exp,
                                  bias=gcs[:, c, h, d:d + 1], scale=-1.0)
                        yield
                        P.add(DVE, "tensor_tensor", r=[("gkq", a), ("gE", a)], w=["gqkd"], out=qkd[:, c, :, :], in0=kq[a][:, 64:128].unsqueeze(1).broadcast_to([64, 2, 64]),
                              in1=E[a][:, :, 0:64], op=ALU.mult)
                        P.add(DVE, "tensor_tensor", r=[("gkq", a), ("gE", a)], w=[("gtA", a)], out=tmpA[a][:], in0=kq[a][:, 0:64].unsqueeze(1).broadcast_to([64, 2, 64]),
                              in1=E[a][:, :, 64:128], op=ALU.mult)
                        ab0 = ABs[2 * a]
                        ab1 = ABs[2 * a + 1]
                        P.add(DVE, "tensor_tensor", r=[("gtA", a), "nbet"], w=[("gAB", 2 * a, "A")], out=ab0[:, :, 0:64], in0=tmpA[a][:],
                              in1=nbet[:, c, h, :].unsqueeze(2).broadcast_to([64, 2, 64]), op=ALU.mult)
                        yield
                        for d in range(2):
                            P.add(PE, "transpose", r=[("gAB", 2 * a, "A"), "ident_f"], w=[PSK[pw]], out=ps[pw][0:64, d * 64:(d + 1) * 64], in_=ab0[:, d, 0:64], identity=id64)
                        yield
                        P.add(ACT, "copy", r=[PSK[pw]], w=[("gAB", 2 * a, "B")], out=ab0[:, :, 64:128], in_=v3(ps[pw][0:64, 0:128], 64))
                        x0 = Xs[2 * a]
                        x1 = Xs[2 * a + 1]
                        P.add(DVE, "tensor_tensor", r=[PSK[pw], "ident_f"], w=[("gX", 2 * a)], out=x0[:], in0=v3(ps[pw][0:64, 0:128], 64),
                              in1=id64.unsqueeze(1).broadcast_to([64, 2, 64]), op=ALU.add)
                        cur_ab, nxt_ab, ci, ni = ab0, ab1, 2 * a, 2 * a + 1
                        cur_x, nxt_x, cxi, nxi = x0, x1, 2 * a, 2 * a + 1
                        for lvl in range(5):
                            last = lvl == 4
                            for d in range(2):
                                P.add(PE, "matmul", r=[("gAB", ci, "A"), ("gAB", ci, "B")], w=[PSK[pw]], out=ps[pw][0:64, 128 + d * 128:128 + d * 128 + 64],
                                      lhsT=cur_ab[:, d, 64:128], rhs=cur_ab[:, d, 0:64], start=True, stop=True)
                                if not last:
                                    P.add(PE, "matmul", r=[("gAB", ci, "A"), ("gAB", ci, "B")], w=[PSK[pw]], out=ps[pw][0:64, 128 + d * 128 + 64:128 + (d + 1) * 128],
                                          lhsT=cur_ab[:, d, 0:64], rhs=cur_ab[:, d, 64:128], start=True, stop=True)
                            yield
                            if last:
                                P.add(ACT, "copy", r=[PSK[pw]], w=[("gAB", ni, "A")], out=nxt_ab[:, :, 0:64], in_=v3(ps[pw][0:64, 128:384], 128)[:, :, 0:64])
                            else:
                                P.add(ACT, "copy", r=[PSK[pw]], w=[("gAB", ni, "A"), ("gAB", ni, "B")], out=nxt_ab[:], in_=v3(ps[pw][0:64, 128:384], 128))
                            yield
                            for d in range(2):
                                P.add(PE, "matmul", r=[("gAB", ni, "A"), ("gX", cxi)], w=[PSK[pw]], out=ps[pw][0:64, 384 + d * 64:384 + (d + 1) * 64],
                                      lhsT=nxt_ab[:, d, 0:64], rhs=cur_x[:, d, :], start=True, stop=True)
                            yield
                            if last:
                                P.add(DVE, "tensor_tensor", r=[PSK[pw], ("gX", cxi)], w=[("gTT", a)], out=TTb[a][:], in0=v3(ps[pw][0:64, 384:512], 64), in1=cur_x[:], op=ALU.add)
                            else:
                                P.add(DVE, "tensor_tensor", r=[PSK[pw], ("gX", cxi)], w=[("gX", nxi)], out=nxt_x[:], in0=v3(ps[pw][0:64, 384:512], 64), in1=cur_x[:], op=ALU.add)
                            cur_ab, nxt_ab, ci, ni = nxt_ab, cur_ab, ni, ci
                            yield
                            cur_x, nxt_x, cxi, nxi = nxt_x, cur_x, nxi, cxi
                        P.add(POOL, "tensor_tensor", r=["gvtok", "bet"], w=[("gvb", a)], out=vb[a][:], in0=v_tok[:, c, :].unsqueeze(1).broadcast_to([64, 2, 128]),
                              in1=bet[:, c, h, :].unsqueeze(2).broadcast_to([64, 2, 128]), op=ALU.mult)
                        P.add(POOL, "tensor_tensor", r=["gktok", "kbs"], w=[("gkbg", a)], out=kbg[a][:], in0=k_tok[:, c, :].unsqueeze(1).broadcast_to([64, 2, 128]),
                              in1=kbs[:, c, h, :].unsqueeze(2).broadcast_to([64, 2, 128]), op=ALU.mult)
                        pu = pk
                        for d in range(2):
                            P.add(PE, "matmul", r=[("gTT", a), ("gvb", a)], w=[PSK[pu]], out=ps[pu][0:64, d * 128:(d + 1) * 128], lhsT=TTb[a][:, d, :], rhs=vb[a][:, d, :], start=True, stop=True)
                            P.add(PE, "matmul", r=[("gTT", a), ("gkbg", a)], w=[PSK[pu]], out=ps[pu][:, 256 + d * 64:256 + (d + 1) * 64], lhsT=kbg[a][:, d, :], rhs=TTb[a][:, d, :], start=True, stop=True)
                        yield
                        P.add(ACT, "copy", r=[PSK[pu]], w=["gutok"], out=u_tok[:, c, :, :], in_=v3(ps[pu][0:64, 0:256], 128))
                        P.add(DVE, "tensor_copy", r=[PSK[pu]], w=["gwT"], out=wT[:, :, c * 64:(c + 1) * 64], in_=v3(ps[pu][:, 256:384], 64))
                    G = 4
                    for c0 in range(0, NCH, G):
                        gens = [solve_chunk(c, i) for i, c in enumerate(range(c0, min(c0 + G, NCH)))]
                        while gens:
                            for g_ in list(gens):
                                try:
                                    next(g_)
                                except StopIteration:
                                    gens.remove(g_)
                    P.flush()
                    if GSTOP == 2:
                        return

                    orders = [list(range(NCH)), [3, 2, 1, 0] + list(range(NCH - 1, 3, -1))]
                    for d in range(2):
                        P.add(POOL, "memset", w=[("gS", d)], ap=S[d][:], constant=0.0)
                        P.add(POOL, "memset", w=[("gSb", d)], ap=Sb[d][:], constant=0.0)
                    P.add(POOL, "memset", w=[("gotok", c) for c in range(4, NCH)], ap=o_tok[:], constant=0.0)
                    for step in range(NCH):
                        for d in range(2):
                            c = orders[d][step]
                            pb = d
                            vi = d * 2 + (step % 2)
                            P.add(PE, "matmul", r=["gwT", ("gSb", d)], w=[PSK[pb]], out=ps[pb][0:64, 0:128], lhsT=wT[:, d, c * 64:(c + 1) * 64], rhs=Sb[d][:], start=True, stop=True)
                            P.add(DVE, "tensor_tensor", r=["gutok", PSK[pb]], w=[("gvn", vi)], out=vnew[vi][:], in0=u_tok[:, c, d, :], in1=ps[pb][0:64, 0:128], op=ALU.subtract)
                            if c >= 4:
                                P.add(PE, "matmul", r=["gqTb", ("gSb", d)], w=[PSK[2 + pb]], out=ps[2 + pb][0:64, 0:128], lhsT=qTb[:, c * 64:(c + 1) * 64], rhs=Sb[d][:], start=True, stop=True)
                                P.add(PE, "matmul", r=["gqkd", ("gvn", vi)], w=[PSK[4 + pb]], out=ps[4 + pb][0:64, 0:128], lhsT=qkd[:, c, d, :], rhs=vnew[vi][:], start=True, stop=True)
                                P.add(ACT, "copy", r=[PSK[4 + pb]], w=[("got", vi)], out=otmp[vi][:], in_=ps[4 + pb][0:64, 0:128])
                                P.add(DVE, "scalar_tensor_tensor", r=[PSK[2 + pb], ("got", vi), "egc"], w=[("got", vi)], out=otmp[vi][:], in0=ps[2 + pb][0:64, 0:128],
                                      scalar=egc[:, c, h, d:d + 1], in1=otmp[vi][:], op0=ALU.mult, op1=ALU.add)
                                P.add(POOL, "tensor_tensor", r=[("got", vi), ("gotok", c)], w=[("gotok", c)], out=o_tok[:, c - 4, :], in0=o_tok[:, c - 4, :], in1=otmp[vi][:], op=ALU.add)
                            P.add(POOL, "tensor_scalar", r=["gktok", "ks"], w=[("gko", vi)], out=ko[vi][:], in0=k_tok[:, c, :], scalar1=ks[:, c, h, d:d + 1], scalar2=None, op0=ALU.mult)
                            P.add(PE, "matmul", r=[("gko", vi), ("gvn", vi)], w=[PSK[6 + pb]], out=ps[6 + pb][:, 0:128], lhsT=ko[vi][:], rhs=vnew[vi][:], start=True, stop=True)
                            P.add(DVE, "scalar_tensor_tensor", r=[("gS", d), PSK[6 + pb], "etot"], w=[("gS", d)], out=S[d][:], in0=S[d][:], scalar=etot[:, c, h, d:d + 1], in1=ps[6 + pb][:, 0:128],
                                  op0=ALU.mult, op1=ALU.add)
                            P.add(ACT, "copy", r=[("gS", d)], w=[("gSb", d)], out=Sb[d][:], in_=S[d][:])
                    P.flush()
                    if dbg:
                        P.dma(out.rearrange("(c p) n -> p c n", p=64)[:, :, h * 128:(h + 1) * 128], o_tok[:], r=[("gotok", c) for c in range(4, NCH)])
                        P.flush()
                        continue
                    sq16 = u_tok[:].rearrange("p c d f -> p (c d) f")[:, 0:32, :]
                    ya_tok = k_tok[:, 0:32, :]
                    P.add(DVE, "tensor_tensor", r=[("gotok", c) for c in range(4, NCH)], w=["gutok"], out=sq16, in0=o_tok[:], in1=o_tok[:], op=ALU.mult)
                    P.add(DVE, "tensor_reduce", r=["gutok"], w=["gssq"], out=ssq[:], in_=sq16, axis=AX.X, op=ALU.add)
                    P.add(ACT, "activation", r=["gssq"], w=["gssq"], out=ssq[:], in_=ssq[:], func=AF.Sqrt, scale=1.0 / 128, bias=epsc[0:64, 0:1])
                    P.add(DVE, "reciprocal", r=["gssq"], w=["gssq"], out=ssq[:], in_=ssq[:])
                    P.add(DVE, "tensor_tensor", r=["gssq"] + [("gotok", c) for c in range(4, NCH)], w=[("gotok", c) for c in range(4, NCH)], out=o_tok[:], in0=o_tok[:],
                          in1=ssq[:].unsqueeze(2).broadcast_to([64, 32, 128]), op=ALU.mult)
                    P.add(DVE, "tensor_tensor", r=["gng"] + [("gotok", c) for c in range(4, NCH)], w=[("gotok", c) for c in range(4, NCH)], out=o_tok[:], in0=o_tok[:],
                          in1=gng[:].unsqueeze(1).broadcast_to([64, 32, 128]), op=ALU.mult)
                    P.add(DVE, "tensor_tensor", r=["gztok"] + [("gotok", c) for c in range(4, NCH)], w=["gktok"], out=ya_tok, in0=o_tok[:], in1=z_tok[:], op=ALU.mult)
                    for bi in range(4):
                        for j in range(8):
                            P.add(PE, "transpose", r=["gktok", "ident_b"], w=[PSK[7]], out=psb[:, j * 64:(j + 1) * 64], in_=ya_tok[:, bi * 8 + j, :], identity=ident_b[0:64, 0:64])
                        P.add(ACT, "copy", r=[PSK[7]], w=[("gyst", bi % 2)], out=yst[bi % 2][:], in_=psb[:, 0:512])
                        P.dma(yab[h, :, bi * 512:(bi + 1) * 512], yst[bi % 2][:], r=[("gyst", bi % 2)])
                    P.flush()

        def ssd(dbg=False):
            XBC0 = 6176
            with ExitStack() as mx:
                m2d = sb(mx, "sm2d", [64, 2, 64])
                neg8 = sb(mx, "sneg8", [64, 2, 4, 64])
                Dt = sb(mx, "sDt", [64, 32])
                for d_, m_ in ((0, m_le), (1, m_ge)):
                    P.add(POOL, "tensor_copy", w=["sm2d"], out=m2d[:, d_, :], in_=m_[:])
                P.add(POOL, "tensor_copy", w=["sneg8"], out=neg8[:, 0, :, :], in_=v3(neg4_f[:], 64))
                P.add(POOL, "tensor_copy", w=["sneg8"], out=neg8[:, 1, :, :], in_=v3(neg4_b[:], 64))
                P.add(POOL, "tensor_copy", w=["sDt"], out=Dt[:], in_=brt[:, 192:224])
                BT = sb(mx, "sBT", [128, T], BF16)
                CT = sb(mx, "sCT", [128, T], BF16)
                B_tok = sb(mx, "sBtok", [64, NCH, 128], BF16)
                cbT = sb(mx, "scbT", [64, NCH, 64])
                wst = sb(mx, "swst", [128, 8, 128])
                wb = [sb(mx, f"swb{i}", [128, 8, 128], BF16) for i in range(2)]
                feat = sb(mx, "sfeat", [128, 2436 + T])
                cin = feat[:, 0:2436]
                co = feat[:, 2436:2436 + T]
                cinL = cin[:, 260:2436].rearrange("p (r c) -> p r c", c=68)
                xs_tok = sb(mx, "sxstok", [64, NCH, 256], BF16)
                z_tok = sb(mx, "sztok", [64, 32, 256], BF16)
                y_tok = sb(mx, "sytok", [64, 32, 256])
                wsm_s = sb(mx, "swsm_s", [128, 8, 8])
                wsm_b = sb(mx, "swsm_b", [128, 8, 8], BF16)
                dtr = sb(mx, "sdtr", [64, NCH, 4, 2])
                ar = sb(mx, "sar", [64, NCH, 4, 2])
                acs = sb(mx, "sacs", [64, NCH, 4, 2])
                eacs = sb(mx, "seacs", [64, NCH, 4, 2])
                wsc = sb(mx, "swsc", [64, NCH, 4, 2])
                etot = sb(mx, "setot", [128, NCH, 4, 2])
                bq = sb(mx, "sbq", [64, 2, 4, 2])
                R1 = [sb(mx, f"sR1{i}", [64, 2, 4, 64]) for i in range(4)]
                MT = [sb(mx, f"sMT{i}", [64, 2, 4, 64]) for i in range(4)]
                MTb = [sb(mx, f"sMTb{i}", [64, 2, 4, 64], BF16) for i in range(4)]
                xsdt = [sb(mx, f"sxsdt{i}", [64, 2, 4, 64], BF16) for i in range(4)]
                xsw = [sb(mx, f"sxsw{i}", [64, 256], BF16) for i in range(4)]
                ytmp = [sb(mx, f"sytmp{i}", [64, 256]) for i in range(2)] * 2
                ST = [sb(mx, f"sST{i}", [128, 256]) for i in range(2)]
                STb = [sb(mx, f"sSTb{i}", [128, 256], BF16) for i in range(2)]
                yst = [sb(mx, "syst0", [128, 512], BF16)] * 2
                psb = ps[7][:].bitcast(BF16)
                id64 = ident_f[0:64, 0:64]

                def project(col0, evac, wbi):
                    P.dma(wst[:], w_in_v[:, :, col0:col0 + 128], w=["swst"])
                    P.add(POOL, "tensor_copy", r=["swst"], w=[("swb", wbi)], out=wb[wbi][:], in_=wst[:])
                    for bi, (t0, n) in enumerate(BLKS):
                        pi = bi % 2
                        for k in range(8):
                            P.add(PE, "matmul", r=[("swb", wbi)], w=[PSK[pi]], out=ps[pi][:, :n], lhsT=wb[wbi][:, k, :], rhs=hT[:, k, t0:t0 + n], start=(k == 0), stop=(k == 7))
                        evac(bi, t0, n, pi)

                def evac_cin(bi, t0, n, pi):
                    if t0 == 0:
                        P.add(ACT, "copy", r=[PSK[pi]], w=["cin"], out=cin[:, 2:258], in_=ps[pi][:, 0:256])
                    else:
                        r0 = (t0 - 256) // 64
                        P.add(ACT, "copy", r=[PSK[pi]], w=["cin"], out=cinL[:, r0:r0 + 8, 2:66], in_=v3(ps[pi][:, 0:512], 64))

                def evac_co(bi, t0, n, pi):
                    P.add(ACT, "copy", r=[PSK[pi]], w=["sco"], out=co[:, t0:t0 + n], in_=ps[pi][:, :n])

                def conv_silu(ctile, dst, dkey):
                    for kk in range(5):
                        wc = cols3[:, kk * 20 + ctile:kk * 20 + ctile + 1]
                        if kk == 0:
                            P.add(DVE, "tensor_scalar", r=["cin"], w=["co_c", "sco"], out=co[:, 0:256], in0=cin[:, kk:kk + 256], scalar1=wc, scalar2=None, op0=ALU.mult)
                            P.add(DVE, "tensor_scalar", r=["cin"], w=["co_l", "sco"], out=v3(co[:, 256:T], 64), in0=cinL[:, :, kk:kk + 64], scalar1=wc, scalar2=None, op0=ALU.mult)
                        else:
                            P.add(DVE, "scalar_tensor_tensor", r=["cin", "co_c"], w=["co_c"], out=co[:, 0:256], in0=cin[:, kk:kk + 256], scalar=wc, in1=co[:, 0:256], op0=ALU.mult, op1=ALU.add)
                            P.add(DVE, "scalar_tensor_tensor", r=["cin", "co_l"], w=["co_l"], out=v3(co[:, 256:T], 64), in0=cinL[:, :, kk:kk + 64], scalar=wc, in1=v3(co[:, 256:T], 64),
                                  op0=ALU.mult, op1=ALU.add)
                    P.add(ACT, "activation", r=["co_c", "co_l"], w=[dkey], out=dst, in_=co, func=AF.Silu, bias=cols3[:, 100 + ctile:101 + ctile])

                def to_tok(src, key, dst_fn, dkey, c0, silu=False):
                    nch = NCH - c0
                    for g4 in range(nch // 4):
                        pi = g4 % 2
                        for j in range(4):
                            c = c0 + g4 * 4 + j
                            P.add(PE, "transpose", r=[key, "ident_f"], w=[PSK[pi]], out=ps[pi][0:64, j * 128:(j + 1) * 128], in_=src[:, c * 64:(c + 1) * 64], identity=ident_f[:])
                        if silu:
                            P.add(ACT, "activation", r=[PSK[pi]], w=[dkey], out=dst_fn(g4), in_=v3(ps[pi][0:64, 0:512], 128), func=AF.Silu)
                        else:
                            P.add(DVE if pi else ACT, "tensor_copy" if pi else "copy", r=[PSK[pi]], w=[dkey], out=dst_fn(g4), in_=v3(ps[pi][0:64, 0:512], 128))

                SG = int(os.environ.get("KSSDG", "0"))
                for quad in (range(4 * SG, 4 * SG + 4) if dbg else range(8)):
                    grp = quad // 4
                    P.add(POOL, "memset", w=["cin"], ap=cin, constant=0.0)
                    if quad % 4 == 0:
                        project(XBC0 + 2048 + grp * 128, evac_cin, 0)
                        conv_silu(16 + grp, co, "sco")
                        P.add(POOL, "tensor_copy", r=["sco"], w=["sBT"], out=BT[:], in_=co)
                        to_tok(co, "sco", lambda g4: B_tok[:, g4 * 4:g4 * 4 + 4, :], "sBtok", 0)
                        project(XBC0 + 2304 + grp * 128, evac_cin, 1)
                        conv_silu(18 + grp, CT[:], "sCT")
                        for c in range(NCH):
                            pi = 2 + (c % 2)
                            P.add(PE, "matmul", r=["sBT", "sCT"], w=[PSK[pi]], out=ps[pi][0:64, 0:64], lhsT=BT[:, c * 64:(c + 1) * 64], rhs=CT[:, c * 64:(c + 1) * 64], start=True, stop=True)
                            P.add(DVE if c % 2 else ACT, "tensor_copy" if c % 2 else "copy", r=[PSK[pi]], w=["scbT"], out=cbT[:, c, :], in_=ps[pi][0:64, 0:64])
                        P.flush()
                    for d in range(2):
                        c0_ = 8736 + d * 32 + quad * 4
                        P.dma(wsm_s[:, :, d * 4:(d + 1) * 4], w_in_v[:, :, c0_:c0_ + 4], w=["swsm_s"])
                    P.add(POOL, "tensor_copy", r=["swsm_s"], w=["swsm_b"], out=wsm_b[:], in_=wsm_s[:])
                    for d in range(2):
                        P.add(DVE, "tensor_copy", w=["sbq"], out=bq[:, 0, :, d], in_=brt[:, 32 + d * 32 + quad * 4:32 + d * 32 + quad * 4 + 4])
                        P.add(ACT, "activation", w=["sbq"], out=bq[:, 1, :, d], in_=brt[:, 128 + d * 32 + quad * 4:128 + d * 32 + quad * 4 + 4], func=AF.Exp)
                    P.add(DVE, "tensor_scalar", r=["sbq"], w=["sbq"], out=bq[:, 1, :, :], in0=bq[:, 1, :, :], scalar1=-1.0, scalar2=None, op0=ALU.mult)
                    for c in range(NCH):
                        pi = 2 + (c % 2)
                        for k in range(8):
                            P.add(PE, "matmul", r=["swsm_b"], w=[PSK[pi]], out=ps[pi][0:64, 0:8], lhsT=hT[:, k, c * 64:(c + 1) * 64], rhs=wsm_b[:, k, :], start=(k == 0), stop=(k == 7))
                        P.add(DVE if c % 2 else ACT, "tensor_copy" if c % 2 else "copy", r=[PSK[pi]], w=["sdtr"], out=dtr[:, c, :, :].rearrange("p h d -> p d h"),
                              in_=ps[pi][0:64, 0:8].rearrange("p (d h) -> p d h", d=2))
                    qshape = [64, NCH, 4, 2]
                    P.add(DVE, "tensor_tensor", r=["sdtr", "sbq"], w=["sdtr"], out=dtr[:], in0=dtr[:], in1=bq[:, 0, :, :].unsqueeze(1).broadcast_to(qshape), op=ALU.add)
                    P.add(ACT, "activation", r=["sdtr"], w=["sdtr"], out=dtr[:], in_=dtr[:], func=AF.Exp)
                    P.add(ACT, "activation", r=["sdtr"], w=["sdtr"], out=dtr[:], in_=dtr[:], func=AF.Ln, bias=1.0)
                    P.add(DVE, "tensor_tensor", r=["sdtr", "sbq"], w=["sar"], out=ar[:], in0=dtr[:], in1=bq[:, 1, :, :].unsqueeze(1).broadcast_to(qshape), op=ALU.mult)
                    rhs = ar[:].rearrange("p c h d -> p (c h d)")
                    P.add(PE, "matmul", r=["sar", "m_le"], w=[PSK[4]], out=ps[4][0:64, 0:288], lhsT=m_le[:], rhs=rhs, start=True, stop=True)
                    P.add(PE, "matmul", r=["sar", "m_ge"], w=[PSK[5]], out=ps[5][0:64, 0:288], lhsT=m_ge[:], rhs=rhs, start=True, stop=True)
                    P.add(PE, "matmul", r=["sar", "ones"], w=[PSK[6]], out=ps[6][:, 0:288], lhsT=ones_f[0:64, :], rhs=rhs, start=True, stop=True)
                    P.add(DVE, "tensor_copy", r=[PSK[4]], w=["sacs"], out=acs[:, :, :, 0], in_=ps[4][0:64, 0:288].rearrange("p (c h d) -> p c h d", h=4, d=2)[:, :, :, 0])
                    P.add(DVE, "tensor_copy", r=[PSK[5]], w=["sacs"], out=acs[:, :, :, 1], in_=ps[5][0:64, 0:288].rearrange("p (c h d) -> p c h d", h=4, d=2)[:, :, :, 1])
                    P.add(ACT, "activation", r=[PSK[6]], w=["setot"], out=etot[:].rearrange("p c h d -> p (c h d)"), in_=ps[6][:, 0:288], func=AF.Exp)
                    P.add(DVE, "tensor_tensor", r=[PSK[6], "sacs"], w=["swsc"], out=wsc[:].rearrange("p c h d -> p (c h d)"), in0=ps[6][0:64, 0:288],
                          in1=acs[:].rearrange("p c h d -> p (c h d)"), op=ALU.subtract)
                    P.add(ACT, "activation", r=["swsc"], w=["swsc"], out=wsc[:], in_=wsc[:], func=AF.Exp)
                    P.add(DVE, "tensor_tensor", r=["swsc", "sdtr"], w=["swsc"], out=wsc[:], in0=wsc[:], in1=dtr[:], op=ALU.mult)
                    P.add(ACT, "activation", r=["sacs"], w=["seacs"], out=eacs[:], in_=acs[:], func=AF.Exp)
                    for ft in range(2):
                        tile_i = quad * 2 + ft
                        project(XBC0 + tile_i * 128, evac_cin, ft)
                        conv_silu(tile_i, co, "sco")
                        to_tok(co, "sco", lambda g4, ft=ft: xs_tok[:, g4 * 4:g4 * 4 + 4, ft * 128:(ft + 1) * 128], "sxstok", 0)
                    for ft in range(2):
                        tile_i = quad * 2 + ft
                        project(4128 + tile_i * 128, evac_co, ft)
                        to_tok(co, "sco", lambda g4, ft=ft: z_tok[:, g4 * 4:g4 * 4 + 4, ft * 128:(ft + 1) * 128], "sztok", 4, silu=True)
                    P.flush()
                    def diag_chunk(c, a):
                        pm = 2 * a
                        py = 2 * a + 1
                        P.add(DVE, "tensor_tensor", r=["sm2d", "sar"], w=[("sR1", a)], out=R1[a][:], in0=m2d[:].unsqueeze(2).broadcast_to([64, 2, 4, 64]),
                              in1=ar[:, c, :, :].rearrange("p h d -> p d h").unsqueeze(3).broadcast_to([64, 2, 4, 64]), op=ALU.mult)
                        yield
                        P.add(PE, "matmul", r=[("sR1", a), "ones"], w=[PSK[pm]], out=ps[pm][0:64, 0:512], lhsT=ones_f[0:64, 0:64], rhs=R1[a][:].rearrange("p d h l -> p (d h l)"), start=True, stop=False)
                        P.add(PE, "matmul", r=["sneg8", "ident_f"], w=[PSK[pm]], out=ps[pm][0:64, 0:512], lhsT=id64, rhs=neg8[:].rearrange("p d h l -> p (d h l)"), start=False, stop=True)
                        yield
                        P.add(DVE, "tensor_tensor", r=[PSK[pm], "sacs"], w=[("sMT", a)], out=MT[a][:], in0=ps[pm][0:64, 0:512].rearrange("p (d h l) -> p d h l", d=2, h=4),
                              in1=acs[:, c, :, :].rearrange("p h d -> p d h").unsqueeze(3).broadcast_to([64, 2, 4, 64]), op=ALU.subtract)
                        yield
                        P.add(ACT, "activation", r=[("sMT", a)], w=[("sMT", a)], out=MT[a][:], in_=MT[a][:], func=AF.Exp)
                        yield
                        P.add(DVE, "tensor_tensor", r=[("sMT", a), "scbT"], w=[("sMTb", a)], out=MTb[a][:].rearrange("p d h l -> p (d h) l"), in0=MT[a][:].rearrange("p d h l -> p (d h) l"),
                              in1=cbT[:, c, :].unsqueeze(1).broadcast_to([64, 8, 64]), op=ALU.mult)
                        P.add(POOL, "tensor_tensor", r=["sxstok", "sdtr"], w=[("sxsdt", a)], out=xsdt[a][:], in0=xs_tok[:, c, :].rearrange("p (h q) -> p h q", h=4).unsqueeze(1).broadcast_to([64, 2, 4, 64]),
                              in1=dtr[:, c, :, :].rearrange("p h d -> p d h").unsqueeze(3).broadcast_to([64, 2, 4, 64]), op=ALU.mult)
                        yield
                        for hh in range(4):
                            for d in range(2):
                                P.add(PE, "matmul", r=[("sMTb", a), ("sxsdt", a)], w=[PSK[py]], out=ps[py][0:64, hh * 64:(hh + 1) * 64], lhsT=MTb[a][:, d, hh, :], rhs=xsdt[a][:, d, hh, :],
                                      start=(d == 0), stop=(d == 1))
                        yield
                        P.add(ACT, "copy", r=[PSK[py]], w=[("sytok", c)], out=y_tok[:, c - 4, :], in_=ps[py][0:64, 0:256])
                    G = 4
                    for c0 in range(4, NCH, G):
                        gens = [diag_chunk(c, i) for i, c in enumerate(range(c0, min(c0 + G, NCH)))]
                        while gens:
                            for g_ in list(gens):
                                try:
                                    next(g_)
                                except StopIteration:
                                    gens.remove(g_)
                    P.flush()
                    orders = [list(range(NCH)), [3, 2, 1, 0] + list(range(NCH - 1, 3, -1))]
                    for d in range(2):
                        P.add(POOL, "memset", w=[("sST", d)], ap=ST[d][:], constant=0.0)
                        P.add(POOL, "memset", w=[("sSTb", d)], ap=STb[d][:], constant=0.0)
                    for step in range(NCH):
                        for d in range(2):
                            c = orders[d][step]
                            vi = d * 2 + (step % 2)
                            po, pst = d, 2 + d
                            if c >= 4:
                                P.add(PE, "matmul", r=["sCT", ("sSTb", d)], w=[PSK[po]], out=ps[po][0:64, 0:256], lhsT=CT[:, c * 64:(c + 1) * 64], rhs=STb[d][:], start=True, stop=True)
                                P.add(DVE, "tensor_tensor", r=[PSK[po], "seacs"], w=[("sytmp", vi % 2)], out=ytmp[vi][:].rearrange("p (h q) -> p h q", h=4),
                                      in0=ps[po][0:64, 0:256].rearrange("p (h q) -> p h q", h=4), in1=eacs[:, c, :, d].unsqueeze(2).broadcast_to([64, 4, 64]), op=ALU.mult)
                                P.add(POOL, "tensor_tensor", r=[("sytmp", vi % 2), ("sytok", c)], w=[("sytok", c)], out=y_tok[:, c - 4, :], in0=y_tok[:, c - 4, :], in1=ytmp[vi][:], op=ALU.add)
                            P.add(POOL, "tensor_tensor", r=["sxstok", "swsc"], w=[("sxsw", vi)], out=xsw[vi][:].rearrange("p (h q) -> p h q", h=4), in0=xs_tok[:, c, :].rearrange("p (h q) -> p h q", h=4),
                                  in1=wsc[:, c, :, d].unsqueeze(2).broadcast_to([64, 4, 64]), op=ALU.mult)
                            P.add(PE, "matmul", r=["sBtok", ("sxsw", vi)], w=[PSK[pst]], out=ps[pst][:, 0:256], lhsT=B_tok[:, c, :], rhs=xsw[vi][:], start=True, stop=True)
                            P.add(DVE, "tensor_tensor", r=[("sST", d), "setot"], w=[("sST", d)], out=ST[d][:].rearrange("p (h q) -> p h q", h=4), in0=ST[d][:].rearrange("p (h q) -> p h q", h=4),
                                  in1=etot[:, c, :, d].unsqueeze(2).broadcast_to([128, 4, 64]), op=ALU.mult)
                            P.add(DVE, "tensor_tensor", r=[("sST", d), PSK[pst]], w=[("sST", d)], out=ST[d][:], in0=ST[d][:], in1=ps[pst][:, 0:256], op=ALU.add)
                            P.add(ACT, "copy", r=[("sST", d)], w=[("sSTb", d)], out=STb[d][:], in_=ST[d][:])
                    P.flush()
                    if dbg:
                        P.dma(out.rearrange("(c p) n -> p c n", p=64)[:, :, (quad % 4) * 256:(quad % 4 + 1) * 256], y_tok[:], r=[("sytok", c) for c in range(4, NCH)])
                        P.flush()
                        continue
                    scr = feat[0:64, 0:16 * 256].rearrange("p (c q) -> p c q", q=256)
                    yk = [("sytok", c) for c in range(4, NCH)]
                    for half in range(2):
                        cs = slice(half * 16, half * 16 + 16)
                        P.add(DVE, "tensor_tensor", r=["sxstok", "sDt"], w=["sscr"], out=scr.rearrange("p c (h q) -> p c h q", h=4),
                              in0=xs_tok[:, 4 + half * 16:4 + half * 16 + 16, :].rearrange("p c (h q) -> p c h q", h=4),
                              in1=Dt[:, quad * 4:quad * 4 + 4].unsqueeze(1).unsqueeze(3).broadcast_to([64, 16, 4, 64]), op=ALU.mult)
                        P.add(DVE, "tensor_tensor", r=["sscr"] + yk, w=yk, out=y_tok[:, cs, :], in0=y_tok[:, cs, :], in1=scr, op=ALU.add)
                    P.add(DVE, "tensor_tensor", r=["sztok"] + yk, w=["sztok"], out=z_tok[:], in0=y_tok[:], in1=z_tok[:], op=ALU.mult)
                    it = 0
                    for ft in range(2):
                        for bi in range(4):
                            for j in range(8):
                                P.add(PE, "transpose", r=["sztok", "ident_b"], w=[PSK[7]], out=psb[:, j * 64:(j + 1) * 64], in_=z_tok[:, bi * 8 + j, ft * 128:(ft + 1) * 128], identity=ident_b[0:64, 0:64])
                            P.add(ACT, "copy", r=[PSK[7]], w=[("syst", 0)], out=yst[it % 2][:], in_=psb[:, 0:512])
                            P.dma(yab[8 + quad * 2 + ft, :, bi * 512:(bi + 1) * 512], yst[it % 2][:], r=[("syst", 0)])
                            it += 1
                    P.flush()

        def merge():
            w_bg_v = w_bg.rearrange("(k p) n -> p k n", p=128)
            w_bm_v = w_bm.rearrange("(k p) n -> p k n", p=128)
            w_o_v = w_o.rearrange("(k p) n -> p k n", p=128)
            with ExitStack() as mx:
                print("merge: sbuf remaining", nc.sbuf_bytes_remaining, flush=True)
                yaT = sb(mx, "myaT", [128, 8, 1024], BF16)
                ybT = sb(mx, "mybT", [128, 16, 1024], BF16)
                mrg = sb(mx, "mmrg", [128, 8, 1024], BF16)
                wst = sb(mx, "mwst", [128, 16, 128])
                wgb = [sb(mx, f"mwgb{i}", [128, 8, 128], BF16) for i in range(2)]
                wmb = [sb(mx, "mwmb0", [128, 16, 128], BF16)] * 2
                gab = [sb(mx, f"mgab{i}", [128, 8, 128], BF16) for i in range(2)]
                gbb = [sb(mx, f"mgbb{i}", [128, 8, 128], BF16) for i in range(2)]
                sq = [sb(mx, "msq0", [128, 512])] * 2
                rs = sb(mx, "mrs", [128, 512])
                sg = [sb(mx, f"msg{i}", [128, 512]) for i in range(2)]
                ma = [sb(mx, f"mma{i}", [128, 512]) for i in range(2)]
                xb = [sb(mx, "mxb0", [128, 1024])] * 2
                for H0 in (0, 1024):
                    for t in range(8):
                        P.dma(yaT[:, t, :], yab[t][:, H0:H0 + 1024], w=[("myaT", t)])
                    for t in range(16):
                        P.dma(ybT[:, t, :], yab[8 + t][:, H0:H0 + 1024], w=[("mybT", t)])
                    for g_ in range(2):
                        for bi in range(2):
                            pn = 6 + (bi % 2)
                            tsl = slice(bi * 512, (bi + 1) * 512)
                            for t in range(8):
                                tt = g_ * 8 + t
                                P.add(ACT, "activation", r=[("mybT", tt)], w=[("msq", 0)], out=sq[t % 2][:], in_=ybT[:, tt, tsl], func=AF.Square)
                                P.add(PE, "matmul", r=[("msq", 0), "ones"], w=[PSK[pn]], out=ps[pn][:, :], lhsT=ones_f[:], rhs=sq[t % 2][:], start=(t == 0), stop=(t == 7))
                            P.add(ACT, "activation", r=[PSK[pn]], w=["mrs"], out=rs[:], in_=ps[pn][:, :], func=AF.Sqrt, scale=1.0 / 1024, bias=epsc[:, 0:1])
                            P.add(DVE, "reciprocal", r=["mrs"], w=["mrs"], out=rs[:], in_=rs[:])
                            for t in range(8):
                                tt = g_ * 8 + t
                                P.add(DVE, "scalar_tensor_tensor", r=[("mybT", tt), "mrs"], w=[("mybT", tt)], out=ybT[:, tt, tsl], in0=ybT[:, tt, tsl], scalar=cols1[:, 105 + tt:106 + tt],
                                      in1=rs[:], op0=ALU.mult, op1=ALU.mult)
                    it = 0
                    for d in range(8):
                        b = d % 2
                        dsl = slice(d * 128, (d + 1) * 128)
                        P.dma(wst[:, 0:8, :], w_bg_v[:, :, dsl], w=["mwst"])
                        P.add(POOL, "tensor_copy", r=["mwst"], w=[("mwgb", b)], out=wgb[b][:], in_=wst[:, 0:8, :])
                        P.dma(wst[:, :, :], w_bm_v[:, :, dsl], w=["mwst"])
                        P.add(POOL, "tensor_copy", r=["mwst"], w=[("mwmb", 0)], out=wmb[b][:], in_=wst[:, :, :])
                        P.dma(wst[:, 0:8, :], w_in_v[:, :, 8800 + d * 128:8800 + (d + 1) * 128], w=["mwst"])
                        P.add(POOL, "tensor_copy", r=["mwst"], w=[("mgab", b)], out=gab[b][:], in_=wst[:, 0:8, :])
                        P.dma(wst[:, 0:8, :], w_in_v[:, :, 9824 + d * 128:9824 + (d + 1) * 128], w=["mwst"])
                        P.add(POOL, "tensor_copy", r=["mwst"], w=[("mgbb", b)], out=gbb[b][:], in_=wst[:, 0:8, :])
                        for bi in range(2):
                            tsl = slice(bi * 512, (bi + 1) * 512)
                            hsl = slice(256 + H0 + bi * 512, 256 + H0 + (bi + 1) * 512)
                            q = it % 2
                            it += 1
                            pa, pga, pb_, pgb = q * 4, q * 4 + 1, q * 4 + 2, q * 4 + 3
                            for k in range(8):
                                P.add(PE, "matmul", r=[("mwgb", b), ("myaT", k)], w=[PSK[pa]], out=ps[pa][:, :], lhsT=wgb[b][:, k, :], rhs=yaT[:, k, tsl], start=(k == 0), stop=(k == 7))
                            for k in range(8):
                                P.add(PE, "matmul", r=[("mgab", b)], w=[PSK[pga]], out=ps[pga][:, :], lhsT=gab[b][:, k, :], rhs=hT[:, k, hsl], start=(k == 0), stop=(k == 7))
                            for k in range(16):
                                P.add(PE, "matmul", r=[("mwmb", 0), ("mybT", k)], w=[PSK[pb_]], out=ps[pb_][:, :], lhsT=wmb[b][:, k, :], rhs=ybT[:, k, tsl], start=(k == 0), stop=(k == 15))
                            for k in range(8):
                                P.add(PE, "matmul", r=[("mgbb", b)], w=[PSK[pgb]], out=ps[pgb][:, :], lhsT=gbb[b][:, k, :], rhs=hT[:, k, hsl], start=(k == 0), stop=(k == 7))
                            P.add(ACT, "activation", r=[PSK[pga]], w=[("msg", 0)], out=sg[0][:], in_=ps[pga][:, :], func=AF.Sigmoid)
                            P.add(DVE, "tensor_tensor", r=[("msg", 0), PSK[pa]], w=[("mma", 0)], out=ma[0][:], in0=sg[0][:], in1=ps[pa][:, :], op=ALU.mult)
                            P.add(ACT, "activation", r=[PSK[pgb]], w=[("msg", 1)], out=sg[1][:], in_=ps[pgb][:, :], func=AF.Sigmoid)
                            P.add(DVE, "tensor_tensor", r=[("msg", 1), PSK[pb_]], w=[("mma", 1)], out=ma[1][:], in0=sg[1][:], in1=ps[pb_][:, :], op=ALU.mult)
                            P.add(POOL, "tensor_tensor", r=[("mma", 0), ("mma", 1)], w=[("mmrg", d)], out=mrg[:, d, tsl], in0=ma[0][:], in1=ma[1][:], op=ALU.add)
                    for d in range(8):
                        b = d % 2
                        dsl = slice(d * 128, (d + 1) * 128)
                        P.dma(wst[:, 0:8, :], w_o_v[:, :, dsl], w=["mwst"])
                        P.add(POOL, "tensor_copy", r=["mwst"], w=[("mwgb", b)], out=wgb[b][:], in_=wst[:, 0:8, :])
                        P.dma(xb[b][:], xsp[:, d, H0:H0 + 1024], w=[("mxb", 0)])
                        for bi in range(2):
                            tsl = slice(bi * 512, (bi + 1) * 512)
                            po = (it % 2) * 4
                            it += 1
                            for k in range(8):
                                P.add(PE, "matmul", r=[("mwgb", b)] + [("mmrg", k)], w=[PSK[po]], out=ps[po][:, :], lhsT=wgb[b][:, k, :], rhs=mrg[:, k, tsl], start=(k == 0), stop=(k == 7))
                            P.add(DVE, "scalar_tensor_tensor", r=[PSK[po], ("mxb", 0)], w=[("mxb", 0)], out=xb[b][:, tsl], in0=ps[po][:, :], scalar=mvap(1, 0, 2, d), in1=xb[b][:, tsl],
                                  op0=ALU.mult, op1=ALU.add)
                        P.dma(xsp[:, d, H0:H0 + 1024], xb[b][:], r=[("mxb", 0)])
                    P.flush()

        with ExitStack() as pa:
            xT = sb(pa, "xT", [128, 8, T])
            cur["xT"] = xT
            with ExitStack() as s1:
                xst = [sb(s1, f"xst{i}", [128, D]) for i in range(2)]
                for t in range(18):
                    b = t % 2
                    P.dma(xst[b][:], xin[t * 128:(t + 1) * 128, :], w=[("xst", b)])
                    for half in range(2):
                        pi = 2 * b + half
                        for j in range(4):
                            P.add(PE, "transpose", r=[("xst", b), "ident_f"], w=[PSK[pi]], out=ps[pi][:, j * 128:(j + 1) * 128],
                                  in_=xst[b][:, (half * 4 + j) * 128:(half * 4 + j + 1) * 128], identity=ident_f[:])
                        wk = [("xT", k, (BLKS[0][0] if t < 2 else BLKS[1 + (t - 2) // 4][0])) for k in range(half * 4, half * 4 + 4)]
                        if half:
                            P.add(ACT, "copy", r=[PSK[pi]], w=wk, out=xT[:, half * 4:half * 4 + 4, t * 128:(t + 1) * 128], in_=v3(ps[pi][:, 0:512], 128))
                        else:
                            P.add(DVE, "tensor_copy", r=[PSK[pi]], w=wk, out=xT[:, half * 4:half * 4 + 4, t * 128:(t + 1) * 128], in_=v3(ps[pi][:, 0:512], 128))
                crt = sb(s1, "crt", [16, 128])
                cs2 = sb(s1, "cs2", [128, 8, 2])
                modT = sb(s1, "modT", [128, 72, 2])
                wst = [sb(s1, f"wst{i}", [128, 8, 512]) for i in range(2)]
                P.dma(crt[:], crow, w=["crt"])
                P.add(ACT, "activation", r=["crt"], w=["crt2"], out=crt[:], in_=crt[:], func=AF.Silu)
                P.add(PE, "transpose", r=["crt2", "ident_f"], w=[PSK[4]], out=ps[4][:, 0:16], in_=crt[:], identity=ident_f[0:16, 0:16])
                P.add(DVE, "tensor_copy", r=[PSK[4]], w=["cs2"], out=cs2[:, :, 0], in_=ps[4][:, 0:8])
                P.add(DVE, "tensor_copy", r=[PSK[4]], w=["cs2"], out=cs2[:, :, 1], in_=ps[4][:, 8:16])
                for cg in range(18):
                    b = cg % 2
                    P.dma(wst[b][:], w_ada_v[:, :, cg * 512:(cg + 1) * 512], w=[("wst", b)])
                    for jj in range(4):
                        j = cg * 4 + jj
                        for k in range(8):
                            P.add(PE, "matmul", r=[("wst", b), "cs2"], w=[PSK[5]], out=ps[5][:, 2 * j:2 * j + 2],
                                  lhsT=wst[b][:, k, jj * 128:(jj + 1) * 128], rhs=cs2[:, k, :], start=(k == 0), stop=(k == 7))
                P.add(DVE, "tensor_tensor", r=[PSK[5], ("cols", 0)], w=["modT"], out=modT[:], in0=v3(ps[5][:, 0:144], 2),
                      in1=cols1[:, 0:72].unsqueeze(2).broadcast_to([128, 72, 2]), op=ALU.add)
                for s in range(3):
                    for w in range(2):
                        i0 = ((s * 2 + w) * 3) * 8
                        P.add(DVE, "scalar_tensor_tensor", r=["modT", ("cols", 0)], w=["mv"], out=mv[:, i0:i0 + 8],
                              in0=modT[:, (3 * s + 1) * 8:(3 * s + 2) * 8, w], scalar=1.0, in1=cols1[:, 72 + s * 8:72 + s * 8 + 8],
                              op0=ALU.add, op1=ALU.mult)
                        P.add(DVE, "tensor_copy", r=["modT"], w=["mv"], out=mv[:, i0 + 8:i0 + 16], in_=modT[:, (3 * s) * 8:(3 * s + 1) * 8, w])
                        P.add(DVE, "tensor_scalar", r=["modT"], w=["mv"], out=mv[:, i0 + 16:i0 + 24],
                              in0=modT[:, (3 * s + 2) * 8:(3 * s + 3) * 8, w], scalar1=(1.0 if s == 1 else 0.5), scalar2=None, op0=ALU.mult)
                P.flush()
            if STAGE >= 1 and not os.environ.get("KSKIPFFN"):
                rmsnorm_mod(0, BLKS)
                ffn(0, 0, BLKS)
            if STAGE <= 1:
                write_out(False)
                return nc
            rmsnorm_mod(1, BLKS)
            for k in range(8):
                if not os.environ.get("KNOSPILL"):
                    P.dma(xsp[:, k, :], xT[:, k, 256:T], r=[("xT", k, t0) for (t0, n) in BLKS])
            P.flush()
        if STAGE != 3:
            gdn(dbg=(STAGE == 2))
        if STAGE >= 3:
            ssd(dbg=(STAGE == 3))
        if STAGE in (2, 3):
            P.final_waits = [i for i in P.dma_last.values()]
            P.add(POOL, "memset", w=["zeros"], ap=zeros_f[:], constant=0.0)
            P.flush(final=True)
            return nc
        merge()
        with ExitStack() as pc:
            xT = sb(pc, "xT2", [128, 8, T])
            cur["xT"] = xT
            for k in range(8):
                P.dma(xT[:, k, 256:T], xsp[:, k, :], w=[("xT", k, t0) for (t0, n) in LBLKS])
            if STAGE == 4:
                write_out(False)
                return nc
            rmsnorm_mod(2, LBLKS)
            ffn(2, 1, LBLKS)
            write_out(True)
        return nc


def prep_inputs(inputs, b):
    f = lambda a: np.ascontiguousarray(np.asarray(a, dtype=np.float32))
    m = {}
    m["xin"] = f(np.concatenate([inputs["ctx"][b], inputs["x"][b]], axis=0))
    m["crow"] = f(np.concatenate([np.asarray(inputs["c"][b]).reshape(8, 128), np.asarray(inputs["c_ctx"]).reshape(8, 128)], 0))
    r1 = np.zeros((128, 128), np.float32)
    r1[0:72] = np.asarray(inputs["b_ada"][0]).reshape(72, 128)
    r1[72:96] = np.asarray(inputs["norm_g"][0]).reshape(24, 128)
    r1[96:104] = np.asarray(inputs["final_g"]).reshape(8, 128)
    r1[104:105] = np.asarray(inputs["gdn_norm_g"][0]).reshape(1, 128)
    r1[105:121] = np.asarray(inputs["mb_norm_g"][0]).reshape(16, 128)
    m["rows1"] = r1
    r2 = np.zeros((128, 128), np.float32)
    r2[0:120] = np.asarray(inputs["gdn_conv_w"][0]).reshape(120, 128)
    m["rows2"] = r2
    r3 = np.zeros((128, 128), np.float32)
    r3[0:100] = np.asarray(inputs["mb_conv_w"][0]).reshape(100, 128)
    r3[100:120] = np.asarray(inputs["mb_conv_b"][0]).reshape(20, 128)
    m["rows3"] = r3
    br = np.zeros((1, 256), np.float32)
    br[0, 0:16] = np.asarray(inputs["gdn_dt_bias"][0]).reshape(16)
    br[0, 32:96] = np.asarray(inputs["mb_dt_bias"][0]).reshape(64)
    br[0, 96:112] = np.asarray(inputs["gdn_A_log"][0]).reshape(16)
    br[0, 128:192] = np.asarray(inputs["mb_A_log"][0]).reshape(64)
    br[0, 192:224] = np.asarray(inputs["mb_D"][0]).reshape(32)
    m["brow"] = br
    m["w_ada"] = f(inputs["w_ada"][0])
    m["w_gu"] = f(inputs["ffn_w_gu"][0])
    m["w_dn"] = f(inputs["ffn_w_down"][0])
    m["w_in"] = f(inputs["w_in"][0])
    m["w_bg"] = f(inputs["w_branch_gdn"][0])
    m["w_bm"] = f(inputs["w_branch_mb"][0])
    m["w_o"] = f(inputs["w_out"][0])
    return m


def kernel(**inputs):
    nc = build_program()
    shared = None
    in_maps = []
    for b in range(8):
        m = prep_inputs(inputs, b)
        if shared is None:
            shared = {k: m[k] for k in ("w_ada", "w_gu", "w_dn", "w_in", "w_bg", "w_bm", "w_o", "rows1", "rows2", "rows3", "brow")}
        else:
            m.update(shared)
        in_maps.append(m)
    res = run_bass_kernel_spmd(nc, in_maps, core_ids=list(range(8)))
    return np.stack([np.asarray(r["out"], dtype=np.float32) for r in res.results], axis=0)
```

```python
import os
from contextlib import ExitStack
import numpy as np
import concourse.bass as bass
import concourse.mybir as mybir
from concourse.bass_utils import run_bass_kernel_spmd

F32 = mybir.dt.float32
BF16 = mybir.dt.bfloat16
AF = mybir.ActivationFunctionType
ALU = mybir.AluOpType
AX = mybir.AxisListType

PE, ACT, DVE, POOL, SP = "pe", "act", "dve", "pool", "sp"
ENGS = [PE, ACT, DVE, POOL, SP]
EPOCH = 24000
NEPOCH = 4
NDMASEM = 12

D = 1024
T = 2304
NCH = 36
DFF = 2816
DIN = 10848
EPS = 1e-6
BLKS = [(0, 256), (256, 512), (768, 512), (1280, 512), (1792, 512)]
LBLKS = BLKS[1:]
STAGE = int(os.environ.get("KSTAGE", "99"))


class Op:
    __slots__ = ("eng", "fn", "deps", "sig", "is_dma", "idx", "signals", "q")


class Prog:
    def __init__(self, nc, stack):
        self.nc = nc
        self.ops = []
        self.pending = []
        self.lastw = {}
        self.readers = {}
        self.dma_count = 0
        self.dma_last = {}
        self.cnt = {e: 0 for e in ENGS}
        self.dcnt = [0] * NDMASEM
        self.sems = {}
        for e in ENGS:
            for ep in range(NEPOCH):
                self.sems[(e, ep)] = stack.enter_context(nc.semaphore(f"s_{e}_{ep}"))
        self.dsems = [stack.enter_context(nc.semaphore(f"s_dma_{i}")) for i in range(NDMASEM)]
        self.seen = {e: {} for e in ENGS}
        self.barrier = []
        self.lastop = {}
        self.got_barrier = set()
        self.final_waits = []

    def add(self, eng, name, r=(), w=(), dma=False, **kw):
        o = Op()
        o.eng = eng
        o.is_dma = dma
        o.sig = None
        o.signals = False
        o.q = None
        o.fn = lambda e: getattr(e, name)(**kw)
        o.idx = len(self.ops)
        deps = set()
        pr = [k for k in r if isinstance(k, tuple) and k and k[0] == "ps"]
        if pr:
            r = [k for k in r if k not in pr]
            w = list(w) + pr
        for k in r:
            lw = self.lastw.get(k)
            if lw is not None:
                deps.add(lw)
        for k in w:
            lw = self.lastw.get(k)
            if lw is not None:
                deps.add(lw)
            for rd in self.readers.get(k, ()):
                deps.add(rd)
        for k in r:
            lst = self.readers.setdefault(k, [])
            if not dma:
                lst[:] = [i for i in lst if self.ops[i].eng != eng or self.ops[i].is_dma]
            lst.append(o.idx)
        for k in w:
            self.lastw[k] = o.idx
            self.readers[k] = []
        if dma:
            slot = self.dma_count % NDMASEM
            o.q = slot
            prev = self.dma_last.get(slot)
            if prev is not None:
                deps.add(prev)
            self.dma_last[slot] = o.idx
            self.dma_count += 1
        if eng not in self.got_barrier:
            self.got_barrier.add(eng)
            deps.update(self.barrier)
        deps.discard(o.idx)
        o.deps = deps
        self.ops.append(o)
        self.pending.append(o)
        if not dma:
            self.lastop[eng] = o.idx
        return o

    def dma(self, out, in_, r=(), w=(), eng=SP):
        return self.add(eng, "dma_start", r=r, w=w, dma=True, out=out, in_=in_)

    def sem_of(self, sig):
        if sig[0] == "d":
            return ("d", sig[1]), self.dsems[sig[1]], sig[2]
        return ("c", sig[1], sig[2]), self.sems[(sig[1], sig[2])], sig[3]

    def flush(self, final=False):
        nc = self.nc
        ops = self.ops
        pend = self.pending
        self.pending = []
        nb = [i for i in self.lastop.values()] + [i for i in self.dma_last.values()]
        for i in nb:
            ops[i].signals = True
        for o in pend:
            for d in o.deps:
                od = ops[d]
                if od.eng == PE and o.eng == PE and not od.is_dma and not o.is_dma:
                    continue
                if od.sig is None:
                    od.signals = True
        for o in pend:
            if o.is_dma:
                self.dcnt[o.q] += 16
                o.sig = ("d", o.q, self.dcnt[o.q])
            elif o.signals:
                c = self.cnt[o.eng]
                assert c // EPOCH < NEPOCH
                o.sig = ("c", o.eng, c // EPOCH, (c % EPOCH) + 1)
                self.cnt[o.eng] = c + 1
        per = {e: [o for o in pend if o.eng == e] for e in ENGS}
        finals = list(self.final_waits) if final else []

        def run(engobj, ename):
            seen = self.seen[ename]
            for o in per[ename]:
                waits = {}
                for d in o.deps:
                    od = ops[d]
                    if od.eng == PE and o.eng == PE and not od.is_dma and not o.is_dma:
                        continue
                    if od.sig is None:
                        raise AssertionError((od.eng, o.eng, d, o.idx))
                    key, sh, val = self.sem_of(od.sig)
                    if seen.get(key, 0) >= val:
                        continue
                    if key not in waits or waits[key][1] < val:
                        waits[key] = (sh, val)
                for key, (sh, val) in waits.items():
                    engobj.wait_ge(sh, val)
                    seen[key] = val
                ins = o.fn(engobj)
                if o.sig is not None:
                    key, sh, val = self.sem_of(o.sig)
                    ins.then_inc(sh, 16 if o.is_dma else 1)
            if ename == SP:
                for i in finals:
                    key, sh, val = self.sem_of(ops[i].sig)
                    engobj.wait_ge(sh, val)

        with nc.Block() as block:
            @block.tensor
            def _(e):
                run(e, PE)

            @block.scalar
            def _(e):
                run(e, ACT)

            @block.vector
            def _(e):
                run(e, DVE)

            @block.gpsimd
            def _(e):
                run(e, POOL)

            @block.sync
            def _(e):
                run(e, SP)
        self.barrier = nb
        self.got_barrier = set()
        self.lastw = {}
        self.readers = {}


def v3(ap, b):
    return ap.rearrange("p (a b) -> p a b", b=b)


def build_program():
    nc = bass.Bass("TRN2", target_bir_lowering=False)
    xin = nc.dram_tensor("xin", [T, D], F32, kind="ExternalInput").ap()
    crow = nc.dram_tensor("crow", [16, 128], F32, kind="ExternalInput").ap()
    rows1 = nc.dram_tensor("rows1", [128, 128], F32, kind="ExternalInput").ap()
    rows2 = nc.dram_tensor("rows2", [128, 128], F32, kind="ExternalInput").ap()
    rows3 = nc.dram_tensor("rows3", [128, 128], F32, kind="ExternalInput").ap()
    brow = nc.dram_tensor("brow", [1, 256], F32, kind="ExternalInput").ap()
    w_ada = nc.dram_tensor("w_ada", [D, 9 * D], F32, kind="ExternalInput").ap()
    w_gu = nc.dram_tensor("w_gu", [2, D, 2 * DFF], F32, kind="ExternalInput").ap()
    w_dn = nc.dram_tensor("w_dn", [2, DFF, D], F32, kind="ExternalInput").ap()
    w_in = nc.dram_tensor("w_in", [D, DIN], F32, kind="ExternalInput").ap()
    w_bg = nc.dram_tensor("w_bg", [D, D], F32, kind="ExternalInput").ap()
    w_bm = nc.dram_tensor("w_bm", [2 * D, D], F32, kind="ExternalInput").ap()
    w_o = nc.dram_tensor("w_o", [D, D], F32, kind="ExternalInput").ap()
    out = nc.dram_tensor("out", [2048, D], F32, kind="ExternalOutput").ap()
    xsp = nc.dram_tensor("xsp", [128, 8, 2048], F32, kind="Internal").ap()
    yab = nc.dram_tensor("yab", [24, 128, 2048], BF16, kind="Internal").ap()

    w_ada_v = w_ada.rearrange("(k p) n -> p k n", p=128)
    w_in_v = w_in.rearrange("(k p) n -> p k n", p=128)

    with ExitStack() as g:
        uniq = [0]

        def sb(st, name, shape, dt=F32):
            uniq[0] += 1
            return st.enter_context(nc.sbuf_tensor(f"{name}_{uniq[0]}", shape, dt))

        P = Prog(nc, g)
        ps = [g.enter_context(nc.psum_tensor(f"ps{i}", [128, 512], F32)) for i in range(8)]
        PSK = [("ps", i) for i in range(8)]

        ident_f = sb(g, "ident_f", [128, 128])
        ident_b = sb(g, "ident_b", [128, 128], BF16)
        ones_f = sb(g, "ones_f", [128, 128])
        zeros_f = sb(g, "zeros_f", [64, 256])
        negones = sb(g, "negones", [64, 64])
        m_le = sb(g, "m_le", [64, 64])
        m_ge = sb(g, "m_ge", [64, 64])
        am_f = sb(g, "am_f", [64, 128])
        am_b = sb(g, "am_b", [64, 128])
        neg4_f = sb(g, "neg4_f", [64, 256])
        neg4_b = sb(g, "neg4_b", [64, 256])
        cols1 = sb(g, "cols1", [128, 128])
        cols2 = sb(g, "cols2", [128, 128])
        cols3 = sb(g, "cols3", [128, 128])
        mv = sb(g, "mv", [128, 144])
        hT = sb(g, "hT", [128, 8, T], BF16)
        brt = sb(g, "brt", [64, 256])
        epsc = sb(g, "epsc", [128, 1])

        def mvap(s, w, kind, k):
            i = ((s * 2 + w) * 3 + kind) * 8 + k
            return mv[:, i:i + 1]

        P.add(POOL, "memset", w=["ones"], ap=ones_f[:], constant=1.0)
        P.add(POOL, "memset", w=["zeros"], ap=zeros_f[:], constant=0.0)
        P.add(POOL, "memset", w=["epsc"], ap=epsc[:], constant=EPS)
        P.add(POOL, "memset", w=["negones"], ap=negones[:], constant=-1.0)
        P.add(POOL, "affine_select", r=["ones"], w=["ident_f"], out=ident_f[:], in_=ones_f[:], pattern=[[-1, 128]],
              compare_op=ALU.is_equal, fill=0.0, base=0, channel_multiplier=1)
        P.add(DVE, "tensor_copy", r=["ident_f"], w=["ident_b"], out=ident_b[:], in_=ident_f[:])
        P.add(POOL, "affine_select", r=["ones"], w=["m_le"], out=m_le[:], in_=ones_f[0:64, 0:64], pattern=[[1, 64]],
              compare_op=ALU.is_ge, fill=0.0, base=0, channel_multiplier=-1)
        P.add(POOL, "affine_select", r=["ones"], w=["m_ge"], out=m_ge[:], in_=ones_f[0:64, 0:64], pattern=[[-1, 64]],
              compare_op=ALU.is_ge, fill=0.0, base=0, channel_multiplier=1)
        P.add(POOL, "affine_select", r=["zeros"], w=["am_f"], out=am_f[:, 0:64], in_=zeros_f[:, 0:64], pattern=[[1, 64]],
              compare_op=ALU.is_ge, fill=-30000.0, base=0, channel_multiplier=-1)
        P.add(POOL, "affine_select", r=["zeros"], w=["am_f"], out=am_f[:, 64:128], in_=zeros_f[:, 0:64], pattern=[[-1, 64]],
              compare_op=ALU.is_gt, fill=30000.0, base=0, channel_multiplier=1)
        P.add(POOL, "affine_select", r=["zeros"], w=["am_b"], out=am_b[:, 0:64], in_=zeros_f[:, 0:64], pattern=[[-1, 64]],
              compare_op=ALU.is_ge, fill=-30000.0, base=0, channel_multiplier=1)
        P.add(POOL, "affine_select", r=["zeros"], w=["am_b"], out=am_b[:, 64:128], in_=zeros_f[:, 0:64], pattern=[[1, 64]],
              compare_op=ALU.is_gt, fill=30000.0, base=0, channel_multiplier=-1)
        P.add(POOL, "affine_select", r=["zeros"], w=["neg4"], out=v3(neg4_f[:], 64), in_=v3(zeros_f[:], 64),
              pattern=[[0, 4], [1, 64]], compare_op=ALU.is_ge, fill=-30000.0, base=0, channel_multiplier=-1)
        P.add(POOL, "affine_select", r=["zeros"], w=["neg4"], out=v3(neg4_b[:], 64), in_=v3(zeros_f[:], 64),
              pattern=[[0, 4], [-1, 64]], compare_op=ALU.is_ge, fill=-30000.0, base=0, channel_multiplier=1)
        rst = sb(g, "rst", [128, 128])
        for i, (rw, cl) in enumerate(((rows1, cols1), (rows2, cols2), (rows3, cols3))):
            P.dma(rst[:], rw, w=["rst"])
            P.add(PE, "transpose", r=["rst", "ident_f"], w=[PSK[0]], out=ps[0][:, 0:128], in_=rst[:], identity=ident_f[:])
            P.add(DVE, "tensor_copy", r=[PSK[0]], w=[("cols", i)], out=cl[:], in_=ps[0][:, 0:128])
        P.dma(brt[:], brow.partition_broadcast(64), w=["brt"])
        P.flush()

        cur = {}

        def rmsnorm_mod(s, blks):
            xT = cur["xT"]
            with ExitStack() as st:
                sq = [sb(st, f"nsq{i}", [128, 512]) for i in range(2)]
                rs = sb(st, "nrs", [128, 512])
                tmp = [sb(st, f"ntmp{i}", [128, 512]) for i in range(2)]
                for bi, (t0, n) in enumerate(blks):
                    w = 1 if t0 == 0 else 0
                    pn = 6 + (bi % 2)
                    for k in range(8):
                        P.add(ACT, "activation", r=[("xT", k, t0)], w=[("nsq", k % 2)], out=sq[k % 2][:, :n], in_=xT[:, k, t0:t0 + n], func=AF.Square)
                        P.add(PE, "matmul", r=[("nsq", k % 2), "ones"], w=[PSK[pn]], out=ps[pn][:, :n], lhsT=ones_f[:], rhs=sq[k % 2][:, :n],
                              start=(k == 0), stop=(k == 7))
                    P.add(ACT, "activation", r=[PSK[pn]], w=["nrs"], out=rs[:, :n], in_=ps[pn][:, :n], func=AF.Sqrt, scale=1.0 / D, bias=epsc[:, 0:1])
                    P.add(DVE, "reciprocal", r=["nrs"], w=["nrs"], out=rs[:, :n], in_=rs[:, :n])
                    for k in range(8):
                        P.add(DVE, "tensor_tensor", r=[("xT", k, t0), "nrs"], w=[("ntmp", k % 2)], out=tmp[k % 2][:, :n], in0=xT[:, k, t0:t0 + n],
                              in1=rs[:, :n], op=ALU.mult)
                        P.add(ACT, "activation", r=[("ntmp", k % 2), "mv"], w=[("hT", k, t0)], out=hT[:, k, t0:t0 + n], in_=tmp[k % 2][:, :n],
                              func=AF.Identity, scale=mvap(s, w, 0, k), bias=mvap(s, w, 1, k))
                P.flush()

        def ffn(s, li, blks):
            xT = cur["xT"]
            wgu_v = w_gu[li].rearrange("(k p) n -> p k n", p=128)
            groups = [list(range(0, 6)), list(range(6, 12)), list(range(12, 17)), list(range(17, 22))]
            with ExitStack() as st:
                act = sb(st, "fact", [128, 6, T], BF16)
                wgs = [sb(st, f"fwgs{i}", [128, 8, 256]) for i in range(2)]
                wgb = [sb(st, f"fwgb{i}", [128, 8, 256], BF16) for i in range(2)]
                wds = [sb(st, f"fwds{i}", [128, D]) for i in range(2)]
                wdb = sb(st, "fwdb", [128, 6, D], BF16)
                sl = [sb(st, f"fsl{i}", [128, 512]) for i in range(2)]
                it = 0
                for grp in groups:
                    for jj, j in enumerate(grp):
                        b = j % 2
                        P.dma(wgs[b][:, :, 0:128], wgu_v[:, :, j * 128:(j + 1) * 128], w=[("fwgs", b)])
                        P.dma(wgs[b][:, :, 128:256], wgu_v[:, :, DFF + j * 128:DFF + (j + 1) * 128], w=[("fwgs", b)])
                        P.add(POOL, "tensor_copy", r=[("fwgs", b)], w=[("fwgb", b)], out=wgb[b][:], in_=wgs[b][:])
                        for (t0, n) in blks:
                            q = it % 2
                            it += 1
                            pg, pu = q * 2, q * 2 + 1
                            for k in range(8):
                                P.add(PE, "matmul", r=[("fwgb", b), ("hT", k, t0)], w=[PSK[pg]], out=ps[pg][:, :n], lhsT=wgb[b][:, k, 0:128],
                                      rhs=hT[:, k, t0:t0 + n], start=(k == 0), stop=(k == 7))
                            for k in range(8):
                                P.add(PE, "matmul", r=[("fwgb", b), ("hT", k, t0)], w=[PSK[pu]], out=ps[pu][:, :n], lhsT=wgb[b][:, k, 128:256],
                                      rhs=hT[:, k, t0:t0 + n], start=(k == 0), stop=(k == 7))
                            P.add(ACT, "activation", r=[PSK[pg]], w=[("fsl", q)], out=sl[q][:, :n], in_=ps[pg][:, :n], func=AF.Silu)
                            P.add(DVE, "tensor_tensor", r=[("fsl", q), PSK[pu]], w=[("fact", jj, t0)], out=act[:, jj, t0:t0 + n], in0=sl[q][:, :n],
                                  in1=ps[pu][:, :n], op=ALU.mult)
                    for jj, j in enumerate(grp):
                        b = j % 2
                        P.dma(wds[b][:], w_dn[li, j * 128:(j + 1) * 128, :], w=[("fwds", b)])
                        P.add(POOL, "tensor_copy", r=[("fwds", b)], w=[("fwdb", jj)], out=wdb[:, jj, :], in_=wds[b][:])
                    ng = len(grp)
                    for d in range(8):
                        for (t0, n) in blks:
                            w = 1 if t0 == 0 else 0
                            q = 4 + (it % 2)
                            it += 1
                            for jj in range(ng):
                                P.add(PE, "matmul", r=[("fwdb", jj), ("fact", jj, t0)], w=[PSK[q]], out=ps[q][:, :n],
                                      lhsT=wdb[:, jj, d * 128:(d + 1) * 128], rhs=act[:, jj, t0:t0 + n], start=(jj == 0), stop=(jj == ng - 1))
                            P.add(DVE, "scalar_tensor_tensor", r=[PSK[q], ("xT", d, t0), "mv"], w=[("xT", d, t0)], out=xT[:, d, t0:t0 + n],
                                  in0=ps[q][:, :n], scalar=mvap(s, w, 2, d), in1=xT[:, d, t0:t0 + n], op0=ALU.mult, op1=ALU.add)
                P.flush()

        def write_out(final_norm):
            xT = cur["xT"]
            with ExitStack() as st:
                sq = [sb(st, f"osq{i}", [128, 512]) for i in range(2)]
                rs = sb(st, "ors", [128, 512])
                yt = [sb(st, f"oyt{i}", [128, 8, 512]) for i in range(2)]
                ost = [sb(st, f"oost{i}", [128, D]) for i in range(2)]
                fin = []
                for bi, (t0, n) in enumerate(LBLKS):
                    yb_ = yt[bi % 2]
                    if final_norm:
                        pn = 6 + (bi % 2)
                        for k in range(8):
                            P.add(ACT, "activation", r=[("xT", k, t0)], w=[("osq", k % 2)], out=sq[k % 2][:, :n], in_=xT[:, k, t0:t0 + n], func=AF.Square)
                            P.add(PE, "matmul", r=[("osq", k % 2), "ones"], w=[PSK[pn]], out=ps[pn][:, :n], lhsT=ones_f[:], rhs=sq[k % 2][:, :n],
                                  start=(k == 0), stop=(k == 7))
                        P.add(ACT, "activation", r=[PSK[pn]], w=["ors"], out=rs[:, :n], in_=ps[pn][:, :n], func=AF.Sqrt, scale=1.0 / D, bias=epsc[:, 0:1])
                        P.add(DVE, "reciprocal", r=["ors"], w=["ors"], out=rs[:, :n], in_=rs[:, :n])
                        for k in range(8):
                            P.add(DVE, "scalar_tensor_tensor", r=[("xT", k, t0), "ors", ("cols", 0)], w=[("oyt", bi % 2, k)], out=yb_[:, k, :n],
                                  in0=xT[:, k, t0:t0 + n], scalar=cols1[:, 96 + k:97 + k], in1=rs[:, :n], op0=ALU.mult, op1=ALU.mult)
                    else:
                        for k in range(8):
                            P.add(DVE if k % 2 else POOL, "tensor_copy", r=[("xT", k, t0)], w=[("oyt", bi % 2, k)], out=yb_[:, k, :n], in_=xT[:, k, t0:t0 + n])
                    for tt in range(4):
                        ti = bi * 4 + tt
                        ob = ost[ti % 2]
                        for half in range(2):
                            pi = 2 * (ti % 2) + half
                            for j in range(4):
                                k = half * 4 + j
                                P.add(PE, "transpose", r=[("oyt", bi % 2, k), "ident_f"], w=[PSK[pi]], out=ps[pi][:, j * 128:(j + 1) * 128],
                                      in_=yb_[:, k, tt * 128:(tt + 1) * 128], identity=ident_f[:])
                            if half:
                                P.add(ACT, "copy", r=[PSK[pi]], w=[("oost", ti % 2)], out=ob[:, half * 512:(half + 1) * 512], in_=ps[pi][:, 0:512])
                            else:
                                P.add(DVE, "tensor_copy", r=[PSK[pi]], w=[("oost", ti % 2)], out=ob[:, half * 512:(half + 1) * 512], in_=ps[pi][:, 0:512])
                        o = P.dma(out[ti * 128:(ti + 1) * 128, :], ob[:], r=[("oost", ti % 2)])
                        fin.append(o.idx)
                P.final_waits = fin
                P.flush(final=True)

        def gdn(dbg=False):
            with ExitStack() as mx:
                gr = sb(mx, "gr", [64, NCH, 8, 2])
                bet = sb(mx, "bet", [64, NCH, 8, 2])
                nbet = sb(mx, "nbet", [64, NCH, 8, 2])
                gcs = sb(mx, "gcs", [64, NCH, 8, 2])
                ngcs = sb(mx, "ngcs", [64, NCH, 8, 2])
                egc = sb(mx, "egc", [64, NCH, 8, 2])
                ks = sb(mx, "ks", [64, NCH, 8, 2])
                kbs = sb(mx, "kbs", [64, NCH, 8, 2])
                etot = sb(mx, "etot", [128, NCH, 8, 2])
                br2 = sb(mx, "br2", [64, 2, 8, 2])
                gng = sb(mx, "gng", [64, 128])
                m2 = sb(mx, "m2", [64, 2, 128])
                am2 = sb(mx, "am2", [64, 2, 128])
                pro = ExitStack()
                wsm_s = sb(pro, "wsm_s", [128, 8, 32])
                wsm_b = sb(pro, "wsm_b", [128, 8, 32], BF16)
                ab = sb(pro, "ab", [64, NCH, 2, 8, 2])
                P.dma(wsm_s[:], w_in_v[:, :, 4096:4128], w=["wsm_s"])
                P.dma(gng[:], rows1[104:105, :].partition_broadcast(64), w=["gng"])
                P.add(POOL, "tensor_copy", r=["wsm_s"], w=["wsm_b"], out=wsm_b[:], in_=wsm_s[:])
                for hh in range(2):
                    P.add(POOL, "tensor_copy", r=["m_le"], w=["m2"], out=m2[:, 0, hh * 64:(hh + 1) * 64], in_=m_le[:])
                    P.add(POOL, "tensor_copy", r=["m_ge"], w=["m2"], out=m2[:, 1, hh * 64:(hh + 1) * 64], in_=m_ge[:])
                P.add(POOL, "tensor_copy", r=["am_f"], w=["am2"], out=am2[:, 0, :], in_=am_f[:])
                P.add(POOL, "tensor_copy", r=["am_b"], w=["am2"], out=am2[:, 1, :], in_=am_b[:])
                P.add(DVE, "tensor_copy", r=["brt"], w=["br2"], out=br2[:, 0, :, :].rearrange("p h d -> p d h"),
                      in_=brt[:, 0:16].rearrange("p (d h) -> p d h", d=2))
                P.add(ACT, "activation", r=["brt"], w=["br2"], out=br2[:, 1, :, :].rearrange("p h d -> p d h"),
                      in_=brt[:, 96:112].rearrange("p (d h) -> p d h", d=2), func=AF.Exp)
                P.add(DVE, "tensor_scalar", r=["br2"], w=["br2"], out=br2[:, 1, :, :], in0=br2[:, 1, :, :], scalar1=-1.0, scalar2=None, op0=ALU.mult)
                PSTOP = int(os.environ.get("KPSTOP", "9"))
                if PSTOP == 0:
                    P.flush(); pro.close(); return
                for c in range(NCH):
                    pi = c % 2
                    for k in range(8):
                        P.add(PE, "matmul", r=["wsm_b"], w=[PSK[pi]], out=ps[pi][0:64, 0:32], lhsT=hT[:, k, c * 64:(c + 1) * 64], rhs=wsm_b[:, k, :],
                              start=(k == 0), stop=(k == 7))
                    P.add(DVE if pi else ACT, "tensor_copy" if pi else "copy", r=[PSK[pi]], w=["ab"],
                          out=ab[:, c, :, :, :].rearrange("p a h d -> p a d h"), in_=ps[pi][0:64, 0:32].rearrange("p (a d h) -> p a d h", a=2, d=2))
                bshape = [64, NCH, 8, 2]
                if PSTOP == 1:
                    P.flush(); pro.close(); return
                P.add(DVE, "tensor_tensor", r=["ab", "br2"], w=["gr"], out=gr[:], in0=ab[:, :, 0, :, :], in1=br2[:, 0, :, :].unsqueeze(1).broadcast_to(bshape), op=ALU.add)
                P.add(ACT, "activation", r=["gr"], w=["gr"], out=gr[:], in_=gr[:], func=AF.Exp)
                P.add(ACT, "activation", r=["gr"], w=["gr"], out=gr[:], in_=gr[:], func=AF.Ln, bias=1.0)
                P.add(DVE, "tensor_tensor", r=["gr", "br2"], w=["gr"], out=gr[:], in0=gr[:], in1=br2[:, 1, :, :].unsqueeze(1).broadcast_to(bshape), op=ALU.mult)
                P.add(ACT, "activation", r=["ab"], w=["bet"], out=bet[:], in_=ab[:, :, 1, :, :], func=AF.Sigmoid)
                P.add(DVE, "tensor_scalar", r=["bet"], w=["nbet"], out=nbet[:], in0=bet[:], scalar1=-1.0, scalar2=None, op0=ALU.mult)
                if PSTOP == 2:
                    P.flush(); pro.close(); return
                for half in range(2):
                    cs = slice(half * 18, half * 18 + 18)
                    rhs = gr[:, cs, :, :].rearrange("p c h d -> p (c h d)")
                    P.add(PE, "matmul", r=["gr", "m_le"], w=[PSK[2]], out=ps[2][0:64, 0:288], lhsT=m_le[:], rhs=rhs, start=True, stop=True)
                    P.add(PE, "matmul", r=["gr", "m_ge"], w=[PSK[3]], out=ps[3][0:64, 0:288], lhsT=m_ge[:], rhs=rhs, start=True, stop=True)
                    P.add(PE, "matmul", r=["gr", "ones"], w=[PSK[4]], out=ps[4][:, 0:288], lhsT=ones_f[0:64, :], rhs=rhs, start=True, stop=True)
                    P.add(DVE, "tensor_copy", r=[PSK[2]], w=["gcs"], out=gcs[:, cs, :, 0], in_=ps[2][0:64, 0:288].rearrange("p (c h d) -> p c h d", h=8, d=2)[:, :, :, 0])
                    P.add(DVE, "tensor_copy", r=[PSK[3]], w=["gcs"], out=gcs[:, cs, :, 1], in_=ps[3][0:64, 0:288].rearrange("p (c h d) -> p c h d", h=8, d=2)[:, :, :, 1])
                    P.add(ACT, "activation", r=[PSK[4]], w=["etot"], out=etot[:, cs, :, :].rearrange("p c h d -> p (c h d)"), in_=ps[4][:, 0:288], func=AF.Exp)
                    P.add(DVE, "tensor_tensor", r=[PSK[4], "gcs"], w=["ks"], out=ks[:, cs, :, :].rearrange("p c h d -> p (c h d)"), in0=ps[4][0:64, 0:288],
                          in1=gcs[:, cs, :, :].rearrange("p c h d -> p (c h d)"), op=ALU.subtract)
                P.add(ACT, "activation", r=["ks"], w=["ks"], out=ks[:], in_=ks[:], func=AF.Exp)
                P.add(ACT, "activation", r=["gcs"], w=["egc"], out=egc[:], in_=gcs[:], func=AF.Exp)
                P.add(DVE, "tensor_scalar", r=["gcs"], w=["ngcs"], out=ngcs[:], in0=gcs[:], scalar1=-1.0, scalar2=None, op0=ALU.mult)
                P.add(DVE, "tensor_tensor", r=["egc", "bet"], w=["kbs"], out=kbs[:], in0=egc[:], in1=bet[:], op=ALU.mult)
                P.flush()
                pro.close()
                GSTOP = int(os.environ.get("KGSTOP", "9"))
                if GSTOP == 0:
                    return

                wst = [sb(mx, "gwst0", [128, 8, 128])] * 2
                wb = [sb(mx, f"gwb{i}", [128, 8, 128], BF16) for i in range(2)]
                feat = sb(mx, "gfeat", [128, 2436 + T])
                cin = feat[:, 0:2436]
                co = feat[:, 2436:2436 + T]
                qT = sb(mx, "gqT", [128, T])
                kT = sb(mx, "gkT", [128, T])
                qTb = sb(mx, "gqTb", [128, T], BF16)
                sqt = [sb(mx, "gsq0", [128, 512])] * 2
                rsn = sb(mx, "grsn", [128, 512])
                k_tok = sb(mx, "gktok", [64, NCH, 128], BF16)
                v_tok = sb(mx, "gvtok", [64, NCH, 128], BF16)
                z_tok = sb(mx, "gztok", [64, 32, 128], BF16)
                o_tok = sb(mx, "gotok", [64, 32, 128])
                u_tok = feat[0:64, 0:4608].bitcast(BF16).rearrange("p (c d f) -> p c d f", c=NCH, d=2)
                wT = sb(mx, "gwT", [128, 2, T], BF16)
                qkd = sb(mx, "gqkd", [64, NCH, 2, 64], BF16)
                ko = [sb(mx, f"gko{i}", [64, 128], BF16) for i in range(4)]
                kq = [sb(mx, f"gkq{i}", [64, 128])[:] for i in range(2)]
                R1 = [sb(mx, f"gR1{i}", [64, 2, 128])[:] for i in range(2)]
                E = [sb(mx, f"gE{i}", [64, 2, 128])[:] for i in range(2)]
                tmpA = [sb(mx, f"gtA{i}", [64, 2, 64])[:] for i in range(2)]
                ABs = [sb(mx, f"gAB{i}", [64, 2, 128])[:] for i in range(4)]
                Xs = [sb(mx, f"gX{i}", [64, 2, 64])[:] for i in range(4)]
                TTb = [sb(mx, f"gTT{i}", [64, 2, 64], BF16)[:] for i in range(2)]
                vb = [sb(mx, f"gvb{i}", [64, 2, 128], BF16)[:] for i in range(2)]
                kbg = [sb(mx, f"gkbg{i}", [64, 2, 128], BF16)[:] for i in range(2)]
                xtra = sb(mx, "gxtra", [64, 640])
                wstf = wst[0][:].rearrange("p k n -> p (k n)")
                wbf = [wb[i][:].rearrange("p k n -> p (k n)").bitcast(F32) for i in range(2)]
                sqf = sqt[0][:]
                rsf = rsn[:]

                def reg(base, off, n, shp=False, bf=False):
                    v = base[0:64, off:off + n]
                    if bf:
                        v = v.bitcast(BF16)
                    if shp:
                        v = v.rearrange("p (d f) -> p d f", d=2)
                    return v
                R1.append(reg(wstf, 0, 256, True)); E.append(reg(wstf, 256, 256, True))
                ABs.append(reg(wstf, 512, 256, True)); ABs.append(reg(wstf, 768, 256, True))
                kq.append(reg(wbf[0], 0, 128)); tmpA.append(reg(wbf[0], 128, 128, True))
                Xs.append(reg(wbf[0], 256, 128, True)); Xs.append(reg(wbf[0], 384, 128, True))
                TTb.append(reg(xtra[:], 0, 64, True, True)); vb.append(reg(xtra[:], 64, 128, True, True)); kbg.append(reg(xtra[:], 192, 128, True, True))
                R1.append(reg(sqf, 0, 256, True)); E.append(reg(sqf, 256, 256, True))
                ABs.append(reg(rsf, 0, 256, True)); ABs.append(reg(rsf, 256, 256, True))
                kq.append(reg(wbf[1], 0, 128)); tmpA.append(reg(wbf[1], 128, 128, True))
                Xs.append(reg(wbf[1], 256, 128, True)); Xs.append(reg(wbf[1], 384, 128, True))
                TTb.append(reg(xtra[:], 320, 64, True, True)); vb.append(reg(xtra[:], 384, 128, True, True)); kbg.append(reg(xtra[:], 512, 128, True, True))
                S = [sb(mx, f"gS{i}", [128, 128]) for i in range(2)]
                Sb = [sb(mx, f"gSb{i}", [128, 128], BF16) for i in range(2)]
                vnew = [sb(mx, f"gvn{i}", [64, 128], BF16) for i in range(4)]
                otmp = [sb(mx, f"got{i}", [64, 128]) for i in range(4)]
                ssq = sb(mx, "gssq", [64, 32])
                yst = [sb(mx, f"gyst{i}", [128, 512], BF16) for i in range(2)]
                psb = ps[7][:].bitcast(BF16)
                print("gdn: sbuf remaining", nc.sbuf_bytes_remaining, flush=True)
                cinL = cin[:, 260:2436].rearrange("p (r c) -> p r c", c=68)
                id64 = ident_f[0:64, 0:64]
                offs4 = [0, 1024, 2048, 3072]

                for h in range(8):
                    P.add(POOL, "memset", w=["cin"], ap=cin, constant=0.0)

                    def project(c4, evac):
                        wbi = c4 % 2
                        P.dma(wst[wbi][:], w_in_v[:, :, offs4[c4] + h * 128:offs4[c4] + (h + 1) * 128], w=["gwst"])
                        P.add(POOL, "tensor_copy", r=["gwst"], w=[("gwb", wbi)], out=wb[wbi][:], in_=wst[wbi][:])
                        for bi, (t0, n) in enumerate(BLKS):
                            pi = bi % 2
                            for k in range(8):
                                P.add(PE, "matmul", r=[("gwb", wbi)], w=[PSK[pi]], out=ps[pi][:, :n], lhsT=wb[wbi][:, k, :], rhs=hT[:, k, t0:t0 + n],
                                      start=(k == 0), stop=(k == 7))
                            evac(bi, t0, n, pi)

                    def evac_cin(bi, t0, n, pi):
                        if t0 == 0:
                            P.add(ACT, "copy", r=[PSK[pi]], w=["cin"], out=cin[:, 2:258], in_=ps[pi][:, 0:256])
                        else:
                            r0 = (t0 - 256) // 64
                            P.add(ACT, "copy", r=[PSK[pi]], w=["cin"], out=cinL[:, r0:r0 + 8, 2:66], in_=v3(ps[pi][:, 0:512], 64))

                    def conv_silu(c4, dst):
                        for kk in range(5):
                            wc = cols2[:, kk * 24 + c4 * 8 + h:kk * 24 + c4 * 8 + h + 1]
                            if kk == 0:
                                P.add(DVE, "tensor_scalar", r=["cin", ("cols", 1)], w=["co_c"], out=co[:, 0:256], in0=cin[:, kk:kk + 256], scalar1=wc, scalar2=None, op0=ALU.mult)
                                P.add(DVE, "tensor_scalar", r=["cin", ("cols", 1)], w=["co_l"], out=v3(co[:, 256:T], 64), in0=cinL[:, :, kk:kk + 64], scalar1=wc, scalar2=None, op0=ALU.mult)
                            else:
                                P.add(DVE, "scalar_tensor_tensor", r=["cin", ("cols", 1), "co_c"], w=["co_c"], out=co[:, 0:256], in0=cin[:, kk:kk + 256], scalar=wc,
                                      in1=co[:, 0:256], op0=ALU.mult, op1=ALU.add)
                                P.add(DVE, "scalar_tensor_tensor", r=["cin", ("cols", 1), "co_l"], w=["co_l"], out=v3(co[:, 256:T], 64), in0=cinL[:, :, kk:kk + 64], scalar=wc,
                                      in1=v3(co[:, 256:T], 64), op0=ALU.mult, op1=ALU.add)
                        P.add(ACT, "activation", r=["co_c", "co_l"], w=[dst[1]], out=dst[0][:, :], in_=co, func=AF.Silu)

                    def l2norm(src, key, scale, extra_bf16=None):
                        for bi, (t0, n) in enumerate(BLKS):
                            pn = 2 + (bi % 2)
                            P.add(ACT, "activation", r=[key], w=["gsq"], out=sqt[bi % 2][:, :n], in_=src[:, t0:t0 + n], func=AF.Square)
                            P.add(PE, "matmul", r=["gsq", "ones"], w=[PSK[pn]], out=ps[pn][:, :n], lhsT=ones_f[:], rhs=sqt[bi % 2][:, :n], start=True, stop=True)
                            P.add(ACT, "activation", r=[PSK[pn]], w=["grsn"], out=rsn[:, :n], in_=ps[pn][:, :n], func=AF.Sqrt, bias=epsc[:, 0:1])
                            P.add(DVE, "reciprocal", r=["grsn"], w=["grsn"], out=rsn[:, :n], in_=rsn[:, :n])
                            P.add(DVE, "scalar_tensor_tensor", r=[key, "grsn"], w=[key], out=src[:, t0:t0 + n], in0=src[:, t0:t0 + n], scalar=scale, in1=rsn[:, :n],
                                  op0=ALU.mult, op1=ALU.mult)
                        if extra_bf16 is not None:
                            P.add(POOL, "tensor_copy", r=[key], w=["gqTb"], out=extra_bf16[:, :], in_=src[:, :])

                    def to_tok(src, key, dst, dkey, c0, silu=False):
                        nch = NCH - c0
                        for g4 in range(nch // 4):
                            pi = g4 % 2
                            for j in range(4):
                                c = c0 + g4 * 4 + j
                                P.add(PE, "transpose", r=[key, "ident_f"], w=[PSK[pi]], out=ps[pi][0:64, j * 128:(j + 1) * 128], in_=src[:, c * 64:(c + 1) * 64], identity=ident_f[:])
                            if silu:
                                P.add(ACT, "activation", r=[PSK[pi]], w=[dkey], out=dst[:, g4 * 4:g4 * 4 + 4, :], in_=v3(ps[pi][0:64, 0:512], 128), func=AF.Silu)
                            else:
                                P.add(DVE if pi else ACT, "tensor_copy" if pi else "copy", r=[PSK[pi]], w=[dkey], out=dst[:, g4 * 4:g4 * 4 + 4, :], in_=v3(ps[pi][0:64, 0:512], 128))

                    project(0, evac_cin); conv_silu(0, (qT, "gqT")); l2norm(qT, "gqT", 128 ** -0.5, qTb)
                    project(1, evac_cin); conv_silu(1, (kT, "gkT")); l2norm(kT, "gkT", 1.0)
                    to_tok(kT, "gkT", k_tok, "gktok", 0)
                    project(2, evac_cin); conv_silu(2, (co, "gco"))
                    to_tok(co, "gco", v_tok, "gvtok", 0)
                    project(3, lambda bi, t0, n, pi: P.add(ACT, "copy", r=[PSK[pi]], w=["gco"], out=co[:, t0:t0 + n], in_=ps[pi][:, :n]))
                    to_tok(co, "gco", z_tok, "gztok", 4, silu=True)
                    P.flush()
                    if GSTOP == 1:
                        return

                    def solve_chunk(c, a):
                        pk = 4 + a
                        pw = 4 + a
                        P.add(PE, "matmul", r=["gkT"], w=[PSK[pk]], out=ps[pk][0:64, 0:64], lhsT=kT[:, c * 64:(c + 1) * 64], rhs=kT[:, c * 64:(c + 1) * 64], start=True, stop=True)
                        P.add(PE, "matmul", r=["gkT", "gqT"], w=[PSK[pk]], out=ps[pk][0:64, 64:128], lhsT=kT[:, c * 64:(c + 1) * 64], rhs=qT[:, c * 64:(c + 1) * 64], start=True, stop=True)
                        P.add(ACT, "copy", r=[PSK[pk]], w=[("gkq", a)], out=kq[a][:], in_=ps[pk][0:64, 0:128])
                        yield
                        for d in range(2):
                            P.add(DVE, "tensor_scalar", r=["m2", "gr"], w=[("gR1", a)], out=R1[a][:, d, :], in0=m2[:, d, :], scalar1=gr[:, c, h, d:d + 1], scalar2=None, op0=ALU.mult)
                        P.add(PE, "matmul", r=[("gR1", a), "ones"], w=[PSK[pk]], out=ps[pk][0:64, 128:384], lhsT=ones_f[0:64, 0:64], rhs=R1[a][:].rearrange("p d f -> p (d f)"), start=True, stop=False)
                        P.add(PE, "matmul", r=["am2", "ident_f"], w=[PSK[pk]], out=ps[pk][0:64, 128:384], lhsT=id64, rhs=am2[:].rearrange("p d f -> p (d f)"), start=False, stop=True)
                        yield
                        for d in range(2):
                            P.add(ACT, "activation", r=[PSK[pk], "ngcs"], w=[("gE", a)], out=E[a][:, d, 0:64], in_=ps[pk][0:64, 128 + d * 128:128 + d * 128 + 64], func=AF.Exp,
                                  bias=ngcs[:, c, h, d:d + 1], scale=1.0)
                            P.add(ACT, "activation", r=[PSK[pk], "gcs"], w=[("gE", a)], out=E[a][:, d, 64:128], in_=ps[pk][0:64, 128 + d * 128 + 64:128 + (d + 1) * 128], func=AF.Exp,
                                  bias=gcs[:, c, h, d:d + 1], scale=-1.0)
                        yield
                        P.add(DVE, "tensor_tensor", r=[("gkq", a), ("gE", a)], w=[("gqkd", c)], out=qkd[:, c, :, :], in0=kq[a][:, 64:128].unsqueeze(1).broadcast_to([64, 2, 64]),
                              in1=E[a][:, :, 0:64], op=ALU.mult)
                        P.add(DVE, "tensor_tensor", r=[("gkq", a), ("gE", a)], w=[("gtA", a)], out=tmpA[a][:], in0=kq[a][:, 0:64].unsqueeze(1).broadcast_to([64, 2, 64]),
                              in1=E[a][:, :, 64:128], op=ALU.mult)
                        ab0 = ABs[2 * a]
                        ab1 = ABs[2 * a + 1]
                        P.add(DVE, "tensor_tensor", r=[("gtA", a), "nbet"], w=[("gAB", 2 * a, "A")], out=ab0[:, :, 0:64], in0=tmpA[a][:],
                              in1=nbet[:, c, h, :].unsqueeze(2).broadcast_to([64, 2, 64]), op=ALU.mult)
                        yield
                        for d in range(2):
                            P.add(PE, "transpose", r=[("gAB", 2 * a, "A"), "ident_f"], w=[PSK[pw]], out=ps[pw][0:64, d * 64:(d + 1) * 64], in_=ab0[:, d, 0:64], identity=id64)
                        yield
                        P.add(ACT, "copy", r=[PSK[pw]], w=[("gAB", 2 * a, "B")], out=ab0[:, :, 64:128], in_=v3(ps[pw][0:64, 0:128], 64))
                        x0 = Xs[2 * a]
                        x1 = Xs[2 * a + 1]
                        P.add(DVE, "tensor_tensor", r=[PSK[pw], "ident_f"], w=[("gX", 2 * a)], out=x0[:], in0=v3(ps[pw][0:64, 0:128], 64),
                              in1=id64.unsqueeze(1).broadcast_to([64, 2, 64]), op=ALU.add)
                        cur_ab, nxt_ab, ci, ni = ab0, ab1, 2 * a, 2 * a + 1
                        cur_x, nxt_x, cxi, nxi = x0, x1, 2 * a, 2 * a + 1
                        for lvl in range(5):
                            last = lvl == 4
                            for d in range(2):
                                P.add(PE, "matmul", r=[("gAB", ci, "A"), ("gAB", ci, "B")], w=[PSK[pw]], out=ps[pw][0:64, 128 + d * 128:128 + d * 128 + 64],
                                      lhsT=cur_ab[:, d, 64:128], rhs=cur_ab[:, d, 0:64], start=True, stop=True)
                                if not last:
                                    P.add(PE, "matmul", r=[("gAB", ci, "A"), ("gAB", ci, "B")], w=[PSK[pw]], out=ps[pw][0:64, 128 + d * 128 + 64:128 + (d + 1) * 128],
                                          lhsT=cur_ab[:, d, 0:64], rhs=cur_ab[:, d, 64:128], start=True, stop=True)
                            yield
                            if last:
                                P.add(ACT, "copy", r=[PSK[pw]], w=[("gAB", ni, "A")], out=nxt_ab[:, :, 0:64], in_=v3(ps[pw][0:64, 128:384], 128)[:, :, 0:64])
                            else:
                                P.add(ACT, "copy", r=[PSK[pw]], w=[("gAB", ni, "A"), ("gAB", ni, "B")], out=nxt_ab[:], in_=v3(ps[pw][0:64, 128:384], 128))
                            yield
                            for d in range(2):
                                P.add(PE, "matmul", r=[("gAB", ni, "A"), ("gX", cxi)], w=[PSK[pw]], out=ps[pw][0:64, 384 + d * 64:384 + (d + 1) * 64],
                                      lhsT=nxt_ab[:, d, 0:64], rhs=cur_x[:, d, :], start=True, stop=True)
                            yield
                            if last:
                                P.add(DVE, "tensor_tensor", r=[PSK[pw], ("gX", cxi)], w=[("gTT", a)], out=TTb[a][:], in0=v3(ps[pw][0:64, 384:512], 64), in1=cur_x[:], op=ALU.add)
                            else:
                                P.add(DVE, "tensor_tensor", r=[PSK[pw], ("gX", cxi)], w=[("gX", nxi)], out=nxt_x[:], in0=v3(ps[pw][0:64, 384:512], 64), in1=cur_x[:], op=ALU.add)
                            cur_ab, nxt_ab, ci, ni = nxt_ab, cur_ab, ni, ci
                            yield
                            cur_x, nxt_x, cxi, nxi = nxt_x, cur_x, nxi, cxi
                        P.add(POOL, "tensor_tensor", r=["gvtok", "bet"], w=[("gvb", a)], out=vb[a][:], in0=v_tok[:, c, :].unsqueeze(1).broadcast_to([64, 2, 128]),
                              in1=bet[:, c, h, :].unsqueeze(2).broadcast_to([64, 2, 128]), op=ALU.mult)
                        P.add(POOL, "tensor_tensor", r=["gktok", "kbs"], w=[("gkbg", a)], out=kbg[a][:], in0=k_tok[:, c, :].unsqueeze(1).broadcast_to([64, 2, 128]),
                              in1=kbs[:, c, h, :].unsqueeze(2).broadcast_to([64, 2, 128]), op=ALU.mult)
                        pu = pk
                        for d in range(2):
                            P.add(PE, "matmul", r=[("gTT", a), ("gvb", a)], w=[PSK[pu]], out=ps[pu][0:64, d * 128:(d + 1) * 128], lhsT=TTb[a][:, d, :], rhs=vb[a][:, d, :], start=True, stop=True)
                            P.add(PE, "matmul", r=[("gTT", a), ("gkbg", a)], w=[PSK[pu]], out=ps[pu][:, 256 + d * 64:256 + (d + 1) * 64], lhsT=kbg[a][:, d, :], rhs=TTb[a][:, d, :], start=True, stop=True)
                        yield
                        P.add(ACT, "copy", r=[PSK[pu]], w=[("gutok", c)], out=u_tok[:, c, :, :], in_=v3(ps[pu][0:64, 0:256], 128))
                        P.add(DVE, "tensor_copy", r=[PSK[pu]], w=[("gwT", c)], out=wT[:, :, c * 64:(c + 1) * 64], in_=v3(ps[pu][:, 256:384], 64))
                    orders = [list(range(NCH)), [3, 2, 1, 0] + list(range(NCH - 1, 3, -1))]
                    for d in range(2):
                        P.add(POOL, "memset", w=[("gS", d)], ap=S[d][:], constant=0.0)
                        P.add(POOL, "memset", w=[("gSb", d)], ap=Sb[d][:], constant=0.0)
                    P.add(POOL, "memset", w=[("gotok", c) for c in range(4, NCH)], ap=o_tok[:], constant=0.0)
                    done = set()

                    def solve_wrap(c, a):
                        yield from solve_chunk(c, a)
                        done.add(c)

                    def scan_gen(d):
                        pb = d
                        for step in range(NCH):
                            c = orders[d][step]
                            while c not in done:
                                yield
                            vi = d * 2 + (step % 2)
                            P.add(PE, "matmul", r=[("gwT", c), ("gSb", d)], w=[PSK[pb]], out=ps[pb][0:64, 0:128], lhsT=wT[:, d, c * 64:(c + 1) * 64], rhs=Sb[d][:], start=True, stop=True)
                            if c >= 4:
                                P.add(PE, "matmul", r=["gqTb", ("gSb", d)], w=[PSK[2 + pb]], out=ps[2 + pb][0:64, 0:128], lhsT=qTb[:, c * 64:(c + 1) * 64], rhs=Sb[d][:], start=True, stop=True)
                            P.add(POOL, "tensor_scalar", r=["gktok", "ks"], w=[("gko", vi)], out=ko[vi][:], in0=k_tok[:, c, :], scalar1=ks[:, c, h, d:d + 1], scalar2=None, op0=ALU.mult)
                            yield
                            P.add(DVE, "tensor_tensor", r=[("gutok", c), PSK[pb]], w=[("gvn", vi)], out=vnew[vi][:], in0=u_tok[:, c, d, :], in1=ps[pb][0:64, 0:128], op=ALU.subtract)
                            yield
                            P.add(PE, "matmul", r=[("gko", vi), ("gvn", vi)], w=[PSK[pb]], out=ps[pb][:, 128:256], lhsT=ko[vi][:], rhs=vnew[vi][:], start=True, stop=True)
                            if c >= 4:
                                P.add(PE, "matmul", r=[("gqkd", c), ("gvn", vi)], w=[PSK[2 + pb]], out=ps[2 + pb][0:64, 128:256], lhsT=qkd[:, c, d, :], rhs=vnew[vi][:], start=True, stop=True)
                            yield
                            P.add(DVE, "scalar_tensor_tensor", r=[("gS", d), PSK[pb], "etot"], w=[("gSb", d)], out=Sb[d][:], in0=S[d][:], scalar=etot[:, c, h, d:d + 1], in1=ps[pb][:, 128:256],
                                  op0=ALU.mult, op1=ALU.add)
                            P.add(DVE, "scalar_tensor_tensor", r=[("gS", d), PSK[pb], "etot"], w=[("gS", d)], out=S[d][:], in0=S[d][:], scalar=etot[:, c, h, d:d + 1], in1=ps[pb][:, 128:256],
                                  op0=ALU.mult, op1=ALU.add)
                            if c >= 4:
                                P.add(ACT, "copy", r=[PSK[2 + pb]], w=[("got", vi)], out=otmp[vi][:], in_=ps[2 + pb][0:64, 128:256])
                                yield
                                P.add(DVE, "scalar_tensor_tensor", r=[PSK[2 + pb], ("got", vi), "egc"], w=[("got", vi)], out=otmp[vi][:], in0=ps[2 + pb][0:64, 0:128],
                                      scalar=egc[:, c, h, d:d + 1], in1=otmp[vi][:], op0=ALU.mult, op1=ALU.add)
                                yield
                                P.add(POOL, "tensor_tensor", r=[("got", vi), ("gotok", c)], w=[("gotok", c)], out=o_tok[:, c - 4, :], in0=o_tok[:, c - 4, :], in1=otmp[vi][:], op=ALU.add)
                            yield

                    sorder = []
                    for i_ in range(NCH):
                        for c_ in (orders[0][i_], orders[1][i_]):
                            if c_ not in sorder:
                                sorder.append(c_)
                    slots = [None, None, None, None]
                    scans = [scan_gen(0), scan_gen(1)]
                    while scans or sorder or any(g_ is not None for g_ in slots):
                        for si in range(4):
                            if slots[si] is None and sorder:
                                slots[si] = solve_wrap(sorder.pop(0), si)
                            if slots[si] is not None:
                                try:
                                    next(slots[si])
                                except StopIteration:
                                    slots[si] = None
                        for g_ in list(scans):
                            try:
                                next(g_)
                            except StopIteration:
                                scans.remove(g_)
                    P.flush()
                    if dbg:
                        P.dma(out.rearrange("(c p) n -> p c n", p=64)[:, :, h * 128:(h + 1) * 128], o_tok[:], r=[("gotok", c) for c in range(4, NCH)])
                        P.flush()
                        continue
                    sq16 = u_tok[:].rearrange("p c d f -> p (c d) f")[:, 0:32, :]
                    ya_tok = k_tok[:, 0:32, :]
                    P.add(DVE, "tensor_tensor", r=[("gotok", c) for c in range(4, NCH)], w=["gutok"], out=sq16, in0=o_tok[:], in1=o_tok[:], op=ALU.mult)
                    P.add(DVE, "tensor_reduce", r=["gutok"], w=["gssq"], out=ssq[:], in_=sq16, axis=AX.X, op=ALU.add)
                    P.add(ACT, "activation", r=["gssq"], w=["gssq"], out=ssq[:], in_=ssq[:], func=AF.Sqrt, scale=1.0 / 128, bias=epsc[0:64, 0:1])
                    P.add(DVE, "reciprocal", r=["gssq"], w=["gssq"], out=ssq[:], in_=ssq[:])
                    P.add(DVE, "tensor_tensor", r=["gssq"] + [("gotok", c) for c in range(4, NCH)], w=[("gotok", c) for c in range(4, NCH)], out=o_tok[:], in0=o_tok[:],
                          in1=ssq[:].unsqueeze(2).broadcast_to([64, 32, 128]), op=ALU.mult)
                    P.add(DVE, "tensor_tensor", r=["gng"] + [("gotok", c) for c in range(4, NCH)], w=[("gotok", c) for c in range(4, NCH)], out=o_tok[:], in0=o_tok[:],
                          in1=gng[:].unsqueeze(1).broadcast_to([64, 32, 128]), op=ALU.mult)
                    P.add(DVE, "tensor_tensor", r=["gztok"] + [("gotok", c) for c in range(4, NCH)], w=["gktok"], out=ya_tok, in0=o_tok[:], in1=z_tok[:], op=ALU.mult)
                    for bi in range(4):
                        for j in range(8):
                            P.add(PE, "transpose", r=["gktok", "ident_b"], w=[PSK[7]], out=psb[:, j * 64:(j + 1) * 64], in_=ya_tok[:, bi * 8 + j, :], identity=ident_b[0:64, 0:64])
                        P.add(ACT, "copy", r=[PSK[7]], w=[("gyst", bi % 2)], out=yst[bi % 2][:], in_=psb[:, 0:512])
                        P.dma(yab[h, :, bi * 512:(bi + 1) * 512], yst[bi % 2][:], r=[("gyst", bi % 2)])
                    P.flush()

        def ssd(dbg=False):
            XBC0 = 6176
            with ExitStack() as mx:
                m2d = sb(mx, "sm2d", [64, 2, 64])
                neg8 = sb(mx, "sneg8", [64, 2, 4, 64])
                Dt = sb(mx, "sDt", [64, 32])
                for d_, m_ in ((0, m_le), (1, m_ge)):
                    P.add(POOL, "tensor_copy", w=["sm2d"], out=m2d[:, d_, :], in_=m_[:])
                P.add(POOL, "tensor_copy", w=["sneg8"], out=neg8[:, 0, :, :], in_=v3(neg4_f[:], 64))
                P.add(POOL, "tensor_copy", w=["sneg8"], out=neg8[:, 1, :, :], in_=v3(neg4_b[:], 64))
                P.add(POOL, "tensor_copy", w=["sDt"], out=Dt[:], in_=brt[:, 192:224])
                BT = sb(mx, "sBT", [128, T], BF16)
                CT = sb(mx, "sCT", [128, T], BF16)
                B_tok = sb(mx, "sBtok", [64, NCH, 128], BF16)
                cbT = sb(mx, "scbT", [64, NCH, 64])
                wst = sb(mx, "swst", [128, 8, 128])
                wb = [sb(mx, f"swb{i}", [128, 8, 128], BF16) for i in range(2)]
                feat = sb(mx, "sfeat", [128, 2436 + T])
                cin = feat[:, 0:2436]
                co = feat[:, 2436:2436 + T]
                cinL = cin[:, 260:2436].rearrange("p (r c) -> p r c", c=68)
                xs_tok = sb(mx, "sxstok", [64, NCH, 256], BF16)
                z_tok = sb(mx, "sztok", [64, 32, 256], BF16)
                y_tok = sb(mx, "sytok", [64, 32, 256])
                wsm_s = sb(mx, "swsm_s", [128, 8, 8])
                wsm_b = sb(mx, "swsm_b", [128, 8, 8], BF16)
                dtr = sb(mx, "sdtr", [64, NCH, 4, 2])
                ar = sb(mx, "sar", [64, NCH, 4, 2])
                acs = sb(mx, "sacs", [64, NCH, 4, 2])
                eacs = sb(mx, "seacs", [64, NCH, 4, 2])
                wsc = sb(mx, "swsc", [64, NCH, 4, 2])
                etot = sb(mx, "setot", [128, NCH, 4, 2])
                bq = sb(mx, "sbq", [64, 2, 4, 2])
                R1 = [sb(mx, f"sR1{i}", [64, 2, 4, 64]) for i in range(4)]
                MT = [sb(mx, f"sMT{i}", [64, 2, 4, 64]) for i in range(4)]
                MTb = [sb(mx, f"sMTb{i}", [64, 2, 4, 64], BF16) for i in range(4)]
                xsdt = [sb(mx, f"sxsdt{i}", [64, 2, 4, 64], BF16) for i in range(4)]
                xsw = [sb(mx, f"sxsw{i}", [64, 256], BF16) for i in range(4)]
                ytmp = [sb(mx, f"sytmp{i}", [64, 256]) for i in range(2)] * 2
                ST = [sb(mx, f"sST{i}", [128, 256]) for i in range(2)]
                STb = [sb(mx, f"sSTb{i}", [128, 256], BF16) for i in range(2)]
                yst = [sb(mx, "syst0", [128, 512], BF16)] * 2
                psb = ps[7][:].bitcast(BF16)
                id64 = ident_f[0:64, 0:64]

                def project(col0, evac, wbi):
                    P.dma(wst[:], w_in_v[:, :, col0:col0 + 128], w=["swst"])
                    P.add(POOL, "tensor_copy", r=["swst"], w=[("swb", wbi)], out=wb[wbi][:], in_=wst[:])
                    for bi, (t0, n) in enumerate(BLKS):
                        pi = bi % 2
                        for k in range(8):
                            P.add(PE, "matmul", r=[("swb", wbi)], w=[PSK[pi]], out=ps[pi][:, :n], lhsT=wb[wbi][:, k, :], rhs=hT[:, k, t0:t0 + n], start=(k == 0), stop=(k == 7))
                        evac(bi, t0, n, pi)

                def evac_cin(bi, t0, n, pi):
                    if t0 == 0:
                        P.add(ACT, "copy", r=[PSK[pi]], w=["cin"], out=cin[:, 2:258], in_=ps[pi][:, 0:256])
                    else:
                        r0 = (t0 - 256) // 64
                        P.add(ACT, "copy", r=[PSK[pi]], w=["cin"], out=cinL[:, r0:r0 + 8, 2:66], in_=v3(ps[pi][:, 0:512], 64))

                def evac_co(bi, t0, n, pi):
                    P.add(ACT, "copy", r=[PSK[pi]], w=["sco"], out=co[:, t0:t0 + n], in_=ps[pi][:, :n])

                def conv_silu(ctile, dst, dkey):
                    for kk in range(5):
                        wc = cols3[:, kk * 20 + ctile:kk * 20 + ctile + 1]
                        if kk == 0:
                            P.add(DVE, "tensor_scalar", r=["cin"], w=["co_c", "sco"], out=co[:, 0:256], in0=cin[:, kk:kk + 256], scalar1=wc, scalar2=None, op0=ALU.mult)
                            P.add(DVE, "tensor_scalar", r=["cin"], w=["co_l", "sco"], out=v3(co[:, 256:T], 64), in0=cinL[:, :, kk:kk + 64], scalar1=wc, scalar2=None, op0=ALU.mult)
                        else:
                            P.add(DVE, "scalar_tensor_tensor", r=["cin", "co_c"], w=["co_c"], out=co[:, 0:256], in0=cin[:, kk:kk + 256], scalar=wc, in1=co[:, 0:256], op0=ALU.mult, op1=ALU.add)
                            P.add(DVE, "scalar_tensor_tensor", r=["cin", "co_l"], w=["co_l"], out=v3(co[:, 256:T], 64), in0=cinL[:, :, kk:kk + 64], scalar=wc, in1=v3(co[:, 256:T], 64),
                                  op0=ALU.mult, op1=ALU.add)
                    P.add(ACT, "activation", r=["co_c", "co_l"], w=[dkey], out=dst, in_=co, func=AF.Silu, bias=cols3[:, 100 + ctile:101 + ctile])

                def to_tok(src, key, dst_fn, dkey, c0, silu=False):
                    nch = NCH - c0
                    for g4 in range(nch // 4):
                        pi = g4 % 2
                        for j in range(4):
                            c = c0 + g4 * 4 + j
                            P.add(PE, "transpose", r=[key, "ident_f"], w=[PSK[pi]], out=ps[pi][0:64, j * 128:(j + 1) * 128], in_=src[:, c * 64:(c + 1) * 64], identity=ident_f[:])
                        if silu:
                            P.add(ACT, "activation", r=[PSK[pi]], w=[dkey], out=dst_fn(g4), in_=v3(ps[pi][0:64, 0:512], 128), func=AF.Silu)
                        else:
                            P.add(DVE if pi else ACT, "tensor_copy" if pi else "copy", r=[PSK[pi]], w=[dkey], out=dst_fn(g4), in_=v3(ps[pi][0:64, 0:512], 128))

                SG = int(os.environ.get("KSSDG", "0"))
                for quad in (range(4 * SG, 4 * SG + 4) if dbg else range(8)):
                    grp = quad // 4
                    P.add(POOL, "memset", w=["cin"], ap=cin, constant=0.0)
                    if quad % 4 == 0:
                        project(XBC0 + 2048 + grp * 128, evac_cin, 0)
                        conv_silu(16 + grp, co, "sco")
                        P.add(POOL, "tensor_copy", r=["sco"], w=["sBT"], out=BT[:], in_=co)
                        to_tok(co, "sco", lambda g4: B_tok[:, g4 * 4:g4 * 4 + 4, :], "sBtok", 0)
                        project(XBC0 + 2304 + grp * 128, evac_cin, 1)
                        conv_silu(18 + grp, CT[:], "sCT")
                        for c in range(NCH):
                            pi = 2 + (c % 2)
                            P.add(PE, "matmul", r=["sBT", "sCT"], w=[PSK[pi]], out=ps[pi][0:64, 0:64], lhsT=BT[:, c * 64:(c + 1) * 64], rhs=CT[:, c * 64:(c + 1) * 64], start=True, stop=True)
                            P.add(DVE if c % 2 else ACT, "tensor_copy" if c % 2 else "copy", r=[PSK[pi]], w=["scbT"], out=cbT[:, c, :], in_=ps[pi][0:64, 0:64])
                        P.flush()
                    for d in range(2):
                        c0_ = 8736 + d * 32 + quad * 4
                        P.dma(wsm_s[:, :, d * 4:(d + 1) * 4], w_in_v[:, :, c0_:c0_ + 4], w=["swsm_s"])
                    P.add(POOL, "tensor_copy", r=["swsm_s"], w=["swsm_b"], out=wsm_b[:], in_=wsm_s[:])
                    for d in range(2):
                        P.add(DVE, "tensor_copy", w=["sbq"], out=bq[:, 0, :, d], in_=brt[:, 32 + d * 32 + quad * 4:32 + d * 32 + quad * 4 + 4])
                        P.add(ACT, "activation", w=["sbq"], out=bq[:, 1, :, d], in_=brt[:, 128 + d * 32 + quad * 4:128 + d * 32 + quad * 4 + 4], func=AF.Exp)
                    P.add(DVE, "tensor_scalar", r=["sbq"], w=["sbq"], out=bq[:, 1, :, :], in0=bq[:, 1, :, :], scalar1=-1.0, scalar2=None, op0=ALU.mult)
                    for c in range(NCH):
                        pi = 2 + (c % 2)
                        for k in range(8):
                            P.add(PE, "matmul", r=["swsm_b"], w=[PSK[pi]], out=ps[pi][0:64, 0:8], lhsT=hT[:, k, c * 64:(c + 1) * 64], rhs=wsm_b[:, k, :], start=(k == 0), stop=(k == 7))
                        P.add(DVE if c % 2 else ACT, "tensor_copy" if c % 2 else "copy", r=[PSK[pi]], w=["sdtr"], out=dtr[:, c, :, :].rearrange("p h d -> p d h"),
                              in_=ps[pi][0:64, 0:8].rearrange("p (d h) -> p d h", d=2))
                    qshape = [64, NCH, 4, 2]
                    P.add(DVE, "tensor_tensor", r=["sdtr", "sbq"], w=["sdtr"], out=dtr[:], in0=dtr[:], in1=bq[:, 0, :, :].unsqueeze(1).broadcast_to(qshape), op=ALU.add)
                    P.add(ACT, "activation", r=["sdtr"], w=["sdtr"], out=dtr[:], in_=dtr[:], func=AF.Exp)
                    P.add(ACT, "activation", r=["sdtr"], w=["sdtr"], out=dtr[:], in_=dtr[:], func=AF.Ln, bias=1.0)
                    P.add(DVE, "tensor_tensor", r=["sdtr", "sbq"], w=["sar"], out=ar[:], in0=dtr[:], in1=bq[:, 1, :, :].unsqueeze(1).broadcast_to(qshape), op=ALU.mult)
                    rhs = ar[:].rearrange("p c h d -> p (c h d)")
                    P.add(PE, "matmul", r=["sar", "m_le"], w=[PSK[4]], out=ps[4][0:64, 0:288], lhsT=m_le[:], rhs=rhs, start=True, stop=True)
                    P.add(PE, "matmul", r=["sar", "m_ge"], w=[PSK[5]], out=ps[5][0:64, 0:288], lhsT=m_ge[:], rhs=rhs, start=True, stop=True)
                    P.add(PE, "matmul", r=["sar", "ones"], w=[PSK[6]], out=ps[6][:, 0:288], lhsT=ones_f[0:64, :], rhs=rhs, start=True, stop=True)
                    P.add(DVE, "tensor_copy", r=[PSK[4]], w=["sacs"], out=acs[:, :, :, 0], in_=ps[4][0:64, 0:288].rearrange("p (c h d) -> p c h d", h=4, d=2)[:, :, :, 0])
                    P.add(DVE, "tensor_copy", r=[PSK[5]], w=["sacs"], out=acs[:, :, :, 1], in_=ps[5][0:64, 0:288].rearrange("p (c h d) -> p c h d", h=4, d=2)[:, :, :, 1])
                    P.add(ACT, "activation", r=[PSK[6]], w=["setot"], out=etot[:].rearrange("p c h d -> p (c h d)"), in_=ps[6][:, 0:288], func=AF.Exp)
                    P.add(DVE, "tensor_tensor", r=[PSK[6], "sacs"], w=["swsc"], out=wsc[:].rearrange("p c h d -> p (c h d)"), in0=ps[6][0:64, 0:288],
                          in1=acs[:].rearrange("p c h d -> p (c h d)"), op=ALU.subtract)
                    P.add(ACT, "activation", r=["swsc"], w=["swsc"], out=wsc[:], in_=wsc[:], func=AF.Exp)
                    P.add(DVE, "tensor_tensor", r=["swsc", "sdtr"], w=["swsc"], out=wsc[:], in0=wsc[:], in1=dtr[:], op=ALU.mult)
                    P.add(ACT, "activation", r=["sacs"], w=["seacs"], out=eacs[:], in_=acs[:], func=AF.Exp)
                    for ft in range(2):
                        tile_i = quad * 2 + ft
                        project(XBC0 + tile_i * 128, evac_cin, ft)
                        conv_silu(tile_i, co, "sco")
                        to_tok(co, "sco", lambda g4, ft=ft: xs_tok[:, g4 * 4:g4 * 4 + 4, ft * 128:(ft + 1) * 128], "sxstok", 0)
                    for ft in range(2):
                        tile_i = quad * 2 + ft
                        project(4128 + tile_i * 128, evac_co, ft)
                        to_tok(co, "sco", lambda g4, ft=ft: z_tok[:, g4 * 4:g4 * 4 + 4, ft * 128:(ft + 1) * 128], "sztok", 4, silu=True)
                    P.flush()
                    def diag_chunk(c, a):
                        pm = 2 * a
                        py = 2 * a + 1
                        P.add(DVE, "tensor_tensor", r=["sm2d", "sar"], w=[("sR1", a)], out=R1[a][:], in0=m2d[:].unsqueeze(2).broadcast_to([64, 2, 4, 64]),
                              in1=ar[:, c, :, :].rearrange("p h d -> p d h").unsqueeze(3).broadcast_to([64, 2, 4, 64]), op=ALU.mult)
                        yield
                        P.add(PE, "matmul", r=[("sR1", a), "ones"], w=[PSK[pm]], out=ps[pm][0:64, 0:512], lhsT=ones_f[0:64, 0:64], rhs=R1[a][:].rearrange("p d h l -> p (d h l)"), start=True, stop=False)
                        P.add(PE, "matmul", r=["sneg8", "ident_f"], w=[PSK[pm]], out=ps[pm][0:64, 0:512], lhsT=id64, rhs=neg8[:].rearrange("p d h l -> p (d h l)"), start=False, stop=True)
                        yield
                        P.add(DVE, "tensor_tensor", r=[PSK[pm], "sacs"], w=[("sMT", a)], out=MT[a][:], in0=ps[pm][0:64, 0:512].rearrange("p (d h l) -> p d h l", d=2, h=4),
                              in1=acs[:, c, :, :].rearrange("p h d -> p d h").unsqueeze(3).broadcast_to([64, 2, 4, 64]), op=ALU.subtract)
                        yield
                        P.add(ACT, "activation", r=[("sMT", a)], w=[("sMT", a)], out=MT[a][:], in_=MT[a][:], func=AF.Exp)
                        yield
                        P.add(DVE, "tensor_tensor", r=[("sMT", a), "scbT"], w=[("sMTb", a)], out=MTb[a][:].rearrange("p d h l -> p (d h) l"), in0=MT[a][:].rearrange("p d h l -> p (d h) l"),
                              in1=cbT[:, c, :].unsqueeze(1).broadcast_to([64, 8, 64]), op=ALU.mult)
                        P.add(POOL, "tensor_tensor", r=["sxstok", "sdtr"], w=[("sxsdt", a)], out=xsdt[a][:], in0=xs_tok[:, c, :].rearrange("p (h q) -> p h q", h=4).unsqueeze(1).broadcast_to([64, 2, 4, 64]),
                              in1=dtr[:, c, :, :].rearrange("p h d -> p d h").unsqueeze(3).broadcast_to([64, 2, 4, 64]), op=ALU.mult)
                        yield
                        for hh in range(4):
                            for d in range(2):
                                P.add(PE, "matmul", r=[("sMTb", a), ("sxsdt", a)], w=[PSK[py]], out=ps[py][0:64, hh * 64:(hh + 1) * 64], lhsT=MTb[a][:, d, hh, :], rhs=xsdt[a][:, d, hh, :],
                                      start=(d == 0), stop=(d == 1))
                        yield
                        P.add(ACT, "copy", r=[PSK[py]], w=[("sytok", c)], out=y_tok[:, c - 4, :], in_=ps[py][0:64, 0:256])
                    G = 4
                    for c0 in range(4, NCH, G):
                        gens = [diag_chunk(c, i) for i, c in enumerate(range(c0, min(c0 + G, NCH)))]
                        while gens:
                            for g_ in list(gens):
                                try:
                                    next(g_)
                                except StopIteration:
                                    gens.remove(g_)
                    P.flush()
                    orders = [list(range(NCH)), [3, 2, 1, 0] + list(range(NCH - 1, 3, -1))]
                    for d in range(2):
                        P.add(POOL, "memset", w=[("sST", d)], ap=ST[d][:], constant=0.0)
                        P.add(POOL, "memset", w=[("sSTb", d)], ap=STb[d][:], constant=0.0)
                    for step in range(NCH):
                        for d in range(2):
                            c = orders[d][step]
                            vi = d * 2 + (step % 2)
                            po, pst = d, 2 + d
                            if c >= 4:
                                P.add(PE, "matmul", r=["sCT", ("sSTb", d)], w=[PSK[po]], out=ps[po][0:64, 0:256], lhsT=CT[:, c * 64:(c + 1) * 64], rhs=STb[d][:], start=True, stop=True)
                                P.add(DVE, "tensor_tensor", r=[PSK[po], "seacs"], w=[("sytmp", vi % 2)], out=ytmp[vi][:].rearrange("p (h q) -> p h q", h=4),
                                      in0=ps[po][0:64, 0:256].rearrange("p (h q) -> p h q", h=4), in1=eacs[:, c, :, d].unsqueeze(2).broadcast_to([64, 4, 64]), op=ALU.mult)
                                P.add(POOL, "tensor_tensor", r=[("sytmp", vi % 2), ("sytok", c)], w=[("sytok", c)], out=y_tok[:, c - 4, :], in0=y_tok[:, c - 4, :], in1=ytmp[vi][:], op=ALU.add)
                            P.add(POOL, "tensor_tensor", r=["sxstok", "swsc"], w=[("sxsw", vi)], out=xsw[vi][:].rearrange("p (h q) -> p h q", h=4), in0=xs_tok[:, c, :].rearrange("p (h q) -> p h q", h=4),
                                  in1=wsc[:, c, :, d].unsqueeze(2).broadcast_to([64, 4, 64]), op=ALU.mult)
                            P.add(PE, "matmul", r=["sBtok", ("sxsw", vi)], w=[PSK[pst]], out=ps[pst][:, 0:256], lhsT=B_tok[:, c, :], rhs=xsw[vi][:], start=True, stop=True)
                            P.add(DVE, "tensor_tensor", r=[("sST", d), "setot"], w=[("sST", d)], out=ST[d][:].rearrange("p (h q) -> p h q", h=4), in0=ST[d][:].rearrange("p (h q) -> p h q", h=4),
                                  in1=etot[:, c, :, d].unsqueeze(2).broadcast_to([128, 4, 64]), op=ALU.mult)
                            P.add(DVE, "tensor_tensor", r=[("sST", d), PSK[pst]], w=[("sST", d)], out=ST[d][:], in0=ST[d][:], in1=ps[pst][:, 0:256], op=ALU.add)
                            P.add(ACT, "copy", r=[("sST", d)], w=[("sSTb", d)], out=STb[d][:], in_=ST[d][:])
                    P.flush()
                    if dbg:
                        P.dma(out.rearrange("(c p) n -> p c n", p=64)[:, :, (quad % 4) * 256:(quad % 4 + 1) * 256], y_tok[:], r=[("sytok", c) for c in range(4, NCH)])
                        P.flush()
                        continue
                    scr = feat[0:64, 0:16 * 256].rearrange("p (c q) -> p c q", q=256)
                    yk = [("sytok", c) for c in range(4, NCH)]
                    for half in range(2):
                        cs = slice(half * 16, half * 16 + 16)
                        P.add(DVE, "tensor_tensor", r=["sxstok", "sDt"], w=["sscr"], out=scr.rearrange("p c (h q) -> p c h q", h=4),
                              in0=xs_tok[:, 4 + half * 16:4 + half * 16 + 16, :].rearrange("p c (h q) -> p c h q", h=4),
                              in1=Dt[:, quad * 4:quad * 4 + 4].unsqueeze(1).unsqueeze(3).broadcast_to([64, 16, 4, 64]), op=ALU.mult)
                        P.add(DVE, "tensor_tensor", r=["sscr"] + yk, w=yk, out=y_tok[:, cs, :], in0=y_tok[:, cs, :], in1=scr, op=ALU.add)
                    P.add(DVE, "tensor_tensor", r=["sztok"] + yk, w=["sztok"], out=z_tok[:], in0=y_tok[:], in1=z_tok[:], op=ALU.mult)
                    it = 0
                    for ft in range(2):
                        for bi in range(4):
                            for j in range(8):
                                P.add(PE, "transpose", r=["sztok", "ident_b"], w=[PSK[7]], out=psb[:, j * 64:(j + 1) * 64], in_=z_tok[:, bi * 8 + j, ft * 128:(ft + 1) * 128], identity=ident_b[0:64, 0:64])
                            P.add(ACT, "copy", r=[PSK[7]], w=[("syst", 0)], out=yst[it % 2][:], in_=psb[:, 0:512])
                            P.dma(yab[8 + quad * 2 + ft, :, bi * 512:(bi + 1) * 512], yst[it % 2][:], r=[("syst", 0)])
                            it += 1
                    P.flush()

        def merge():
            w_bg_v = w_bg.rearrange("(k p) n -> p k n", p=128)
            w_bm_v = w_bm.rearrange("(k p) n -> p k n", p=128)
            w_o_v = w_o.rearrange("(k p) n -> p k n", p=128)
            with ExitStack() as mx:
                print("merge: sbuf remaining", nc.sbuf_bytes_remaining, flush=True)
                yaT = sb(mx, "myaT", [128, 8, 1024], BF16)
                ybT = sb(mx, "mybT", [128, 16, 1024], BF16)
                mrg = sb(mx, "mmrg", [128, 8, 1024], BF16)
                wst = sb(mx, "mwst", [128, 16, 128])
                wgb = [sb(mx, f"mwgb{i}", [128, 8, 128], BF16) for i in range(2)]
                wmb = [sb(mx, "mwmb0", [128, 16, 128], BF16)] * 2
                gab = [sb(mx, f"mgab{i}", [128, 8, 128], BF16) for i in range(2)]
                gbb = [sb(mx, f"mgbb{i}", [128, 8, 128], BF16) for i in range(2)]
                sq = [sb(mx, "msq0", [128, 512])] * 2
                rs = sb(mx, "mrs", [128, 512])
                sg = [sb(mx, f"msg{i}", [128, 512]) for i in range(2)]
                ma = [sb(mx, f"mma{i}", [128, 512]) for i in range(2)]
                xb = [sb(mx, "mxb0", [128, 1024])] * 2
                for H0 in (0, 1024):
                    for t in range(8):
                        P.dma(yaT[:, t, :], yab[t][:, H0:H0 + 1024], w=[("myaT", t)])
                    for t in range(16):
                        P.dma(ybT[:, t, :], yab[8 + t][:, H0:H0 + 1024], w=[("mybT", t)])
                    for g_ in range(2):
                        for bi in range(2):
                            pn = 6 + (bi % 2)
                            tsl = slice(bi * 512, (bi + 1) * 512)
                            for t in range(8):
                                tt = g_ * 8 + t
                                P.add(ACT, "activation", r=[("mybT", tt)], w=[("msq", 0)], out=sq[t % 2][:], in_=ybT[:, tt, tsl], func=AF.Square)
                                P.add(PE, "matmul", r=[("msq", 0), "ones"], w=[PSK[pn]], out=ps[pn][:, :], lhsT=ones_f[:], rhs=sq[t % 2][:], start=(t == 0), stop=(t == 7))
                            P.add(ACT, "activation", r=[PSK[pn]], w=["mrs"], out=rs[:], in_=ps[pn][:, :], func=AF.Sqrt, scale=1.0 / 1024, bias=epsc[:, 0:1])
                            P.add(DVE, "reciprocal", r=["mrs"], w=["mrs"], out=rs[:], in_=rs[:])
                            for t in range(8):
                                tt = g_ * 8 + t
                                P.add(DVE, "scalar_tensor_tensor", r=[("mybT", tt), "mrs"], w=[("mybT", tt)], out=ybT[:, tt, tsl], in0=ybT[:, tt, tsl], scalar=cols1[:, 105 + tt:106 + tt],
                                      in1=rs[:], op0=ALU.mult, op1=ALU.mult)
                    it = 0
                    for d in range(8):
                        b = d % 2
                        dsl = slice(d * 128, (d + 1) * 128)
                        P.dma(wst[:, 0:8, :], w_bg_v[:, :, dsl], w=["mwst"])
                        P.add(POOL, "tensor_copy", r=["mwst"], w=[("mwgb", b)], out=wgb[b][:], in_=wst[:, 0:8, :])
                        P.dma(wst[:, :, :], w_bm_v[:, :, dsl], w=["mwst"])
                        P.add(POOL, "tensor_copy", r=["mwst"], w=[("mwmb", 0)], out=wmb[b][:], in_=wst[:, :, :])
                        P.dma(wst[:, 0:8, :], w_in_v[:, :, 8800 + d * 128:8800 + (d + 1) * 128], w=["mwst"])
                        P.add(POOL, "tensor_copy", r=["mwst"], w=[("mgab", b)], out=gab[b][:], in_=wst[:, 0:8, :])
                        P.dma(wst[:, 0:8, :], w_in_v[:, :, 9824 + d * 128:9824 + (d + 1) * 128], w=["mwst"])
                        P.add(POOL, "tensor_copy", r=["mwst"], w=[("mgbb", b)], out=gbb[b][:], in_=wst[:, 0:8, :])
                        for bi in range(2):
                            tsl = slice(bi * 512, (bi + 1) * 512)
                            hsl = slice(256 + H0 + bi * 512, 256 + H0 + (bi + 1) * 512)
                            q = it % 2
                            it += 1
                            pa, pga, pb_, pgb = q * 4, q * 4 + 1, q * 4 + 2, q * 4 + 3
                            for k in range(8):
                                P.add(PE, "matmul", r=[("mwgb", b), ("myaT", k)], w=[PSK[pa]], out=ps[pa][:, :], lhsT=wgb[b][:, k, :], rhs=yaT[:, k, tsl], start=(k == 0), stop=(k == 7))
                            for k in range(8):
                                P.add(PE, "matmul", r=[("mgab", b)], w=[PSK[pga]], out=ps[pga][:, :], lhsT=gab[b][:, k, :], rhs=hT[:, k, hsl], start=(k == 0), stop=(k == 7))
                            for k in range(16):
                                P.add(PE, "matmul", r=[("mwmb", 0), ("mybT", k)], w=[PSK[pb_]], out=ps[pb_][:, :], lhsT=wmb[b][:, k, :], rhs=ybT[:, k, tsl], start=(k == 0), stop=(k == 15))
                            for k in range(8):
                                P.add(PE, "matmul", r=[("mgbb", b)], w=[PSK[pgb]], out=ps[pgb][:, :], lhsT=gbb[b][:, k, :], rhs=hT[:, k, hsl], start=(k == 0), stop=(k == 7))
                            P.add(ACT, "activation", r=[PSK[pga]], w=[("msg", 0)], out=sg[0][:], in_=ps[pga][:, :], func=AF.Sigmoid)
                            P.add(DVE, "tensor_tensor", r=[("msg", 0), PSK[pa]], w=[("mma", 0)], out=ma[0][:], in0=sg[0][:], in1=ps[pa][:, :], op=ALU.mult)
                            P.add(ACT, "activation", r=[PSK[pgb]], w=[("msg", 1)], out=sg[1][:], in_=ps[pgb][:, :], func=AF.Sigmoid)
                            P.add(DVE, "tensor_tensor", r=[("msg", 1), PSK[pb_]], w=[("mma", 1)], out=ma[1][:], in0=sg[1][:], in1=ps[pb_][:, :], op=ALU.mult)
                            P.add(POOL, "tensor_tensor", r=[("mma", 0), ("mma", 1)], w=[("mmrg", d)], out=mrg[:, d, tsl], in0=ma[0][:], in1=ma[1][:], op=ALU.add)
                    for d in range(8):
                        b = d % 2
                        dsl = slice(d * 128, (d + 1) * 128)
                        P.dma(wst[:, 0:8, :], w_o_v[:, :, dsl], w=["mwst"])
                        P.add(POOL, "tensor_copy", r=["mwst"], w=[("mwgb", b)], out=wgb[b][:], in_=wst[:, 0:8, :])
                        P.dma(xb[b][:], xsp[:, d, H0:H0 + 1024], w=[("mxb", 0)])
                        for bi in range(2):
                            tsl = slice(bi * 512, (bi + 1) * 512)
                            po = (it % 2) * 4
                            it += 1
                            for k in range(8):
                                P.add(PE, "matmul", r=[("mwgb", b)] + [("mmrg", k)], w=[PSK[po]], out=ps[po][:, :], lhsT=wgb[b][:, k, :], rhs=mrg[:, k, tsl], start=(k == 0), stop=(k == 7))
                            P.add(DVE, "scalar_tensor_tensor", r=[PSK[po], ("mxb", 0)], w=[("mxb", 0)], out=xb[b][:, tsl], in0=ps[po][:, :], scalar=mvap(1, 0, 2, d), in1=xb[b][:, tsl],
                                  op0=ALU.mult, op1=ALU.add)
                        P.dma(xsp[:, d, H0:H0 + 1024], xb[b][:], r=[("mxb", 0)])
                    P.flush()

        with ExitStack() as pa:
            xT = sb(pa, "xT", [128, 8, T])
            cur["xT"] = xT
            with ExitStack() as s1:
                xst = [sb(s1, f"xst{i}", [128, D]) for i in range(2)]
                for t in range(18):
                    b = t % 2
                    P.dma(xst[b][:], xin[t * 128:(t + 1) * 128, :], w=[("xst", b)])
                    for half in range(2):
                        pi = 2 * b + half
                        for j in range(4):
                            P.add(PE, "transpose", r=[("xst", b), "ident_f"], w=[PSK[pi]], out=ps[pi][:, j * 128:(j + 1) * 128],
                                  in_=xst[b][:, (half * 4 + j) * 128:(half * 4 + j + 1) * 128], identity=ident_f[:])
                        wk = [("xT", k, (BLKS[0][0] if t < 2 else BLKS[1 + (t - 2) // 4][0])) for k in range(half * 4, half * 4 + 4)]
                        if half:
                            P.add(ACT, "copy", r=[PSK[pi]], w=wk, out=xT[:, half * 4:half * 4 + 4, t * 128:(t + 1) * 128], in_=v3(ps[pi][:, 0:512], 128))
                        else:
                            P.add(DVE, "tensor_copy", r=[PSK[pi]], w=wk, out=xT[:, half * 4:half * 4 + 4, t * 128:(t + 1) * 128], in_=v3(ps[pi][:, 0:512], 128))
                crt = sb(s1, "crt", [16, 128])
                cs2 = sb(s1, "cs2", [128, 8, 2])
                modT = sb(s1, "modT", [128, 72, 2])
                wst = [sb(s1, f"wst{i}", [128, 8, 512]) for i in range(2)]
                P.dma(crt[:], crow, w=["crt"])
                P.add(ACT, "activation", r=["crt"], w=["crt2"], out=crt[:], in_=crt[:], func=AF.Silu)
                P.add(PE, "transpose", r=["crt2", "ident_f"], w=[PSK[4]], out=ps[4][:, 0:16], in_=crt[:], identity=ident_f[0:16, 0:16])
                P.add(DVE, "tensor_copy", r=[PSK[4]], w=["cs2"], out=cs2[:, :, 0], in_=ps[4][:, 0:8])
                P.add(DVE, "tensor_copy", r=[PSK[4]], w=["cs2"], out=cs2[:, :, 1], in_=ps[4][:, 8:16])
                for cg in range(18):
                    b = cg % 2
                    P.dma(wst[b][:], w_ada_v[:, :, cg * 512:(cg + 1) * 512], w=[("wst", b)])
                    for jj in range(4):
                        j = cg * 4 + jj
                        for k in range(8):
                            P.add(PE, "matmul", r=[("wst", b), "cs2"], w=[PSK[5]], out=ps[5][:, 2 * j:2 * j + 2],
                                  lhsT=wst[b][:, k, jj * 128:(jj + 1) * 128], rhs=cs2[:, k, :], start=(k == 0), stop=(k == 7))
                P.add(DVE, "tensor_tensor", r=[PSK[5], ("cols", 0)], w=["modT"], out=modT[:], in0=v3(ps[5][:, 0:144], 2),
                      in1=cols1[:, 0:72].unsqueeze(2).broadcast_to([128, 72, 2]), op=ALU.add)
                for s in range(3):
                    for w in range(2):
                        i0 = ((s * 2 + w) * 3) * 8
                        P.add(DVE, "scalar_tensor_tensor", r=["modT", ("cols", 0)], w=["mv"], out=mv[:, i0:i0 + 8],
                              in0=modT[:, (3 * s + 1) * 8:(3 * s + 2) * 8, w], scalar=1.0, in1=cols1[:, 72 + s * 8:72 + s * 8 + 8],
                              op0=ALU.add, op1=ALU.mult)
                        P.add(DVE, "tensor_copy", r=["modT"], w=["mv"], out=mv[:, i0 + 8:i0 + 16], in_=modT[:, (3 * s) * 8:(3 * s + 1) * 8, w])
                        P.add(DVE, "tensor_scalar", r=["modT"], w=["mv"], out=mv[:, i0 + 16:i0 + 24],
                              in0=modT[:, (3 * s + 2) * 8:(3 * s + 3) * 8, w], scalar1=(1.0 if s == 1 else 0.5), scalar2=None, op0=ALU.mult)
                P.flush()
            if STAGE >= 1 and not os.environ.get("KSKIPFFN"):
                rmsnorm_mod(0, BLKS)
                ffn(0, 0, BLKS)
            if STAGE <= 1:
                write_out(False)
                return nc
            rmsnorm_mod(1, BLKS)
            for k in range(8):
                if not os.environ.get("KNOSPILL"):
                    P.dma(xsp[:, k, :], xT[:, k, 256:T], r=[("xT", k, t0) for (t0, n) in BLKS])
            P.flush()
        if STAGE != 3:
            gdn(dbg=(STAGE == 2))
        if STAGE >= 3:
            ssd(dbg=(STAGE == 3))
        if STAGE in (2, 3):
            P.final_waits = [i for i in P.dma_last.values()]
            P.add(POOL, "memset", w=["zeros"], ap=zeros_f[:], constant=0.0)
            P.flush(final=True)
            return nc
        merge()
        with ExitStack() as pc:
            xT = sb(pc, "xT2", [128, 8, T])
            cur["xT"] = xT
            for k in range(8):
                P.dma(xT[:, k, 256:T], xsp[:, k, :], w=[("xT", k, t0) for (t0, n) in LBLKS])
            if STAGE == 4:
                write_out(False)
                return nc
            rmsnorm_mod(2, LBLKS)
            ffn(2, 1, LBLKS)
            write_out(True)
        return nc


def prep_inputs(inputs, b):
    f = lambda a: np.ascontiguousarray(np.asarray(a, dtype=np.float32))
    m = {}
    m["xin"] = f(np.concatenate([inputs["ctx"][b], inputs["x"][b]], axis=0))
    m["crow"] = f(np.concatenate([np.asarray(inputs["c"][b]).reshape(8, 128), np.asarray(inputs["c_ctx"]).reshape(8, 128)], 0))
    r1 = np.zeros((128, 128), np.float32)
    r1[0:72] = np.asarray(inputs["b_ada"][0]).reshape(72, 128)
    r1[72:96] = np.asarray(inputs["norm_g"][0]).reshape(24, 128)
    r1[96:104] = np.asarray(inputs["final_g"]).reshape(8, 128)
    r1[104:105] = np.asarray(inputs["gdn_norm_g"][0]).reshape(1, 128)
    r1[105:121] = np.asarray(inputs["mb_norm_g"][0]).reshape(16, 128)
    m["rows1"] = r1
    r2 = np.zeros((128, 128), np.float32)
    r2[0:120] = np.asarray(inputs["gdn_conv_w"][0]).reshape(120, 128)
    m["rows2"] = r2
    r3 = np.zeros((128, 128), np.float32)
    r3[0:100] = np.asarray(inputs["mb_conv_w"][0]).reshape(100, 128)
    r3[100:120] = np.asarray(inputs["mb_conv_b"][0]).reshape(20, 128)
    m["rows3"] = r3
    br = np.zeros((1, 256), np.float32)
    br[0, 0:16] = np.asarray(inputs["gdn_dt_bias"][0]).reshape(16)
    br[0, 32:96] = np.asarray(inputs["mb_dt_bias"][0]).reshape(64)
    br[0, 96:112] = np.asarray(inputs["gdn_A_log"][0]).reshape(16)
    br[0, 128:192] = np.asarray(inputs["mb_A_log"][0]).reshape(64)
    br[0, 192:224] = np.asarray(inputs["mb_D"][0]).reshape(32)
    m["brow"] = br
    m["w_ada"] = f(inputs["w_ada"][0])
    m["w_gu"] = f(inputs["ffn_w_gu"][0])
    m["w_dn"] = f(inputs["ffn_w_down"][0])
    m["w_in"] = f(inputs["w_in"][0])
    m["w_bg"] = f(inputs["w_branch_gdn"][0])
    m["w_bm"] = f(inputs["w_branch_mb"][0])
    m["w_o"] = f(inputs["w_out"][0])
    return m


def kernel(**inputs):
    nc = build_program()
    shared = None
    in_maps = []
    for b in range(8):
        m = prep_inputs(inputs, b)
        if shared is None:
            shared = {k: m[k] for k in ("w_ada", "w_gu", "w_dn", "w_in", "w_bg", "w_bm", "w_o", "rows1", "rows2", "rows3", "brow")}
        else:
            m.update(shared)
        in_maps.append(m)
    res = run_bass_kernel_spmd(nc, in_maps, core_ids=list(range(8)))
    return np.stack([np.asarray(r["out"], dtype=np.float32) for r in res.results], axis=0)
```

```python
import os
from contextlib import ExitStack
import numpy as np
import concourse.bass as bass
import concourse.mybir as mybir
from concourse.bass_utils import run_bass_kernel_spmd

F32 = mybir.dt.float32
BF16 = mybir.dt.bfloat16
AF = mybir.ActivationFunctionType
ALU = mybir.AluOpType
AX = mybir.AxisListType

PE, ACT, DVE, POOL, SP = "pe", "act", "dve", "pool", "sp"
ENGS = [PE, ACT, DVE, POOL, SP]
EPOCH = 24000
NEPOCH = 4
NDMASEM = 12

D = 1024
T = 2304
NCH = 36
DFF = 2816
DIN = 10848
EPS = 1e-6
BLKS = [(0, 256), (256, 512), (768, 512), (1280, 512), (1792, 512)]
LBLKS = BLKS[1:]
STAGE = int(os.environ.get("KSTAGE", "99"))


class Op:
    __slots__ = ("eng", "fn", "deps", "sig", "is_dma", "idx", "signals", "q")


class Prog:
    def __init__(self, nc, stack):
        self.nc = nc
        self.ops = []
        self.pending = []
        self.lastw = {}
        self.readers = {}
        self.dma_count = 0
        self.dma_last = {}
        self.cnt = {e: 0 for e in ENGS}
        self.dcnt = [0] * NDMASEM
        self.sems = {}
        for e in ENGS:
            for ep in range(NEPOCH):
                self.sems[(e, ep)] = stack.enter_context(nc.semaphore(f"s_{e}_{ep}"))
        self.dsems = [stack.enter_context(nc.semaphore(f"s_dma_{i}")) for i in range(NDMASEM)]
        self.seen = {e: {} for e in ENGS}
        self.barrier = []
        self.lastop = {}
        self.got_barrier = set()
        self.final_waits = []

    def add(self, eng, name, r=(), w=(), dma=False, **kw):
        o = Op()
        o.eng = eng
        o.is_dma = dma
        o.sig = None
        o.signals = False
        o.q = None
        o.fn = lambda e: getattr(e, name)(**kw)
        o.idx = len(self.ops)
        deps = set()
        pr = [k for k in r if isinstance(k, tuple) and k and k[0] == "ps"]
        if pr:
            r = [k for k in r if k not in pr]
            w = list(w) + pr
        for k in r:
            lw = self.lastw.get(k)
            if lw is not None:
                deps.add(lw)
        for k in w:
            lw = self.lastw.get(k)
            if lw is not None:
                deps.add(lw)
            for rd in self.readers.get(k, ()):
                deps.add(rd)
        for k in r:
            lst = self.readers.setdefault(k, [])
            if not dma:
                lst[:] = [i for i in lst if self.ops[i].eng != eng or self.ops[i].is_dma]
            lst.append(o.idx)
        for k in w:
            self.lastw[k] = o.idx
            self.readers[k] = []
        if dma:
            slot = self.dma_count % NDMASEM
            o.q = slot
            prev = self.dma_last.get(slot)
            if prev is not None:
                deps.add(prev)
            self.dma_last[slot] = o.idx
            self.dma_count += 1
        if eng not in self.got_barrier:
            self.got_barrier.add(eng)
            deps.update(self.barrier)
        deps.discard(o.idx)
        o.deps = deps
        self.ops.append(o)
        self.pending.append(o)
        if not dma:
            self.lastop[eng] = o.idx
        return o

    def dma(self, out, in_, r=(), w=(), eng=SP):
        return self.add(eng, "dma_start", r=r, w=w, dma=True, out=out, in_=in_)

    def sem_of(self, sig):
        if sig[0] == "d":
            return ("d", sig[1]), self.dsems[sig[1]], sig[2]
        return ("c", sig[1], sig[2]), self.sems[(sig[1], sig[2])], sig[3]

    def flush(self, final=False):
        nc = self.nc
        ops = self.ops
        pend = self.pending
        self.pending = []
        nb = [i for i in self.lastop.values()] + [i for i in self.dma_last.values()]
        for i in nb:
            ops[i].signals = True
        for o in pend:
            for d in o.deps:
                od = ops[d]
                if od.eng == PE and o.eng == PE and not od.is_dma and not o.is_dma:
                    continue
                if od.sig is None:
                    od.signals = True
        for o in pend:
            if o.is_dma:
                self.dcnt[o.q] += 16
                o.sig = ("d", o.q, self.dcnt[o.q])
            elif o.signals:
                c = self.cnt[o.eng]
                assert c // EPOCH < NEPOCH
                o.sig = ("c", o.eng, c // EPOCH, (c % EPOCH) + 1)
                self.cnt[o.eng] = c + 1
        per = {e: [o for o in pend if o.eng == e] for e in ENGS}
        finals = list(self.final_waits) if final else []

        def run(engobj, ename):
            seen = self.seen[ename]
            for o in per[ename]:
                waits = {}
                for d in o.deps:
                    od = ops[d]
                    if od.eng == PE and o.eng == PE and not od.is_dma and not o.is_dma:
                        continue
                    if od.sig is None:
                        raise AssertionError((od.eng, o.eng, d, o.idx))
                    key, sh, val = self.sem_of(od.sig)
                    if seen.get(key, 0) >= val:
                        continue
                    if key not in waits or waits[key][1] < val:
                        waits[key] = (sh, val)
                for key, (sh, val) in waits.items():
                    engobj.wait_ge(sh, val)
                    seen[key] = val
                ins = o.fn(engobj)
                if o.sig is not None:
                    key, sh, val = self.sem_of(o.sig)
                    ins.then_inc(sh, 16 if o.is_dma else 1)
            if ename == SP:
                for i in finals:
                    key, sh, val = self.sem_of(ops[i].sig)
                    engobj.wait_ge(sh, val)

        with nc.Block() as block:
            @block.tensor
            def _(e):
                run(e, PE)

            @block.scalar
            def _(e):
                run(e, ACT)

            @block.vector
            def _(e):
                run(e, DVE)

            @block.gpsimd
            def _(e):
                run(e, POOL)

            @block.sync
            def _(e):
                run(e, SP)
        self.barrier = nb
        self.got_barrier = set()
        self.lastw = {}
        self.readers = {}


def v3(ap, b):
    return ap.rearrange("p (a b) -> p a b", b=b)


def build_program():
    nc = bass.Bass("TRN2", target_bir_lowering=False)
    xin = nc.dram_tensor("xin", [T, D], F32, kind="ExternalInput").ap()
    crow = nc.dram_tensor("crow", [16, 128], F32, kind="ExternalInput").ap()
    rows1 = nc.dram_tensor("rows1", [128, 128], F32, kind="ExternalInput").ap()
    rows2 = nc.dram_tensor("rows2", [128, 128], F32, kind="ExternalInput").ap()
    rows3 = nc.dram_tensor("rows3", [128, 128], F32, kind="ExternalInput").ap()
    brow = nc.dram_tensor("brow", [1, 256], F32, kind="ExternalInput").ap()
    w_ada = nc.dram_tensor("w_ada", [D, 9 * D], F32, kind="ExternalInput").ap()
    w_gu = nc.dram_tensor("w_gu", [2, D, 2 * DFF], F32, kind="ExternalInput").ap()
    w_dn = nc.dram_tensor("w_dn", [2, DFF, D], F32, kind="ExternalInput").ap()
    w_in = nc.dram_tensor("w_in", [D, DIN], F32, kind="ExternalInput").ap()
    w_bg = nc.dram_tensor("w_bg", [D, D], F32, kind="ExternalInput").ap()
    w_bm = nc.dram_tensor("w_bm", [2 * D, D], F32, kind="ExternalInput").ap()
    w_o = nc.dram_tensor("w_o", [D, D], F32, kind="ExternalInput").ap()
    out = nc.dram_tensor("out", [2048, D], F32, kind="ExternalOutput").ap()
    xsp = nc.dram_tensor("xsp", [128, 8, 2048], F32, kind="Internal").ap()
    yab = nc.dram_tensor("yab", [24, 128, 2048], BF16, kind="Internal").ap()

    w_ada_v = w_ada.rearrange("(k p) n -> p k n", p=128)
    w_in_v = w_in.rearrange("(k p) n -> p k n", p=128)

    with ExitStack() as g:
        uniq = [0]

        def sb(st, name, shape, dt=F32):
            uniq[0] += 1
            return st.enter_context(nc.sbuf_tensor(f"{name}_{uniq[0]}", shape, dt))

        P = Prog(nc, g)
        ps = [g.enter_context(nc.psum_tensor(f"ps{i}", [128, 512], F32)) for i in range(8)]
        PSK = [("ps", i) for i in range(8)]

        ident_f = sb(g, "ident_f", [128, 128])
        ident_b = sb(g, "ident_b", [128, 128], BF16)
        ones_f = sb(g, "ones_f", [128, 128])
        zeros_f = sb(g, "zeros_f", [64, 256])
        negones = sb(g, "negones", [64, 64])
        m_le = sb(g, "m_le", [64, 64])
        m_ge = sb(g, "m_ge", [64, 64])
        am_f = sb(g, "am_f", [64, 128])
        am_b = sb(g, "am_b", [64, 128])
        neg4_f = sb(g, "neg4_f", [64, 256])
        neg4_b = sb(g, "neg4_b", [64, 256])
        cols1 = sb(g, "cols1", [128, 128])
        cols2 = sb(g, "cols2", [128, 128])
        cols3 = sb(g, "cols3", [128, 128])
        mv = sb(g, "mv", [128, 144])
        hT = sb(g, "hT", [128, 8, T], BF16)
        brt = sb(g, "brt", [64, 256])
        epsc = sb(g, "epsc", [128, 1])

        def mvap(s, w, kind, k):
            i = ((s * 2 + w) * 3 + kind) * 8 + k
            return mv[:, i:i + 1]

        P.add(POOL, "memset", w=["ones"], ap=ones_f[:], constant=1.0)
        P.add(POOL, "memset", w=["zeros"], ap=zeros_f[:], constant=0.0)
        P.add(POOL, "memset", w=["epsc"], ap=epsc[:], constant=EPS)
        P.add(POOL, "memset", w=["negones"], ap=negones[:], constant=-1.0)
        P.add(POOL, "affine_select", r=["ones"], w=["ident_f"], out=ident_f[:], in_=ones_f[:], pattern=[[-1, 128]],
              compare_op=ALU.is_equal, fill=0.0, base=0, channel_multiplier=1)
        P.add(DVE, "tensor_copy", r=["ident_f"], w=["ident_b"], out=ident_b[:], in_=ident_f[:])
        P.add(POOL, "affine_select", r=["ones"], w=["m_le"], out=m_le[:], in_=ones_f[0:64, 0:64], pattern=[[1, 64]],
              compare_op=ALU.is_ge, fill=0.0, base=0, channel_multiplier=-1)
        P.add(POOL, "affine_select", r=["ones"], w=["m_ge"], out=m_ge[:], in_=ones_f[0:64, 0:64], pattern=[[-1, 64]],
              compare_op=ALU.is_ge, fill=0.0, base=0, channel_multiplier=1)
        P.add(POOL, "affine_select", r=["zeros"], w=["am_f"], out=am_f[:, 0:64], in_=zeros_f[:, 0:64], pattern=[[1, 64]],
              compare_op=ALU.is_ge, fill=-30000.0, base=0, channel_multiplier=-1)
        P.add(POOL, "affine_select", r=["zeros"], w=["am_f"], out=am_f[:, 64:128], in_=zeros_f[:, 0:64], pattern=[[-1, 64]],
              compare_op=ALU.is_gt, fill=30000.0, base=0, channel_multiplier=1)
        P.add(POOL, "affine_select", r=["zeros"], w=["am_b"], out=am_b[:, 0:64], in_=zeros_f[:, 0:64], pattern=[[-1, 64]],
              compare_op=ALU.is_ge, fill=-30000.0, base=0, channel_multiplier=1)
        P.add(POOL, "affine_select", r=["zeros"], w=["am_b"], out=am_b[:, 64:128], in_=zeros_f[:, 0:64], pattern=[[1, 64]],
              compare_op=ALU.is_gt, fill=30000.0, base=0, channel_multiplier=-1)
        P.add(POOL, "affine_select", r=["zeros"], w=["neg4"], out=v3(neg4_f[:], 64), in_=v3(zeros_f[:], 64),
              pattern=[[0, 4], [1, 64]], compare_op=ALU.is_ge, fill=-30000.0, base=0, channel_multiplier=-1)
        P.add(POOL, "affine_select", r=["zeros"], w=["neg4"], out=v3(neg4_b[:], 64), in_=v3(zeros_f[:], 64),
              pattern=[[0, 4], [-1, 64]], compare_op=ALU.is_ge, fill=-30000.0, base=0, channel_multiplier=1)
        rst = sb(g, "rst", [128, 128])
        for i, (rw, cl) in enumerate(((rows1, cols1), (rows2, cols2), (rows3, cols3))):
            P.dma(rst[:], rw, w=["rst"])
            P.add(PE, "transpose", r=["rst", "ident_f"], w=[PSK[0]], out=ps[0][:, 0:128], in_=rst[:], identity=ident_f[:])
            P.add(DVE, "tensor_copy", r=[PSK[0]], w=[("cols", i)], out=cl[:], in_=ps[0][:, 0:128])
        P.dma(brt[:], brow.partition_broadcast(64), w=["brt"])
        P.flush()

        cur = {}

        def rmsnorm_mod(s, blks):
            xT = cur["xT"]
            with ExitStack() as st:
                sq = [sb(st, f"nsq{i}", [128, 512]) for i in range(2)]
                rs = sb(st, "nrs", [128, 512])
                tmp = [sb(st, f"ntmp{i}", [128, 512]) for i in range(2)]
                for bi, (t0, n) in enumerate(blks):
                    w = 1 if t0 == 0 else 0
                    pn = 6 + (bi % 2)
                    for k in range(8):
                        P.add(ACT, "activation", r=[("xT", k, t0)], w=[("nsq", k % 2)], out=sq[k % 2][:, :n], in_=xT[:, k, t0:t0 + n], func=AF.Square)
                        P.add(PE, "matmul", r=[("nsq", k % 2), "ones"], w=[PSK[pn]], out=ps[pn][:, :n], lhsT=ones_f[:], rhs=sq[k % 2][:, :n],
                              start=(k == 0), stop=(k == 7))
                    P.add(ACT, "activation", r=[PSK[pn]], w=["nrs"], out=rs[:, :n], in_=ps[pn][:, :n], func=AF.Sqrt, scale=1.0 / D, bias=epsc[:, 0:1])
                    P.add(DVE, "reciprocal", r=["nrs"], w=["nrs"], out=rs[:, :n], in_=rs[:, :n])
                    for k in range(8):
                        P.add(DVE, "tensor_tensor", r=[("xT", k, t0), "nrs"], w=[("ntmp", k % 2)], out=tmp[k % 2][:, :n], in0=xT[:, k, t0:t0 + n],
                              in1=rs[:, :n], op=ALU.mult)
                        P.add(ACT, "activation", r=[("ntmp", k % 2), "mv"], w=[("hT", k, t0)], out=hT[:, k, t0:t0 + n], in_=tmp[k % 2][:, :n],
                              func=AF.Identity, scale=mvap(s, w, 0, k), bias=mvap(s, w, 1, k))
                P.flush()

        def ffn(s, li, blks):
            xT = cur["xT"]
            wgu_v = w_gu[li].rearrange("(k p) n -> p k n", p=128)
            groups = [list(range(0, 6)), list(range(6, 12)), list(range(12, 17)), list(range(17, 22))]
            with ExitStack() as st:
                act = sb(st, "fact", [128, 6, T], BF16)
                wgs = [sb(st, f"fwgs{i}", [128, 8, 256]) for i in range(2)]
                wgb = [sb(st, f"fwgb{i}", [128, 8, 256], BF16) for i in range(2)]
                wds = [sb(st, f"fwds{i}", [128, D]) for i in range(2)]
                wdb = sb(st, "fwdb", [128, 6, D], BF16)
                sl = [sb(st, f"fsl{i}", [128, 512]) for i in range(2)]
                it = 0
                for grp in groups:
                    for jj, j in enumerate(grp):
                        b = j % 2
                        P.dma(wgs[b][:, :, 0:128], wgu_v[:, :, j * 128:(j + 1) * 128], w=[("fwgs", b)])
                        P.dma(wgs[b][:, :, 128:256], wgu_v[:, :, DFF + j * 128:DFF + (j + 1) * 128], w=[("fwgs", b)])
                        P.add(POOL, "tensor_copy", r=[("fwgs", b)], w=[("fwgb", b)], out=wgb[b][:], in_=wgs[b][:])
                        for (t0, n) in blks:
                            q = it % 2
                            it += 1
                            pg, pu = q * 2, q * 2 + 1
                            for k in range(8):
                                P.add(PE, "matmul", r=[("fwgb", b), ("hT", k, t0)], w=[PSK[pg]], out=ps[pg][:, :n], lhsT=wgb[b][:, k, 0:128],
                                      rhs=hT[:, k, t0:t0 + n], start=(k == 0), stop=(k == 7))
                            for k in range(8):
                                P.add(PE, "matmul", r=[("fwgb", b), ("hT", k, t0)], w=[PSK[pu]], out=ps[pu][:, :n], lhsT=wgb[b][:, k, 128:256],
                                      rhs=hT[:, k, t0:t0 + n], start=(k == 0), stop=(k == 7))
                            P.add(ACT, "activation", r=[PSK[pg]], w=[("fsl", q)], out=sl[q][:, :n], in_=ps[pg][:, :n], func=AF.Silu)
                            P.add(DVE, "tensor_tensor", r=[("fsl", q), PSK[pu]], w=[("fact", jj, t0)], out=act[:, jj, t0:t0 + n], in0=sl[q][:, :n],
                                  in1=ps[pu][:, :n], op=ALU.mult)
                    for jj, j in enumerate(grp):
                        b = j % 2
                        P.dma(wds[b][:], w_dn[li, j * 128:(j + 1) * 128, :], w=[("fwds", b)])
                        P.add(POOL, "tensor_copy", r=[("fwds", b)], w=[("fwdb", jj)], out=wdb[:, jj, :], in_=wds[b][:])
                    ng = len(grp)
                    for d in range(8):
                        for (t0, n) in blks:
                            w = 1 if t0 == 0 else 0
                            q = 4 + (it % 2)
                            it += 1
                            for jj in range(ng):
                                P.add(PE, "matmul", r=[("fwdb", jj), ("fact", jj, t0)], w=[PSK[q]], out=ps[q][:, :n],
                                      lhsT=wdb[:, jj, d * 128:(d + 1) * 128], rhs=act[:, jj, t0:t0 + n], start=(jj == 0), stop=(jj == ng - 1))
                            P.add(DVE, "scalar_tensor_tensor", r=[PSK[q], ("xT", d, t0), "mv"], w=[("xT", d, t0)], out=xT[:, d, t0:t0 + n],
                                  in0=ps[q][:, :n], scalar=mvap(s, w, 2, d), in1=xT[:, d, t0:t0 + n], op0=ALU.mult, op1=ALU.add)
                P.flush()

        def write_out(final_norm):
            xT = cur["xT"]
            with ExitStack() as st:
                sq = [sb(st, f"osq{i}", [128, 512]) for i in range(2)]
                rs = sb(st, "ors", [128, 512])
                yt = [sb(st, f"oyt{i}", [128, 8, 512]) for i in range(2)]
                ost = [sb(st, f"oost{i}", [128, D]) for i in range(2)]
                fin = []
                for bi, (t0, n) in enumerate(LBLKS):
                    yb_ = yt[bi % 2]
                    if final_norm:
                        pn = 6 + (bi % 2)
                        for k in range(8):
                            P.add(ACT, "activation", r=[("xT", k, t0)], w=[("osq", k % 2)], out=sq[k % 2][:, :n], in_=xT[:, k, t0:t0 + n], func=AF.Square)
                            P.add(PE, "matmul", r=[("osq", k % 2), "ones"], w=[PSK[pn]], out=ps[pn][:, :n], lhsT=ones_f[:], rhs=sq[k % 2][:, :n],
                                  start=(k == 0), stop=(k == 7))
                        P.add(ACT, "activation", r=[PSK[pn]], w=["ors"], out=rs[:, :n], in_=ps[pn][:, :n], func=AF.Sqrt, scale=1.0 / D, bias=epsc[:, 0:1])
                        P.add(DVE, "reciprocal", r=["ors"], w=["ors"], out=rs[:, :n], in_=rs[:, :n])
                        for k in range(8):
                            P.add(DVE, "scalar_tensor_tensor", r=[("xT", k, t0), "ors", ("cols", 0)], w=[("oyt", bi % 2, k)], out=yb_[:, k, :n],
                                  in0=xT[:, k, t0:t0 + n], scalar=cols1[:, 96 + k:97 + k], in1=rs[:, :n], op0=ALU.mult, op1=ALU.mult)
                    else:
                        for k in range(8):
                            P.add(DVE if k % 2 else POOL, "tensor_copy", r=[("xT", k, t0)], w=[("oyt", bi % 2, k)], out=yb_[:, k, :n], in_=xT[:, k, t0:t0 + n])
                    for tt in range(4):
                        ti = bi * 4 + tt
                        ob = ost[ti % 2]
                        for half in range(2):
                            pi = 2 * (ti % 2) + half
                            for j in range(4):
                                k = half * 4 + j
                                P.add(PE, "transpose", r=[("oyt", bi % 2, k), "ident_f"], w=[PSK[pi]], out=ps[pi][:, j * 128:(j + 1) * 128],
                                      in_=yb_[:, k, tt * 128:(tt + 1) * 128], identity=ident_f[:])
                            if half:
                                P.add(ACT, "copy", r=[PSK[pi]], w=[("oost", ti % 2)], out=ob[:, half * 512:(half + 1) * 512], in_=ps[pi][:, 0:512])
                            else:
                                P.add(DVE, "tensor_copy", r=[PSK[pi]], w=[("oost", ti % 2)], out=ob[:, half * 512:(half + 1) * 512], in_=ps[pi][:, 0:512])
                        o = P.dma(out[ti * 128:(ti + 1) * 128, :], ob[:], r=[("oost", ti % 2)])
                        fin.append(o.idx)
                P.final_waits = fin
                P.flush(final=True)

        def gdn(dbg=False):
            with ExitStack() as mx:
                gr = sb(mx, "gr", [64, NCH, 8, 2])
                bet = sb(mx, "bet", [64, NCH, 8, 2])
                nbet = sb(mx, "nbet", [64, NCH, 8, 2])
                gcs = sb(mx, "gcs", [64, NCH, 8, 2])
                ngcs = sb(mx, "ngcs", [64, NCH, 8, 2])
                egc = sb(mx, "egc", [64, NCH, 8, 2])
                ks = sb(mx, "ks", [64, NCH, 8, 2])
                kbs = sb(mx, "kbs", [64, NCH, 8, 2])
                etot = sb(mx, "etot", [128, NCH, 8, 2])
                br2 = sb(mx, "br2", [64, 2, 8, 2])
                gng = sb(mx, "gng", [64, 128])
                m2 = sb(mx, "m2", [64, 2, 128])
                am2 = sb(mx, "am2", [64, 2, 128])
                pro = ExitStack()
                wsm_s = sb(pro, "wsm_s", [128, 8, 32])
                wsm_b = sb(pro, "wsm_b", [128, 8, 32], BF16)
                ab = sb(pro, "ab", [64, NCH, 2, 8, 2])
                P.dma(wsm_s[:], w_in_v[:, :, 4096:4128], w=["wsm_s"])
                P.dma(gng[:], rows1[104:105, :].partition_broadcast(64), w=["gng"])
                P.add(POOL, "tensor_copy", r=["wsm_s"], w=["wsm_b"], out=wsm_b[:], in_=wsm_s[:])
                for hh in range(2):
                    P.add(POOL, "tensor_copy", r=["m_le"], w=["m2"], out=m2[:, 0, hh * 64:(hh + 1) * 64], in_=m_le[:])
                    P.add(POOL, "tensor_copy", r=["m_ge"], w=["m2"], out=m2[:, 1, hh * 64:(hh + 1) * 64], in_=m_ge[:])
                P.add(POOL, "tensor_copy", r=["am_f"], w=["am2"], out=am2[:, 0, :], in_=am_f[:])
                P.add(POOL, "tensor_copy", r=["am_b"], w=["am2"], out=am2[:, 1, :], in_=am_b[:])
                P.add(DVE, "tensor_copy", r=["brt"], w=["br2"], out=br2[:, 0, :, :].rearrange("p h d -> p d h"),
                      in_=brt[:, 0:16].rearrange("p (d h) -> p d h", d=2))
                P.add(ACT, "activation", r=["brt"], w=["br2"], out=br2[:, 1, :, :].rearrange("p h d -> p d h"),
                      in_=brt[:, 96:112].rearrange("p (d h) -> p d h", d=2), func=AF.Exp)
                P.add(DVE, "tensor_scalar", r=["br2"], w=["br2"], out=br2[:, 1, :, :], in0=br2[:, 1, :, :], scalar1=-1.0, scalar2=None, op0=ALU.mult)
                PSTOP = int(os.environ.get("KPSTOP", "9"))
                if PSTOP == 0:
                    P.flush(); pro.close(); return
                for c in range(NCH):
                    pi = c % 2
                    for k in range(8):
                        P.add(PE, "matmul", r=["wsm_b"], w=[PSK[pi]], out=ps[pi][0:64, 0:32], lhsT=hT[:, k, c * 64:(c + 1) * 64], rhs=wsm_b[:, k, :],
                              start=(k == 0), stop=(k == 7))
                    P.add(DVE if pi else ACT, "tensor_copy" if pi else "copy", r=[PSK[pi]], w=["ab"],
                          out=ab[:, c, :, :, :].rearrange("p a h d -> p a d h"), in_=ps[pi][0:64, 0:32].rearrange("p (a d h) -> p a d h", a=2, d=2))
                bshape = [64, NCH, 8, 2]
                if PSTOP == 1:
                    P.flush(); pro.close(); return
                P.add(DVE, "tensor_tensor", r=["ab", "br2"], w=["gr"], out=gr[:], in0=ab[:, :, 0, :, :], in1=br2[:, 0, :, :].unsqueeze(1).broadcast_to(bshape), op=ALU.add)
                P.add(ACT, "activation", r=["gr"], w=["gr"], out=gr[:], in_=gr[:], func=AF.Exp)
                P.add(ACT, "activation", r=["gr"], w=["gr"], out=gr[:], in_=gr[:], func=AF.Ln, bias=1.0)
                P.add(DVE, "tensor_tensor", r=["gr", "br2"], w=["gr"], out=gr[:], in0=gr[:], in1=br2[:, 1, :, :].unsqueeze(1).broadcast_to(bshape), op=ALU.mult)
                P.add(ACT, "activation", r=["ab"], w=["bet"], out=bet[:], in_=ab[:, :, 1, :, :], func=AF.Sigmoid)
                P.add(DVE, "tensor_scalar", r=["bet"], w=["nbet"], out=nbet[:], in0=bet[:], scalar1=-1.0, scalar2=None, op0=ALU.mult)
                if PSTOP == 2:
                    P.flush(); pro.close(); return
                for half in range(2):
                    cs = slice(half * 18, half * 18 + 18)
                    rhs = gr[:, cs, :, :].rearrange("p c h d -> p (c h d)")
                    P.add(PE, "matmul", r=["gr", "m_le"], w=[PSK[2]], out=ps[2][0:64, 0:288], lhsT=m_le[:], rhs=rhs, start=True, stop=True)
                    P.add(PE, "matmul", r=["gr", "m_ge"], w=[PSK[3]], out=ps[3][0:64, 0:288], lhsT=m_ge[:], rhs=rhs, start=True, stop=True)
                    P.add(PE, "matmul", r=["gr", "ones"], w=[PSK[4]], out=ps[4][:, 0:288], lhsT=ones_f[0:64, :], rhs=rhs, start=True, stop=True)
                    P.add(DVE, "tensor_copy", r=[PSK[2]], w=["gcs"], out=gcs[:, cs, :, 0], in_=ps[2][0:64, 0:288].rearrange("p (c h d) -> p c h d", h=8, d=2)[:, :, :, 0])
                    P.add(DVE, "tensor_copy", r=[PSK[3]], w=["gcs"], out=gcs[:, cs, :, 1], in_=ps[3][0:64, 0:288].rearrange("p (c h d) -> p c h d", h=8, d=2)[:, :, :, 1])
                    P.add(ACT, "activation", r=[PSK[4]], w=["etot"], out=etot[:, cs, :, :].rearrange("p c h d -> p (c h d)"), in_=ps[4][:, 0:288], func=AF.Exp)
                    P.add(DVE, "tensor_tensor", r=[PSK[4], "gcs"], w=["ks"], out=ks[:, cs, :, :].rearrange("p c h d -> p (c h d)"), in0=ps[4][0:64, 0:288],
                          in1=gcs[:, cs, :, :].rearrange("p c h d -> p (c h d)"), op=ALU.subtract)
                P.add(ACT, "activation", r=["ks"], w=["ks"], out=ks[:], in_=ks[:], func=AF.Exp)
                P.add(ACT, "activation", r=["gcs"], w=["egc"], out=egc[:], in_=gcs[:], func=AF.Exp)
                P.add(DVE, "tensor_scalar", r=["gcs"], w=["ngcs"], out=ngcs[:], in0=gcs[:], scalar1=-1.0, scalar2=None, op0=ALU.mult)
                P.add(DVE, "tensor_tensor", r=["egc", "bet"], w=["kbs"], out=kbs[:], in0=egc[:], in1=bet[:], op=ALU.mult)
                P.flush()
                pro.close()
                GSTOP = int(os.environ.get("KGSTOP", "9"))
                if GSTOP == 0:
                    return

                wst = [sb(mx, "gwst0", [128, 8, 128])] * 2
                wb = [sb(mx, f"gwb{i}", [128, 8, 128], BF16) for i in range(2)]
                feat = sb(mx, "gfeat", [128, 2436 + T])
                cin = feat[:, 0:2436]
                co = feat[:, 2436:2436 + T]
                qT = sb(mx, "gqT", [128, T])
                kT = sb(mx, "gkT", [128, T])
                qTb = sb(mx, "gqTb", [128, T], BF16)
                sqt = [sb(mx, "gsq0", [128, 512])] * 2
                rsn = sb(mx, "grsn", [128, 512])
                k_tok = sb(mx, "gktok", [64, NCH, 128], BF16)
                v_tok = sb(mx, "gvtok", [64, NCH, 128], BF16)
                z_tok = sb(mx, "gztok", [64, 32, 128], BF16)
                o_tok = sb(mx, "gotok", [64, 32, 128])
                u_tok = feat[0:64, 0:4608].bitcast(BF16).rearrange("p (c d f) -> p c d f", c=NCH, d=2)
                wT = sb(mx, "gwT", [128, 2, T], BF16)
                qkd = sb(mx, "gqkd", [64, NCH, 2, 64], BF16)
                ko = [sb(mx, f"gko{i}", [64, 128], BF16) for i in range(4)]
                kq = [sb(mx, f"gkq{i}", [64, 128])[:] for i in range(2)]
                R1 = [sb(mx, f"gR1{i}", [64, 2, 128])[:] for i in range(2)]
                E = [sb(mx, f"gE{i}", [64, 2, 128])[:] for i in range(2)]
                tmpA = [sb(mx, f"gtA{i}", [64, 2, 64])[:] for i in range(2)]
                ABs = [sb(mx, f"gAB{i}", [64, 2, 128])[:] for i in range(4)]
                Xs = [sb(mx, f"gX{i}", [64, 2, 64])[:] for i in range(4)]
                TTb = [sb(mx, f"gTT{i}", [64, 2, 64], BF16)[:] for i in range(2)]
                vb = [sb(mx, f"gvb{i}", [64, 2, 128], BF16)[:] for i in range(2)]
                kbg = [sb(mx, f"gkbg{i}", [64, 2, 128], BF16)[:] for i in range(2)]
                xtra = sb(mx, "gxtra", [64, 640])
                wstf = wst[0][:].rearrange("p k n -> p (k n)")
                wbf = [wb[i][:].rearrange("p k n -> p (k n)").bitcast(F32) for i in range(2)]
                sqf = sqt[0][:]
                rsf = rsn[:]

                def reg(base, off, n, shp=False, bf=False):
                    v = base[0:64, off:off + n]
                    if bf:
                        v = v.bitcast(BF16)
                    if shp:
                        v = v.rearrange("p (d f) -> p d f", d=2)
                    return v
                R1.append(reg(wstf, 0, 256, True)); E.append(reg(wstf, 256, 256, True))
                ABs.append(reg(wstf, 512, 256, True)); ABs.append(reg(wstf, 768, 256, True))
                kq.append(reg(wbf[0], 0, 128)); tmpA.append(reg(wbf[0], 128, 128, True))
                Xs.append(reg(wbf[0], 256, 128, True)); Xs.append(reg(wbf[0], 384, 128, True))
                TTb.append(reg(xtra[:], 0, 64, True, True)); vb.append(reg(xtra[:], 64, 128, True, True)); kbg.append(reg(xtra[:], 192, 128, True, True))
                R1.append(reg(sqf, 0, 256, True)); E.append(reg(sqf, 256, 256, True))
                ABs.append(reg(rsf, 0, 256, True)); ABs.append(reg(rsf, 256, 256, True))
                kq.append(reg(wbf[1], 0, 128)); tmpA.append(reg(wbf[1], 128, 128, True))
                Xs.append(reg(wbf[1], 256, 128, True)); Xs.append(reg(wbf[1], 384, 128, True))
                TTb.append(reg(xtra[:], 320, 64, True, True)); vb.append(reg(xtra[:], 384, 128, True, True)); kbg.append(reg(xtra[:], 512, 128, True, True))
                S = [sb(mx, f"gS{i}", [128, 128]) for i in range(2)]
                Sb = [sb(mx, f"gSb{i}", [128, 128], BF16) for i in range(2)]
                vnew = [sb(mx, f"gvn{i}", [64, 128], BF16) for i in range(4)]
                otmp = [sb(mx, f"got{i}", [64, 128]) for i in range(4)]
                ssq = sb(mx, "gssq", [64, 32])
                yst = [sb(mx, f"gyst{i}", [128, 512], BF16) for i in range(2)]
                psb = ps[7][:].bitcast(BF16)
                print("gdn: sbuf remaining", nc.sbuf_bytes_remaining, flush=True)
                cinL = cin[:, 260:2436].rearrange("p (r c) -> p r c", c=68)
                id64 = ident_f[0:64, 0:64]
                offs4 = [0, 1024, 2048, 3072]

                for h in range(8):
                    P.add(POOL, "memset", w=["cin"], ap=cin, constant=0.0)

                    def project(c4, evac):
                        wbi = c4 % 2
                        P.dma(wst[wbi][:], w_in_v[:, :, offs4[c4] + h * 128:offs4[c4] + (h + 1) * 128], w=["gwst"])
                        P.add(POOL, "tensor_copy", r=["gwst"], w=[("gwb", wbi)], out=wb[wbi][:], in_=wst[wbi][:])
                        for bi, (t0, n) in enumerate(BLKS):
                            pi = bi % 2
                            for k in range(8):
                                P.add(PE, "matmul", r=[("gwb", wbi)], w=[PSK[pi]], out=ps[pi][:, :n], lhsT=wb[wbi][:, k, :], rhs=hT[:, k, t0:t0 + n],
                                      start=(k == 0), stop=(k == 7))
                            evac(bi, t0, n, pi)

                    def evac_cin(bi, t0, n, pi):
                        if t0 == 0:
                            P.add(ACT, "copy", r=[PSK[pi]], w=["cin"], out=cin[:, 2:258], in_=ps[pi][:, 0:256])
                        else:
                            r0 = (t0 - 256) // 64
                            P.add(ACT, "copy", r=[PSK[pi]], w=["cin"], out=cinL[:, r0:r0 + 8, 2:66], in_=v3(ps[pi][:, 0:512], 64))

                    def conv_silu(c4, dst):
                        for kk in range(5):
                            wc = cols2[:, kk * 24 + c4 * 8 + h:kk * 24 + c4 * 8 + h + 1]
                            if kk == 0:
                                P.add(DVE, "tensor_scalar", r=["cin", ("cols", 1)], w=["co_c"], out=co[:, 0:256], in0=cin[:, kk:kk + 256], scalar1=wc, scalar2=None, op0=ALU.mult)
                                P.add(DVE, "tensor_scalar", r=["cin", ("cols", 1)], w=["co_l"], out=v3(co[:, 256:T], 64), in0=cinL[:, :, kk:kk + 64], scalar1=wc, scalar2=None, op0=ALU.mult)
                            else:
                                P.add(DVE, "scalar_tensor_tensor", r=["cin", ("cols", 1), "co_c"], w=["co_c"], out=co[:, 0:256], in0=cin[:, kk:kk + 256], scalar=wc,
                                      in1=co[:, 0:256], op0=ALU.mult, op1=ALU.add)
                                P.add(DVE, "scalar_tensor_tensor", r=["cin", ("cols", 1), "co_l"], w=["co_l"], out=v3(co[:, 256:T], 64), in0=cinL[:, :, kk:kk + 64], scalar=wc,
                                      in1=v3(co[:, 256:T], 64), op0=ALU.mult, op1=ALU.add)
                        P.add(ACT, "activation", r=["co_c", "co_l"], w=[dst[1]], out=dst[0][:, :], in_=co, func=AF.Silu)

                    def l2norm(src, key, scale, extra_bf16=None):
                        for bi, (t0, n) in enumerate(BLKS):
                            pn = 2 + (bi % 2)
                            P.add(ACT, "activation", r=[key], w=["gsq"], out=sqt[bi % 2][:, :n], in_=src[:, t0:t0 + n], func=AF.Square)
                            P.add(PE, "matmul", r=["gsq", "ones"], w=[PSK[pn]], out=ps[pn][:, :n], lhsT=ones_f[:], rhs=sqt[bi % 2][:, :n], start=True, stop=True)
                            P.add(ACT, "activation", r=[PSK[pn]], w=["grsn"], out=rsn[:, :n], in_=ps[pn][:, :n], func=AF.Sqrt, bias=epsc[:, 0:1])
                            P.add(DVE, "reciprocal", r=["grsn"], w=["grsn"], out=rsn[:, :n], in_=rsn[:, :n])
                            P.add(DVE, "scalar_tensor_tensor", r=[key, "grsn"], w=[key], out=src[:, t0:t0 + n], in0=src[:, t0:t0 + n], scalar=scale, in1=rsn[:, :n],
                                  op0=ALU.mult, op1=ALU.mult)
                        if extra_bf16 is not None:
                            P.add(POOL, "tensor_copy", r=[key], w=["gqTb"], out=extra_bf16[:, :], in_=src[:, :])

                    def to_tok(src, key, dst, dkey, c0, silu=False):
                        nch = NCH - c0
                        for g4 in range(nch // 4):
                            pi = g4 % 2
                            for j in range(4):
                                c = c0 + g4 * 4 + j
                                P.add(PE, "transpose", r=[key, "ident_f"], w=[PSK[pi]], out=ps[pi][0:64, j * 128:(j + 1) * 128], in_=src[:, c * 64:(c + 1) * 64], identity=ident_f[:])
                            if silu:
                                P.add(ACT, "activation", r=[PSK[pi]], w=[dkey], out=dst[:, g4 * 4:g4 * 4 + 4, :], in_=v3(ps[pi][0:64, 0:512], 128), func=AF.Silu)
                            else:
                                P.add(DVE if pi else ACT, "tensor_copy" if pi else "copy", r=[PSK[pi]], w=[dkey], out=dst[:, g4 * 4:g4 * 4 + 4, :], in_=v3(ps[pi][0:64, 0:512], 128))

                    project(0, evac_cin); conv_silu(0, (qT, "gqT")); l2norm(qT, "gqT", 128 ** -0.5, qTb)
                    project(1, evac_cin); conv_silu(1, (kT, "gkT")); l2norm(kT, "gkT", 1.0)
                    to_tok(kT, "gkT", k_tok, "gktok", 0)
                    project(2, evac_cin); conv_silu(2, (co, "gco"))
                    to_tok(co, "gco", v_tok, "gvtok", 0)
                    project(3, lambda bi, t0, n, pi: P.add(ACT, "copy", r=[PSK[pi]], w=["gco"], out=co[:, t0:t0 + n], in_=ps[pi][:, :n]))
                    to_tok(co, "gco", z_tok, "gztok", 4, silu=True)
                    P.flush()
                    if GSTOP == 1:
                        return

                    def solve_chunk(c, a):
                        pk = 4 + a
                        pw = 4 + a
                        P.add(PE, "matmul", r=["gkT"], w=[PSK[pk]], out=ps[pk][0:64, 0:64], lhsT=kT[:, c * 64:(c + 1) * 64], rhs=kT[:, c * 64:(c + 1) * 64], start=True, stop=True)
                        P.add(PE, "matmul", r=["gkT", "gqT"], w=[PSK[pk]], out=ps[pk][0:64, 64:128], lhsT=kT[:, c * 64:(c + 1) * 64], rhs=qT[:, c * 64:(c + 1) * 64], start=True, stop=True)
                        P.add(ACT, "copy", r=[PSK[pk]], w=[("gkq", a)], out=kq[a][:], in_=ps[pk][0:64, 0:128])
                        yield
                        for d in range(2):
                            P.add(DVE, "tensor_scalar", r=["m2", "gr"], w=[("gR1", a)], out=R1[a][:, d, :], in0=m2[:, d, :], scalar1=gr[:, c, h, d:d + 1], scalar2=None, op0=ALU.mult)
                        P.add(PE, "matmul", r=[("gR1", a), "ones"], w=[PSK[pk]], out=ps[pk][0:64, 128:384], lhsT=ones_f[0:64, 0:64], rhs=R1[a][:].rearrange("p d f -> p (d f)"), start=True, stop=False)
                        P.add(PE, "matmul", r=["am2", "ident_f"], w=[PSK[pk]], out=ps[pk][0:64, 128:384], lhsT=id64, rhs=am2[:].rearrange("p d f -> p (d f)"), start=False, stop=True)
                        yield
                        for d in range(2):
                            P.add(ACT, "activation", r=[PSK[pk], "ngcs"], w=[("gE", a)], out=E[a][:, d, 0:64], in_=ps[pk][0:64, 128 + d * 128:128 + d * 128 + 64], func=AF.Exp,
                                  bias=ngcs[:, c, h, d:d + 1], scale=1.0)
                            P.add(ACT, "activation", r=[PSK[pk], "gcs"], w=[("gE", a)], out=E[a][:, d, 64:128], in_=ps[pk][0:64, 128 + d * 128 + 64:128 + (d + 1) * 128], func=AF.Exp,
                                  bias=gcs[:, c, h, d:d + 1], scale=-1.0)
                        yield
                        P.add(DVE, "tensor_tensor", r=[("gkq", a), ("gE", a)], w=[("gqkd", c)], out=qkd[:, c, :, :], in0=kq[a][:, 64:128].unsqueeze(1).broadcast_to([64, 2, 64]),
                              in1=E[a][:, :, 0:64], op=ALU.mult)
                        P.add(DVE, "tensor_tensor", r=[("gkq", a), ("gE", a)], w=[("gtA", a)], out=tmpA[a][:], in0=kq[a][:, 0:64].unsqueeze(1).broadcast_to([64, 2, 64]),
                              in1=E[a][:, :, 64:128], op=ALU.mult)
                        ab0 = ABs[2 * a]
                        ab1 = ABs[2 * a + 1]
                        P.add(DVE, "tensor_tensor", r=[("gtA", a), "nbet"], w=[("gAB", 2 * a, "A")], out=ab0[:, :, 0:64], in0=tmpA[a][:],
                              in1=nbet[:, c, h, :].unsqueeze(2).broadcast_to([64, 2, 64]), op=ALU.mult)
                        yield
                        for d in range(2):
                            P.add(PE, "transpose", r=[("gAB", 2 * a, "A"), "ident_f"], w=[PSK[pw]], out=ps[pw][0:64, d * 64:(d + 1) * 64], in_=ab0[:, d, 0:64], identity=id64)
                        yield
                        P.add(ACT, "copy", r=[PSK[pw]], w=[("gAB", 2 * a, "B")], out=ab0[:, :, 64:128], in_=v3(ps[pw][0:64, 0:128], 64))
                        x0 = Xs[2 * a]
                        x1 = Xs[2 * a + 1]
                        P.add(DVE, "tensor_tensor", r=[PSK[pw], "ident_f"], w=[("gX", 2 * a)], out=x0[:], in0=v3(ps[pw][0:64, 0:128], 64),
                              in1=id64.unsqueeze(1).broadcast_to([64, 2, 64]), op=ALU.add)
                        cur_ab, nxt_ab, ci, ni = ab0, ab1, 2 * a, 2 * a + 1
                        cur_x, nxt_x, cxi, nxi = x0, x1, 2 * a, 2 * a + 1
                        for lvl in range(5):
                            last = lvl == 4
                            for d in range(2):
                                P.add(PE, "matmul", r=[("gAB", ci, "A"), ("gAB", ci, "B")], w=[PSK[pw]], out=ps[pw][0:64, 128 + d * 128:128 + d * 128 + 64],
                                      lhsT=cur_ab[:, d, 64:128], rhs=cur_ab[:, d, 0:64], start=True, stop=True)
                                if not last:
                                    P.add(PE, "matmul", r=[("gAB", ci, "A"), ("gAB", ci, "B")], w=[PSK[pw]], out=ps[pw][0:64, 128 + d * 128 + 64:128 + (d + 1) * 128],
                                          lhsT=cur_ab[:, d, 0:64], rhs=cur_ab[:, d, 64:128], start=True, stop=True)
                            yield
                            if last:
                                P.add(ACT, "copy", r=[PSK[pw]], w=[("gAB", ni, "A")], out=nxt_ab[:, :, 0:64], in_=v3(ps[pw][0:64, 128:384], 128)[:, :, 0:64])
                            else:
                                P.add(ACT, "copy", r=[PSK[pw]], w=[("gAB", ni, "A"), ("gAB", ni, "B")], out=nxt_ab[:], in_=v3(ps[pw][0:64, 128:384], 128))
                            yield
                            for d in range(2):
                                P.add(PE, "matmul", r=[("gAB", ni, "A"), ("gX", cxi)], w=[PSK[pw]], out=ps[pw][0:64, 384 + d * 64:384 + (d + 1) * 64],
                                      lhsT=nxt_ab[:, d, 0:64], rhs=cur_x[:, d, :], start=True, stop=True)
                            yield
                            if last:
                                P.add(DVE, "tensor_tensor", r=[PSK[pw], ("gX", cxi)], w=[("gTT", a)], out=TTb[a][:], in0=v3(ps[pw][0:64, 384:512], 64), in1=cur_x[:], op=ALU.add)
                            else:
                                P.add(DVE, "tensor_tensor", r=[PSK[pw], ("gX", cxi)], w=[("gX", nxi)], out=nxt_x[:], in0=v3(ps[pw][0:64, 384:512], 64), in1=cur_x[:], op=ALU.add)
                            cur_ab, nxt_ab, ci, ni = nxt_ab, cur_ab, ni, ci
                            yield
                            cur_x, nxt_x, cxi, nxi = nxt_x, cur_x, nxi, cxi
                        P.add(POOL, "tensor_tensor", r=["gvtok", "bet"], w=[("gvb", a)], out=vb[a][:], in0=v_tok[:, c, :].unsqueeze(1).broadcast_to([64, 2, 128]),
                              in1=bet[:, c, h, :].unsqueeze(2).broadcast_to([64, 2, 128]), op=ALU.mult)
                        P.add(POOL, "tensor_tensor", r=["gktok", "kbs"], w=[("gkbg", a)], out=kbg[a][:], in0=k_tok[:, c, :].unsqueeze(1).broadcast_to([64, 2, 128]),
                              in1=kbs[:, c, h, :].unsqueeze(2).broadcast_to([64, 2, 128]), op=ALU.mult)
                        pu = pk
                        for d in range(2):
                            P.add(PE, "matmul", r=[("gTT", a), ("gvb", a)], w=[PSK[pu]], out=ps[pu][0:64, d * 128:(d + 1) * 128], lhsT=TTb[a][:, d, :], rhs=vb[a][:, d, :], start=True, stop=True)
                            P.add(PE, "matmul", r=[("gTT", a), ("gkbg", a)], w=[PSK[pu]], out=ps[pu][:, 256 + d * 64:256 + (d + 1) * 64], lhsT=kbg[a][:, d, :], rhs=TTb[a][:, d, :], start=True, stop=True)
                        yield
                        P.add(ACT, "copy", r=[PSK[pu]], w=[("gutok", c)], out=u_tok[:, c, :, :], in_=v3(ps[pu][0:64, 0:256], 128))
                        P.add(DVE, "tensor_copy", r=[PSK[pu]], w=[("gwT", c)], out=wT[:, :, c * 64:(c + 1) * 64], in_=v3(ps[pu][:, 256:384], 64))
                    orders = [list(range(NCH)), [3, 2, 1, 0] + list(range(NCH - 1, 3, -1))]
                    for d in range(2):
                        P.add(POOL, "memset", w=[("gS", d)], ap=S[d][:], constant=0.0)
                        P.add(POOL, "memset", w=[("gSb", d)], ap=Sb[d][:], constant=0.0)
                    P.add(POOL, "memset", w=[("gotok", c) for c in range(4, NCH)], ap=o_tok[:], constant=0.0)
                    done = set()

                    def solve_wrap(c, a):
                        yield from solve_chunk(c, a)
                        done.add(c)

                    def scan_gen(d):
                        pb = d
                        for step in range(NCH):
                            c = orders[d][step]
                            while c not in done:
                                yield
                            vi = d * 2 + (step % 2)
                            P.add(PE, "matmul", r=[("gwT", c), ("gSb", d)], w=[PSK[pb]], out=ps[pb][0:64, 0:128], lhsT=wT[:, d, c * 64:(c + 1) * 64], rhs=Sb[d][:], start=True, stop=True)
                            if c >= 4:
                                P.add(PE, "matmul", r=["gqTb", ("gSb", d)], w=[PSK[2 + pb]], out=ps[2 + pb][0:64, 0:128], lhsT=qTb[:, c * 64:(c + 1) * 64], rhs=Sb[d][:], start=True, stop=True)
                            P.add(POOL, "tensor_scalar", r=["gktok", "ks"], w=[("gko", vi)], out=ko[vi][:], in0=k_tok[:, c, :], scalar1=ks[:, c, h, d:d + 1], scalar2=None, op0=ALU.mult)
                            yield
                            P.add(DVE, "tensor_tensor", r=[("gutok", c), PSK[pb]], w=[("gvn", vi)], out=vnew[vi][:], in0=u_tok[:, c, d, :], in1=ps[pb][0:64, 0:128], op=ALU.subtract)
                            yield
                            P.add(PE, "matmul", r=[("gko", vi), ("gvn", vi)], w=[PSK[pb]], out=ps[pb][:, 128:256], lhsT=ko[vi][:], rhs=vnew[vi][:], start=True, stop=True)
                            if c >= 4:
                                P.add(PE, "matmul", r=[("gqkd", c), ("gvn", vi)], w=[PSK[2 + pb]], out=ps[2 + pb][0:64, 128:256], lhsT=qkd[:, c, d, :], rhs=vnew[vi][:], start=True, stop=True)
                            yield
                            P.add(DVE, "scalar_tensor_tensor", r=[("gS", d), PSK[pb], "etot"], w=[("gSb", d)], out=Sb[d][:], in0=S[d][:], scalar=etot[:, c, h, d:d + 1], in1=ps[pb][:, 128:256],
                                  op0=ALU.mult, op1=ALU.add)
                            P.add(DVE, "scalar_tensor_tensor", r=[("gS", d), PSK[pb], "etot"], w=[("gS", d)], out=S[d][:], in0=S[d][:], scalar=etot[:, c, h, d:d + 1], in1=ps[pb][:, 128:256],
                                  op0=ALU.mult, op1=ALU.add)
                            if c >= 4:
                                P.add(ACT, "copy", r=[PSK[2 + pb]], w=[("got", vi)], out=otmp[vi][:], in_=ps[2 + pb][0:64, 128:256])
                                yield
                                P.add(DVE, "scalar_tensor_tensor", r=[PSK[2 + pb], ("got", vi), "egc"], w=[("got", vi)], out=otmp[vi][:], in0=ps[2 + pb][0:64, 0:128],
                                      scalar=egc[:, c, h, d:d + 1], in1=otmp[vi][:], op0=ALU.mult, op1=ALU.add)
                                yield
                                P.add(POOL, "tensor_tensor", r=[("got", vi), ("gotok", c)], w=[("gotok", c)], out=o_tok[:, c - 4, :], in0=o_tok[:, c - 4, :], in1=otmp[vi][:], op=ALU.add)
                            yield

                    sorder = []
                    for i_ in range(NCH):
                        for c_ in (orders[0][i_], orders[1][i_]):
                            if c_ not in sorder:
                                sorder.append(c_)
                    slots = [None, None, None, None]
                    scans = [scan_gen(0), scan_gen(1)]
                    while scans or sorder or any(g_ is not None for g_ in slots):
                        for si in range(4):
                            if slots[si] is None and sorder:
                                slots[si] = solve_wrap(sorder.pop(0), si)
                            if slots[si] is not None:
                                try:
                                    next(slots[si])
                                except StopIteration:
                                    slots[si] = None
                        for g_ in list(scans):
                            try:
                                next(g_)
                            except StopIteration:
                                scans.remove(g_)
                    P.flush()
                    if dbg:
                        P.dma(out.rearrange("(c p) n -> p c n", p=64)[:, :, h * 128:(h + 1) * 128], o_tok[:], r=[("gotok", c) for c in range(4, NCH)])
                        P.flush()
                        continue
                    sq16 = u_tok[:].rearrange("p c d f -> p (c d) f")[:, 0:32, :]
                    ya_tok = k_tok[:, 0:32, :]
                    P.add(DVE, "tensor_tensor", r=[("gotok", c) for c in range(4, NCH)], w=["gutok"], out=sq16, in0=o_tok[:], in1=o_tok[:], op=ALU.mult)
                    P.add(DVE, "tensor_reduce", r=["gutok"], w=["gssq"], out=ssq[:], in_=sq16, axis=AX.X, op=ALU.add)
                    P.add(ACT, "activation", r=["gssq"], w=["gssq"], out=ssq[:], in_=ssq[:], func=AF.Sqrt, scale=1.0 / 128, bias=epsc[0:64, 0:1])
                    P.add(DVE, "reciprocal", r=["gssq"], w=["gssq"], out=ssq[:], in_=ssq[:])
                    P.add(DVE, "tensor_tensor", r=["gssq"] + [("gotok", c) for c in range(4, NCH)], w=[("gotok", c) for c in range(4, NCH)], out=o_tok[:], in0=o_tok[:],
                          in1=ssq[:].unsqueeze(2).broadcast_to([64, 32, 128]), op=ALU.mult)
                    P.add(DVE, "tensor_tensor", r=["gng"] + [("gotok", c) for c in range(4, NCH)], w=[("gotok", c) for c in range(4, NCH)], out=o_tok[:], in0=o_tok[:],
                          in1=gng[:].unsqueeze(1).broadcast_to([64, 32, 128]), op=ALU.mult)
                    P.add(DVE, "tensor_tensor", r=["gztok"] + [("gotok", c) for c in range(4, NCH)], w=["gktok"], out=ya_tok, in0=o_tok[:], in1=z_tok[:], op=ALU.mult)
                    for bi in range(4):
                        for j in range(8):
                            P.add(PE, "transpose", r=["gktok", "ident_b"], w=[PSK[7]], out=psb[:, j * 64:(j + 1) * 64], in_=ya_tok[:, bi * 8 + j, :], identity=ident_b[0:64, 0:64])
                        P.add(ACT, "copy", r=[PSK[7]], w=[("gyst", bi % 2)], out=yst[bi % 2][:], in_=psb[:, 0:512])
                        P.dma(yab[h, :, bi * 512:(bi + 1) * 512], yst[bi % 2][:], r=[("gyst", bi % 2)])
                    P.flush()

        def ssd(dbg=False):
            XBC0 = 6176
            with ExitStack() as mx:
                m2d = sb(mx, "sm2d", [64, 2, 64])
                neg8 = sb(mx, "sneg8", [64, 2, 4, 64])
                Dt = sb(mx, "sDt", [64, 32])
                for d_, m_ in ((0, m_le), (1, m_ge)):
                    P.add(POOL, "tensor_copy", w=["sm2d"], out=m2d[:, d_, :], in_=m_[:])
                P.add(POOL, "tensor_copy", w=["sneg8"], out=neg8[:, 0, :, :], in_=v3(neg4_f[:], 64))
                P.add(POOL, "tensor_copy", w=["sneg8"], out=neg8[:, 1, :, :], in_=v3(neg4_b[:], 64))
                P.add(POOL, "tensor_copy", w=["sDt"], out=Dt[:], in_=brt[:, 192:224])
                BT = sb(mx, "sBT", [128, T], BF16)
                CT = sb(mx, "sCT", [128, T], BF16)
                B_tok = sb(mx, "sBtok", [64, NCH, 128], BF16)
                cbT = sb(mx, "scbT", [64, NCH, 64])
                wst = sb(mx, "swst", [128, 8, 128])
                wb = [sb(mx, f"swb{i}", [128, 8, 128], BF16) for i in range(2)]
                feat = sb(mx, "sfeat", [128, 2436 + T])
                cin = feat[:, 0:2436]
                co = feat[:, 2436:2436 + T]
                cinL = cin[:, 260:2436].rearrange("p (r c) -> p r c", c=68)
                xs_tok = sb(mx, "sxstok", [64, NCH, 256], BF16)
                z_tok = sb(mx, "sztok", [64, 32, 256], BF16)
                y_tok = sb(mx, "sytok", [64, 32, 256])
                wsm_s = sb(mx, "swsm_s", [128, 8, 8])
                wsm_b = sb(mx, "swsm_b", [128, 8, 8], BF16)
                dtr = sb(mx, "sdtr", [64, NCH, 4, 2])
                ar = sb(mx, "sar", [64, NCH, 4, 2])
                acs = sb(mx, "sacs", [64, NCH, 4, 2])
                eacs = sb(mx, "seacs", [64, NCH, 4, 2])
                wsc = sb(mx, "swsc", [64, NCH, 4, 2])
                etot = sb(mx, "setot", [128, NCH, 4, 2])
                bq = sb(mx, "sbq", [64, 2, 4, 2])
                R1 = [sb(mx, f"sR1{i}", [64, 2, 4, 64]) for i in range(4)]
                MT = [sb(mx, f"sMT{i}", [64, 2, 4, 64]) for i in range(4)]
                MTb = [sb(mx, f"sMTb{i}", [64, 2, 4, 64], BF16) for i in range(4)]
                xsdt = [sb(mx, f"sxsdt{i}", [64, 2, 4, 64], BF16) for i in range(4)]
                xsw = [sb(mx, f"sxsw{i}", [64, 256], BF16) for i in range(4)]
                ytmp = [sb(mx, f"sytmp{i}", [64, 256]) for i in range(2)] * 2
                ST = [sb(mx, f"sST{i}", [128, 256]) for i in range(2)]
                STb = [sb(mx, f"sSTb{i}", [128, 256], BF16) for i in range(2)]
                yst = [sb(mx, "syst0", [128, 512], BF16)] * 2
                psb = ps[7][:].bitcast(BF16)
                id64 = ident_f[0:64, 0:64]

                def project(col0, evac, wbi):
                    P.dma(wst[:], w_in_v[:, :, col0:col0 + 128], w=["swst"])
                    P.add(POOL, "tensor_copy", r=["swst"], w=[("swb", wbi)], out=wb[wbi][:], in_=wst[:])
                    for bi, (t0, n) in enumerate(BLKS):
                        pi = bi % 2
                        for k in range(8):
                            P.add(PE, "matmul", r=[("swb", wbi)], w=[PSK[pi]], out=ps[pi][:, :n], lhsT=wb[wbi][:, k, :], rhs=hT[:, k, t0:t0 + n], start=(k == 0), stop=(k == 7))
                        evac(bi, t0, n, pi)

                def evac_cin(bi, t0, n, pi):
                    if t0 == 0:
                        P.add(ACT, "copy", r=[PSK[pi]], w=["cin"], out=cin[:, 2:258], in_=ps[pi][:, 0:256])
                    else:
                        r0 = (t0 - 256) // 64
                        P.add(ACT, "copy", r=[PSK[pi]], w=["cin"], out=cinL[:, r0:r0 + 8, 2:66], in_=v3(ps[pi][:, 0:512], 64))

                def evac_co(bi, t0, n, pi):
                    P.add(ACT, "copy", r=[PSK[pi]], w=["sco"], out=co[:, t0:t0 + n], in_=ps[pi][:, :n])

                def conv_silu(ctile, dst, dkey):
                    for kk in range(5):
                        wc = cols3[:, kk * 20 + ctile:kk * 20 + ctile + 1]
                        if kk == 0:
                            P.add(DVE, "tensor_scalar", r=["cin"], w=["co_c", "sco"], out=co[:, 0:256], in0=cin[:, kk:kk + 256], scalar1=wc, scalar2=None, op0=ALU.mult)
                            P.add(DVE, "tensor_scalar", r=["cin"], w=["co_l", "sco"], out=v3(co[:, 256:T], 64), in0=cinL[:, :, kk:kk + 64], scalar1=wc, scalar2=None, op0=ALU.mult)
                        else:
                            P.add(DVE, "scalar_tensor_tensor", r=["cin", "co_c"], w=["co_c"], out=co[:, 0:256], in0=cin[:, kk:kk + 256], scalar=wc, in1=co[:, 0:256], op0=ALU.mult, op1=ALU.add)
                            P.add(DVE, "scalar_tensor_tensor", r=["cin", "co_l"], w=["co_l"], out=v3(co[:, 256:T], 64), in0=cinL[:, :, kk:kk + 64], scalar=wc, in1=v3(co[:, 256:T], 64),
                                  op0=ALU.mult, op1=ALU.add)
                    P.add(ACT, "activation", r=["co_c", "co_l"], w=[dkey], out=dst, in_=co, func=AF.Silu, bias=cols3[:, 100 + ctile:101 + ctile])

                def to_tok(src, key, dst_fn, dkey, c0, silu=False):
                    nch = NCH - c0
                    for g4 in range(nch // 4):
                        pi = g4 % 2
                        for j in range(4):
                            c = c0 + g4 * 4 + j
                            P.add(PE, "transpose", r=[key, "ident_f"], w=[PSK[pi]], out=ps[pi][0:64, j * 128:(j + 1) * 128], in_=src[:, c * 64:(c + 1) * 64], identity=ident_f[:])
                        if silu:
                            P.add(ACT, "activation", r=[PSK[pi]], w=[dkey], out=dst_fn(g4), in_=v3(ps[pi][0:64, 0:512], 128), func=AF.Silu)
                        else:
                            P.add(DVE if pi else ACT, "tensor_copy" if pi else "copy", r=[PSK[pi]], w=[dkey], out=dst_fn(g4), in_=v3(ps[pi][0:64, 0:512], 128))

                SG = int(os.environ.get("KSSDG", "0"))
                for quad in (range(4 * SG, 4 * SG + 4) if dbg else range(8)):
                    grp = quad // 4
                    P.add(POOL, "memset", w=["cin"], ap=cin, constant=0.0)
                    if quad % 4 == 0:
                        project(XBC0 + 2048 + grp * 128, evac_cin, 0)
                        conv_silu(16 + grp, co, "sco")
                        P.add(POOL, "tensor_copy", r=["sco"], w=["sBT"], out=BT[:], in_=co)
                        to_tok(co, "sco", lambda g4: B_tok[:, g4 * 4:g4 * 4 + 4, :], "sBtok", 0)
                        project(XBC0 + 2304 + grp * 128, evac_cin, 1)
                        conv_silu(18 + grp, CT[:], "sCT")
                        for c in range(NCH):
                            pi = 2 + (c % 2)
                            P.add(PE, "matmul", r=["sBT", "sCT"], w=[PSK[pi]], out=ps[pi][0:64, 0:64], lhsT=BT[:, c * 64:(c + 1) * 64], rhs=CT[:, c * 64:(c + 1) * 64], start=True, stop=True)
                            P.add(DVE if c % 2 else ACT, "tensor_copy" if c % 2 else "copy", r=[PSK[pi]], w=["scbT"], out=cbT[:, c, :], in_=ps[pi][0:64, 0:64])
                        P.flush()
                    for d in range(2):
                        c0_ = 8736 + d * 32 + quad * 4
                        P.dma(wsm_s[:, :, d * 4:(d + 1) * 4], w_in_v[:, :, c0_:c0_ + 4], w=["swsm_s"])
                    P.add(POOL, "tensor_copy", r=["swsm_s"], w=["swsm_b"], out=wsm_b[:], in_=wsm_s[:])
                    for d in range(2):
                        P.add(DVE, "tensor_copy", w=["sbq"], out=bq[:, 0, :, d], in_=brt[:, 32 + d * 32 + quad * 4:32 + d * 32 + quad * 4 + 4])
                        P.add(ACT, "activation", w=["sbq"], out=bq[:, 1, :, d], in_=brt[:, 128 + d * 32 + quad * 4:128 + d * 32 + quad * 4 + 4], func=AF.Exp)
                    P.add(DVE, "tensor_scalar", r=["sbq"], w=["sbq"], out=bq[:, 1, :, :], in0=bq[:, 1, :, :], scalar1=-1.0, scalar2=None, op0=ALU.mult)
                    for c in range(NCH):
                        pi = 2 + (c % 2)
                        for k in range(8):
                            P.add(PE, "matmul", r=["swsm_b"], w=[PSK[pi]], out=ps[pi][0:64, 0:8], lhsT=hT[:, k, c * 64:(c + 1) * 64], rhs=wsm_b[:, k, :], start=(k == 0), stop=(k == 7))
                        P.add(DVE if c % 2 else ACT, "tensor_copy" if c % 2 else "copy", r=[PSK[pi]], w=["sdtr"], out=dtr[:, c, :, :].rearrange("p h d -> p d h"),
                              in_=ps[pi][0:64, 0:8].rearrange("p (d h) -> p d h", d=2))
                    qshape = [64, NCH, 4, 2]
                    P.add(DVE, "tensor_tensor", r=["sdtr", "sbq"], w=["sdtr"], out=dtr[:], in0=dtr[:], in1=bq[:, 0, :, :].unsqueeze(1).broadcast_to(qshape), op=ALU.add)
                    P.add(ACT, "activation", r=["sdtr"], w=["sdtr"], out=dtr[:], in_=dtr[:], func=AF.Exp)
                    P.add(ACT, "activation", r=["sdtr"], w=["sdtr"], out=dtr[:], in_=dtr[:], func=AF.Ln, bias=1.0)
                    P.add(DVE, "tensor_tensor", r=["sdtr", "sbq"], w=["sar"], out=ar[:], in0=dtr[:], in1=bq[:, 1, :, :].unsqueeze(1).broadcast_to(qshape), op=ALU.mult)
                    rhs = ar[:].rearrange("p c h d -> p (c h d)")
                    P.add(PE, "matmul", r=["sar", "m_le"], w=[PSK[4]], out=ps[4][0:64, 0:288], lhsT=m_le[:], rhs=rhs, start=True, stop=True)
                    P.add(PE, "matmul", r=["sar", "m_ge"], w=[PSK[5]], out=ps[5][0:64, 0:288], lhsT=m_ge[:], rhs=rhs, start=True, stop=True)
                    P.add(PE, "matmul", r=["sar", "ones"], w=[PSK[6]], out=ps[6][:, 0:288], lhsT=ones_f[0:64, :], rhs=rhs, start=True, stop=True)
                    P.add(DVE, "tensor_copy", r=[PSK[4]], w=["sacs"], out=acs[:, :, :, 0], in_=ps[4][0:64, 0:288].rearrange("p (c h d) -> p c h d", h=4, d=2)[:, :, :, 0])
                    P.add(DVE, "tensor_copy", r=[PSK[5]], w=["sacs"], out=acs[:, :, :, 1], in_=ps[5][0:64, 0:288].rearrange("p (c h d) -> p c h d", h=4, d=2)[:, :, :, 1])
                    P.add(ACT, "activation", r=[PSK[6]], w=["setot"], out=etot[:].rearrange("p c h d -> p (c h d)"), in_=ps[6][:, 0:288], func=AF.Exp)
                    P.add(DVE, "tensor_tensor", r=[PSK[6], "sacs"], w=["swsc"], out=wsc[:].rearrange("p c h d -> p (c h d)"), in0=ps[6][0:64, 0:288],
                          in1=acs[:].rearrange("p c h d -> p (c h d)"), op=ALU.subtract)
                    P.add(ACT, "activation", r=["swsc"], w=["swsc"], out=wsc[:], in_=wsc[:], func=AF.Exp)
                    P.add(DVE, "tensor_tensor", r=["swsc", "sdtr"], w=["swsc"], out=wsc[:], in0=wsc[:], in1=dtr[:], op=ALU.mult)
                    P.add(ACT, "activation", r=["sacs"], w=["seacs"], out=eacs[:], in_=acs[:], func=AF.Exp)
                    for ft in range(2):
                        tile_i = quad * 2 + ft
                        project(XBC0 + tile_i * 128, evac_cin, ft)
                        conv_silu(tile_i, co, "sco")
                        to_tok(co, "sco", lambda g4, ft=ft: xs_tok[:, g4 * 4:g4 * 4 + 4, ft * 128:(ft + 1) * 128], "sxstok", 0)
                    for ft in range(2):
                        tile_i = quad * 2 + ft
                        project(4128 + tile_i * 128, evac_co, ft)
                        to_tok(co, "sco", lambda g4, ft=ft: z_tok[:, g4 * 4:g4 * 4 + 4, ft * 128:(ft + 1) * 128], "sztok", 4, silu=True)
                    P.flush()
                    ddone = set()

                    def diag_chunk(c, a):
                        pm = 4 + a
                        py = 4 + a
                        P.add(DVE, "tensor_tensor", r=["sm2d", "sar"], w=[("sR1", a)], out=R1[a][:], in0=m2d[:].unsqueeze(2).broadcast_to([64, 2, 4, 64]),
                              in1=ar[:, c, :, :].rearrange("p h d -> p d h").unsqueeze(3).broadcast_to([64, 2, 4, 64]), op=ALU.mult)
                        yield
                        P.add(PE, "matmul", r=[("sR1", a), "ones"], w=[PSK[pm]], out=ps[pm][0:64, 0:512], lhsT=ones_f[0:64, 0:64], rhs=R1[a][:].rearrange("p d h l -> p (d h l)"), start=True, stop=False)
                        P.add(PE, "matmul", r=["sneg8", "ident_f"], w=[PSK[pm]], out=ps[pm][0:64, 0:512], lhsT=id64, rhs=neg8[:].rearrange("p d h l -> p (d h l)"), start=False, stop=True)
                        yield
                        P.add(DVE, "tensor_tensor", r=[PSK[pm], "sacs"], w=[("sMT", a)], out=MT[a][:], in0=ps[pm][0:64, 0:512].rearrange("p (d h l) -> p d h l", d=2, h=4),
                              in1=acs[:, c, :, :].rearrange("p h d -> p d h").unsqueeze(3).broadcast_to([64, 2, 4, 64]), op=ALU.subtract)
                        yield
                        P.add(ACT, "activation", r=[("sMT", a)], w=[("sMT", a)], out=MT[a][:], in_=MT[a][:], func=AF.Exp)
                        yield
                        P.add(DVE, "tensor_tensor", r=[("sMT", a), "scbT"], w=[("sMTb", a)], out=MTb[a][:].rearrange("p d h l -> p (d h) l"), in0=MT[a][:].rearrange("p d h l -> p (d h) l"),
                              in1=cbT[:, c, :].unsqueeze(1).broadcast_to([64, 8, 64]), op=ALU.mult)
                        P.add(POOL, "tensor_tensor", r=["sxstok", "sdtr"], w=[("sxsdt", a)], out=xsdt[a][:], in0=xs_tok[:, c, :].rearrange("p (h q) -> p h q", h=4).unsqueeze(1).broadcast_to([64, 2, 4, 64]),
                              in1=dtr[:, c, :, :].rearrange("p h d -> p d h").unsqueeze(3).broadcast_to([64, 2, 4, 64]), op=ALU.mult)
                        yield
                        for hh in range(4):
                            for d in range(2):
                                P.add(PE, "matmul", r=[("sMTb", a), ("sxsdt", a)], w=[PSK[py]], out=ps[py][0:64, hh * 64:(hh + 1) * 64], lhsT=MTb[a][:, d, hh, :], rhs=xsdt[a][:, d, hh, :],
                                      start=(d == 0), stop=(d == 1))
                        yield
                        P.add(ACT, "copy", r=[PSK[py]], w=[("sytok", c)], out=y_tok[:, c - 4, :], in_=ps[py][0:64, 0:256])
                        ddone.add(c)
                    orders = [list(range(NCH)), [3, 2, 1, 0] + list(range(NCH - 1, 3, -1))]
                    for d in range(2):
                        P.add(POOL, "memset", w=[("sST", d)], ap=ST[d][:], constant=0.0)
                        P.add(POOL, "memset", w=[("sSTb", d)], ap=STb[d][:], constant=0.0)
                    ddone.clear()

                    def scan_gen(d):
                        po, pst = d, 2 + d
                        for step in range(NCH):
                            c = orders[d][step]
                            vi = d * 2 + (step % 2)
                            while c >= 4 and c not in ddone:
                                yield
                            P.add(POOL, "tensor_tensor", r=["sxstok", "swsc"], w=[("sxsw", vi)], out=xsw[vi][:].rearrange("p (h q) -> p h q", h=4), in0=xs_tok[:, c, :].rearrange("p (h q) -> p h q", h=4),
                                  in1=wsc[:, c, :, d].unsqueeze(2).broadcast_to([64, 4, 64]), op=ALU.mult)
                            if c >= 4:
                                P.add(PE, "matmul", r=["sCT", ("sSTb", d)], w=[PSK[po]], out=ps[po][0:64, 0:256], lhsT=CT[:, c * 64:(c + 1) * 64], rhs=STb[d][:], start=True, stop=True)
                            P.add(DVE, "tensor_tensor", r=[("sST", d), "setot"], w=[("sST", d)], out=ST[d][:].rearrange("p (h q) -> p h q", h=4), in0=ST[d][:].rearrange("p (h q) -> p h q", h=4),
                                  in1=etot[:, c, :, d].unsqueeze(2).broadcast_to([128, 4, 64]), op=ALU.mult)
                            yield
                            P.add(PE, "matmul", r=["sBtok", ("sxsw", vi)], w=[PSK[pst]], out=ps[pst][:, 0:256], lhsT=B_tok[:, c, :], rhs=xsw[vi][:], start=True, stop=True)
                            if c >= 4:
                                P.add(DVE, "tensor_tensor", r=[PSK[po], "seacs"], w=[("sytmp", d)], out=ytmp[d][:].rearrange("p (h q) -> p h q", h=4),
                                      in0=ps[po][0:64, 0:256].rearrange("p (h q) -> p h q", h=4), in1=eacs[:, c, :, d].unsqueeze(2).broadcast_to([64, 4, 64]), op=ALU.mult)
                            yield
                            P.add(DVE, "tensor_tensor", r=[("sST", d), PSK[pst]], w=[("sSTb", d)], out=STb[d][:], in0=ST[d][:], in1=ps[pst][:, 0:256], op=ALU.add)
                            P.add(DVE, "tensor_tensor", r=[("sST", d), PSK[pst]], w=[("sST", d)], out=ST[d][:], in0=ST[d][:], in1=ps[pst][:, 0:256], op=ALU.add)
                            if c >= 4:
                                P.add(POOL, "tensor_tensor", r=[("sytmp", d), ("sytok", c)], w=[("sytok", c)], out=y_tok[:, c - 4, :], in0=y_tok[:, c - 4, :], in1=ytmp[d][:], op=ALU.add)
                            yield

                    dq = []
                    for i_ in range(16):
                        dq += [4 + i_, NCH - 1 - i_]
                    slots = [None, None, None, None]
                    scans = [scan_gen(0), scan_gen(1)]
                    if os.environ.get("KSSDSEQ"):
                        for c_ in dq:
                            for _ in diag_chunk(c_, c_ % 4):
                                pass
                        dq = []
                    while scans or dq or any(g_ is not None for g_ in slots):
                        for si in range(4):
                            if slots[si] is None and dq:
                                slots[si] = diag_chunk(dq.pop(0), si)
                            if slots[si] is not None:
                                try:
                                    next(slots[si])
                                except StopIteration:
                                    slots[si] = None
                        for g_ in list(scans):
                            try:
                                next(g_)
                            except StopIteration:
                                scans.remove(g_)
                    P.flush()
                    if dbg:
                        P.dma(out.rearrange("(c p) n -> p c n", p=64)[:, :, (quad % 4) * 256:(quad % 4 + 1) * 256], y_tok[:], r=[("sytok", c) for c in range(4, NCH)])
                        P.flush()
                        continue
                    scr = feat[0:64, 0:16 * 256].rearrange("p (c q) -> p c q", q=256)
                    yk = [("sytok", c) for c in range(4, NCH)]
                    for half in range(2):
                        cs = slice(half * 16, half * 16 + 16)
                        P.add(DVE, "tensor_tensor", r=["sxstok", "sDt"], w=["sscr"], out=scr.rearrange("p c (h q) -> p c h q", h=4),
                              in0=xs_tok[:, 4 + half * 16:4 + half * 16 + 16, :].rearrange("p c (h q) -> p c h q", h=4),
                              in1=Dt[:, quad * 4:quad * 4 + 4].unsqueeze(1).unsqueeze(3).broadcast_to([64, 16, 4, 64]), op=ALU.mult)
                        P.add(DVE, "tensor_tensor", r=["sscr"] + yk, w=yk, out=y_tok[:, cs, :], in0=y_tok[:, cs, :], in1=scr, op=ALU.add)
                    P.add(DVE, "tensor_tensor", r=["sztok"] + yk, w=["sztok"], out=z_tok[:], in0=y_tok[:], in1=z_tok[:], op=ALU.mult)
                    it = 0
                    for ft in range(2):
                        for bi in range(4):
                            for j in range(8):
                                P.add(PE, "transpose", r=["sztok", "ident_b"], w=[PSK[7]], out=psb[:, j * 64:(j + 1) * 64], in_=z_tok[:, bi * 8 + j, ft * 128:(ft + 1) * 128], identity=ident_b[0:64, 0:64])
                            P.add(ACT, "copy", r=[PSK[7]], w=[("syst", 0)], out=yst[it % 2][:], in_=psb[:, 0:512])
                            P.dma(yab[8 + quad * 2 + ft, :, bi * 512:(bi + 1) * 512], yst[it % 2][:], r=[("syst", 0)])
                            it += 1
                    P.flush()

        def merge():
            w_bg_v = w_bg.rearrange("(k p) n -> p k n", p=128)
            w_bm_v = w_bm.rearrange("(k p) n -> p k n", p=128)
            w_o_v = w_o.rearrange("(k p) n -> p k n", p=128)
            with ExitStack() as mx:
                print("merge: sbuf remaining", nc.sbuf_bytes_remaining, flush=True)
                yaT = sb(mx, "myaT", [128, 8, 1024], BF16)
                ybT = sb(mx, "mybT", [128, 16, 1024], BF16)
                mrg = sb(mx, "mmrg", [128, 8, 1024], BF16)
                wsts = [sb(mx, f"mwst{i}", [128, 16, 128]) for i in range(2)]
                wsti = [0]
                wgb = [sb(mx, f"mwgb{i}", [128, 8, 128], BF16) for i in range(2)]
                wmb = [sb(mx, "mwmb0", [128, 16, 128], BF16)] * 2
                gab = [sb(mx, f"mgab{i}", [128, 8, 128], BF16) for i in range(2)]
                gbb = [sb(mx, f"mgbb{i}", [128, 8, 128], BF16) for i in range(2)]
                sq = [sb(mx, "msq0", [128, 512])] * 2
                rs = sb(mx, "mrs", [128, 512])
                sg = [sb(mx, f"msg{i}", [128, 512]) for i in range(2)]
                ma = [sb(mx, f"mma{i}", [128, 512]) for i in range(2)]
                xb = [sb(mx, "mxb0", [128, 1024])] * 2
                for H0 in (0, 1024):
                    for t in range(8):
                        P.dma(yaT[:, t, :], yab[t][:, H0:H0 + 1024], w=[("myaT", t)])
                    for t in range(16):
                        P.dma(ybT[:, t, :], yab[8 + t][:, H0:H0 + 1024], w=[("mybT", t)])
                    for g_ in range(2):
                        for bi in range(2):
                            pn = 6 + (bi % 2)
                            tsl = slice(bi * 512, (bi + 1) * 512)
                            for t in range(8):
                                tt = g_ * 8 + t
                                P.add(ACT, "activation", r=[("mybT", tt)], w=[("msq", 0)], out=sq[t % 2][:], in_=ybT[:, tt, tsl], func=AF.Square)
                                P.add(PE, "matmul", r=[("msq", 0), "ones"], w=[PSK[pn]], out=ps[pn][:, :], lhsT=ones_f[:], rhs=sq[t % 2][:], start=(t == 0), stop=(t == 7))
                            P.add(ACT, "activation", r=[PSK[pn]], w=["mrs"], out=rs[:], in_=ps[pn][:, :], func=AF.Sqrt, scale=1.0 / 1024, bias=epsc[:, 0:1])
                            P.add(DVE, "reciprocal", r=["mrs"], w=["mrs"], out=rs[:], in_=rs[:])
                            for t in range(8):
                                tt = g_ * 8 + t
                                P.add(DVE, "scalar_tensor_tensor", r=[("mybT", tt), "mrs"], w=[("mybT", tt)], out=ybT[:, tt, tsl], in0=ybT[:, tt, tsl], scalar=cols1[:, 105 + tt:106 + tt],
                                      in1=rs[:], op0=ALU.mult, op1=ALU.mult)
                    it = 0
                    for d in range(8):
                        b = d % 2
                        dsl = slice(d * 128, (d + 1) * 128)
                        wsti[0] ^= 1; wst = wsts[wsti[0]]
                        P.dma(wst[:, 0:8, :], w_bg_v[:, :, dsl], w=[("mwst", wsti[0])])
                        P.add(POOL, "tensor_copy", r=[("mwst", wsti[0])], w=[("mwgb", b)], out=wgb[b][:], in_=wst[:, 0:8, :])
                        wsti[0] ^= 1; wst = wsts[wsti[0]]
                        P.dma(wst[:, :, :], w_bm_v[:, :, dsl], w=[("mwst", wsti[0])])
                        P.add(POOL, "tensor_copy", r=[("mwst", wsti[0])], w=[("mwmb", 0)], out=wmb[b][:], in_=wst[:, :, :])
                        wsti[0] ^= 1; wst = wsts[wsti[0]]
                        P.dma(wst[:, 0:8, :], w_in_v[:, :, 8800 + d * 128:8800 + (d + 1) * 128], w=[("mwst", wsti[0])])
                        P.add(POOL, "tensor_copy", r=[("mwst", wsti[0])], w=[("mgab", b)], out=gab[b][:], in_=wst[:, 0:8, :])
                        wsti[0] ^= 1; wst = wsts[wsti[0]]
                        P.dma(wst[:, 0:8, :], w_in_v[:, :, 9824 + d * 128:9824 + (d + 1) * 128], w=[("mwst", wsti[0])])
                        P.add(POOL, "tensor_copy", r=[("mwst", wsti[0])], w=[("mgbb", b)], out=gbb[b][:], in_=wst[:, 0:8, :])
                        for bi in range(2):
                            tsl = slice(bi * 512, (bi + 1) * 512)
                            hsl = slice(256 + H0 + bi * 512, 256 + H0 + (bi + 1) * 512)
                            q = it % 2
                            it += 1
                            pa, pga, pb_, pgb = q * 4, q * 4 + 1, q * 4 + 2, q * 4 + 3
                            for k in range(8):
                                P.add(PE, "matmul", r=[("mwgb", b), ("myaT", k)], w=[PSK[pa]], out=ps[pa][:, :], lhsT=wgb[b][:, k, :], rhs=yaT[:, k, tsl], start=(k == 0), stop=(k == 7))
                            for k in range(8):
                                P.add(PE, "matmul", r=[("mgab", b)], w=[PSK[pga]], out=ps[pga][:, :], lhsT=gab[b][:, k, :], rhs=hT[:, k, hsl], start=(k == 0), stop=(k == 7))
                            for k in range(16):
                                P.add(PE, "matmul", r=[("mwmb", 0), ("mybT", k)], w=[PSK[pb_]], out=ps[pb_][:, :], lhsT=wmb[b][:, k, :], rhs=ybT[:, k, tsl], start=(k == 0), stop=(k == 15))
                            for k in range(8):
                                P.add(PE, "matmul", r=[("mgbb", b)], w=[PSK[pgb]], out=ps[pgb][:, :], lhsT=gbb[b][:, k, :], rhs=hT[:, k, hsl], start=(k == 0), stop=(k == 7))
                            P.add(ACT, "activation", r=[PSK[pga]], w=[("msg", 0)], out=sg[0][:], in_=ps[pga][:, :], func=AF.Sigmoid)
                            P.add(DVE, "tensor_tensor", r=[("msg", 0), PSK[pa]], w=[("mma", 0)], out=ma[0][:], in0=sg[0][:], in1=ps[pa][:, :], op=ALU.mult)
                            P.add(ACT, "activation", r=[PSK[pgb]], w=[("msg", 1)], out=sg[1][:], in_=ps[pgb][:, :], func=AF.Sigmoid)
                            P.add(DVE, "tensor_tensor", r=[("msg", 1), PSK[pb_]], w=[("mma", 1)], out=ma[1][:], in0=sg[1][:], in1=ps[pb_][:, :], op=ALU.mult)
                            P.add(POOL, "tensor_tensor", r=[("mma", 0), ("mma", 1)], w=[("mmrg", d)], out=mrg[:, d, tsl], in0=ma[0][:], in1=ma[1][:], op=ALU.add)
                    for d in range(8):
                        b = d % 2
                        dsl = slice(d * 128, (d + 1) * 128)
                        wsti[0] ^= 1; wst = wsts[wsti[0]]
                        P.dma(wst[:, 0:8, :], w_o_v[:, :, dsl], w=[("mwst", wsti[0])])
                        P.add(POOL, "tensor_copy", r=[("mwst", wsti[0])], w=[("mwgb", b)], out=wgb[b][:], in_=wst[:, 0:8, :])
                        P.dma(xb[b][:], xsp[:, d, H0:H0 + 1024], w=[("mxb", 0)])
                        for bi in range(2):
                            tsl = slice(bi * 512, (bi + 1) * 512)
                            po = (it % 2) * 4
                            it += 1
                            for k in range(8):
                                P.add(PE, "matmul", r=[("mwgb", b)] + [("mmrg", k)], w=[PSK[po]], out=ps[po][:, :], lhsT=wgb[b][:, k, :], rhs=mrg[:, k, tsl], start=(k == 0), stop=(k == 7))
                            P.add(DVE, "scalar_tensor_tensor", r=[PSK[po], ("mxb", 0)], w=[("mxb", 0)], out=xb[b][:, tsl], in0=ps[po][:, :], scalar=mvap(1, 0, 2, d), in1=xb[b][:, tsl],
                                  op0=ALU.mult, op1=ALU.add)
                        P.dma(xsp[:, d, H0:H0 + 1024], xb[b][:], r=[("mxb", 0)])
                    P.flush()

        with ExitStack() as pa:
            xT = sb(pa, "xT", [128, 8, T])
            cur["xT"] = xT
            with ExitStack() as s1:
                xst = [sb(s1, f"xst{i}", [128, D]) for i in range(2)]
                for t in range(18):
                    b = t % 2
                    P.dma(xst[b][:], xin[t * 128:(t + 1) * 128, :], w=[("xst", b)])
                    for half in range(2):
                        pi = 2 * b + half
                        for j in range(4):
                            P.add(PE, "transpose", r=[("xst", b), "ident_f"], w=[PSK[pi]], out=ps[pi][:, j * 128:(j + 1) * 128],
                                  in_=xst[b][:, (half * 4 + j) * 128:(half * 4 + j + 1) * 128], identity=ident_f[:])
                        wk = [("xT", k, (BLKS[0][0] if t < 2 else BLKS[1 + (t - 2) // 4][0])) for k in range(half * 4, half * 4 + 4)]
                        if half:
                            P.add(ACT, "copy", r=[PSK[pi]], w=wk, out=xT[:, half * 4:half * 4 + 4, t * 128:(t + 1) * 128], in_=v3(ps[pi][:, 0:512], 128))
                        else:
                            P.add(DVE, "tensor_copy", r=[PSK[pi]], w=wk, out=xT[:, half * 4:half * 4 + 4, t * 128:(t + 1) * 128], in_=v3(ps[pi][:, 0:512], 128))
                crt = sb(s1, "crt", [16, 128])
                cs2 = sb(s1, "cs2", [128, 8, 2])
                modT = sb(s1, "modT", [128, 72, 2])
                wst = [sb(s1, f"wst{i}", [128, 8, 512]) for i in range(2)]
                P.dma(crt[:], crow, w=["crt"])
                P.add(ACT, "activation", r=["crt"], w=["crt2"], out=crt[:], in_=crt[:], func=AF.Silu)
                P.add(PE, "transpose", r=["crt2", "ident_f"], w=[PSK[4]], out=ps[4][:, 0:16], in_=crt[:], identity=ident_f[0:16, 0:16])
                P.add(DVE, "tensor_copy", r=[PSK[4]], w=["cs2"], out=cs2[:, :, 0], in_=ps[4][:, 0:8])
                P.add(DVE, "tensor_copy", r=[PSK[4]], w=["cs2"], out=cs2[:, :, 1], in_=ps[4][:, 8:16])
                for cg in range(18):
                    b = cg % 2
                    P.dma(wst[b][:], w_ada_v[:, :, cg * 512:(cg + 1) * 512], w=[("wst", b)])
                    for jj in range(4):
                        j = cg * 4 + jj
                        for k in range(8):
                            P.add(PE, "matmul", r=[("wst", b), "cs2"], w=[PSK[5]], out=ps[5][:, 2 * j:2 * j + 2],
                                  lhsT=wst[b][:, k, jj * 128:(jj + 1) * 128], rhs=cs2[:, k, :], start=(k == 0), stop=(k == 7))
                P.add(DVE, "tensor_tensor", r=[PSK[5], ("cols", 0)], w=["modT"], out=modT[:], in0=v3(ps[5][:, 0:144], 2),
                      in1=cols1[:, 0:72].unsqueeze(2).broadcast_to([128, 72, 2]), op=ALU.add)
                for s in range(3):
                    for w in range(2):
                        i0 = ((s * 2 + w) * 3) * 8
                        P.add(DVE, "scalar_tensor_tensor", r=["modT", ("cols", 0)], w=["mv"], out=mv[:, i0:i0 + 8],
                              in0=modT[:, (3 * s + 1) * 8:(3 * s + 2) * 8, w], scalar=1.0, in1=cols1[:, 72 + s * 8:72 + s * 8 + 8],
                              op0=ALU.add, op1=ALU.mult)
                        P.add(DVE, "tensor_copy", r=["modT"], w=["mv"], out=mv[:, i0 + 8:i0 + 16], in_=modT[:, (3 * s) * 8:(3 * s + 1) * 8, w])
                        P.add(DVE, "tensor_scalar", r=["modT"], w=["mv"], out=mv[:, i0 + 16:i0 + 24],
                              in0=modT[:, (3 * s + 2) * 8:(3 * s + 3) * 8, w], scalar1=(1.0 if s == 1 else 0.5), scalar2=None, op0=ALU.mult)
                P.flush()
            if STAGE >= 1 and not os.environ.get("KSKIPFFN"):
                rmsnorm_mod(0, BLKS)
                ffn(0, 0, BLKS)
            if STAGE <= 1:
                write_out(False)
                return nc
            rmsnorm_mod(1, BLKS)
            for k in range(8):
                if not os.environ.get("KNOSPILL"):
                    P.dma(xsp[:, k, :], xT[:, k, 256:T], r=[("xT", k, t0) for (t0, n) in BLKS])
            P.flush()
        if STAGE != 3:
            gdn(dbg=(STAGE == 2))
        if STAGE >= 3:
            ssd(dbg=(STAGE == 3))
        if STAGE in (2, 3):
            P.final_waits = [i for i in P.dma_last.values()]
            P.add(POOL, "memset", w=["zeros"], ap=zeros_f[:], constant=0.0)
            P.flush(final=True)
            return nc
        merge()
        with ExitStack() as pc:
            xT = sb(pc, "xT2", [128, 8, T])
            cur["xT"] = xT
            for k in range(8):
                P.dma(xT[:, k, 256:T], xsp[:, k, :], w=[("xT", k, t0) for (t0, n) in LBLKS])
            if STAGE == 4:
                write_out(False)
                return nc
            rmsnorm_mod(2, LBLKS)
            ffn(2, 1, LBLKS)
            write_out(True)
        return nc


def prep_inputs(inputs, b):
    f = lambda a: np.ascontiguousarray(np.asarray(a, dtype=np.float32))
    m = {}
    m["xin"] = f(np.concatenate([inputs["ctx"][b], inputs["x"][b]], axis=0))
    m["crow"] = f(np.concatenate([np.asarray(inputs["c"][b]).reshape(8, 128), np.asarray(inputs["c_ctx"]).reshape(8, 128)], 0))
    r1 = np.zeros((128, 128), np.float32)
    r1[0:72] = np.asarray(inputs["b_ada"][0]).reshape(72, 128)
    r1[72:96] = np.asarray(inputs["norm_g"][0]).reshape(24, 128)
    r1[96:104] = np.asarray(inputs["final_g"]).reshape(8, 128)
    r1[104:105] = np.asarray(inputs["gdn_norm_g"][0]).reshape(1, 128)
    r1[105:121] = np.asarray(inputs["mb_norm_g"][0]).reshape(16, 128)
    m["rows1"] = r1
    r2 = np.zeros((128, 128), np.float32)
    r2[0:120] = np.asarray(inputs["gdn_conv_w"][0]).reshape(120, 128)
    m["rows2"] = r2
    r3 = np.zeros((128, 128), np.float32)
    r3[0:100] = np.asarray(inputs["mb_conv_w"][0]).reshape(100, 128)
    r3[100:120] = np.asarray(inputs["mb_conv_b"][0]).reshape(20, 128)
    m["rows3"] = r3
    br = np.zeros((1, 256), np.float32)
    br[0, 0:16] = np.asarray(inputs["gdn_dt_bias"][0]).reshape(16)
    br[0, 32:96] = np.asarray(inputs["mb_dt_bias"][0]).reshape(64)
    br[0, 96:112] = np.asarray(inputs["gdn_A_log"][0]).reshape(16)
    br[0, 128:192] = np.asarray(inputs["mb_A_log"][0]).reshape(64)
    br[0, 192:224] = np.asarray(inputs["mb_D"][0]).reshape(32)
    m["brow"] = br
    m["w_ada"] = f(inputs["w_ada"][0])
    m["w_gu"] = f(inputs["ffn_w_gu"][0])
    m["w_dn"] = f(inputs["ffn_w_down"][0])
    m["w_in"] = f(inputs["w_in"][0])
    m["w_bg"] = f(inputs["w_branch_gdn"][0])
    m["w_bm"] = f(inputs["w_branch_mb"][0])
    m["w_o"] = f(inputs["w_out"][0])
    return m


def kernel(**inputs):
    nc = build_program()
    shared = None
    in_maps = []
    for b in range(8):
        m = prep_inputs(inputs, b)
        if shared is None:
            shared = {k: m[k] for k in ("w_ada", "w_gu", "w_dn", "w_in", "w_bg", "w_bm", "w_o", "rows1", "rows2", "rows3", "brow")}
        else:
            m.update(shared)
        in_maps.append(m)
    res = run_bass_kernel_spmd(nc, in_maps, core_ids=list(range(8)))
    return np.stack([np.asarray(r["out"], dtype=np.float32) for r in res.results], axis=0)
```
